# Optimizing a Trainium2 kernel written in Bass

```python
import math
import jax, jax.numpy as jnp
from jax import lax
import numpy as np

D_MODEL = 1024
BATCH = 8
SEQ = 4096
DEPTH = 4

GRID_W = 64
CTX_LEN = 256
EPS = 1e-6
CONV_WIDTH = 4
GDN_HEAD_DIM = 128
GDN_HEADS = (D_MODEL // 2) // GDN_HEAD_DIM
GDN_WIDTH = GDN_HEADS * GDN_HEAD_DIM
GDN_CHUNK = 64
LRU_WIDTH = D_MODEL // 4
LRU_BLOCKS = 4
LRU_BLOCK = LRU_WIDTH // LRU_BLOCKS
LRU_C = 8.0
MLA_V = 64
MLA_HEADS = (D_MODEL // 4) // MLA_V
MLA_WIDTH = MLA_HEADS * MLA_V
MLA_NOPE = 64
MLA_ROPE = 32
MLA_Q_RANK = D_MODEL // 4
MLA_KV_RANK = D_MODEL // 8
ROPE_BASE = 10000.0
Q_BLOCK = 128
D_FF = 4 * D_MODEL
MIX_WIDTH = GDN_WIDTH + LRU_WIDTH + MLA_WIDTH
IN_SIZES = (3 * GDN_WIDTH, GDN_WIDTH, 2 * GDN_HEADS, 2 * GDN_HEADS, LRU_WIDTH, LRU_WIDTH, MLA_Q_RANK, MLA_KV_RANK, MLA_ROPE)
IN_WIDTH = sum(IN_SIZES)

kernel_name = 'hybrid_gdn_rglru_mla_dit_block'


def rms_norm(x, g):
    xf = x.astype(jnp.float32)
    y = xf * lax.rsqrt(jnp.mean(xf * xf, axis=-1, keepdims=True) + EPS)
    return (y * g.astype(jnp.float32)).astype(x.dtype)


def l2norm(t):
    return t * lax.rsqrt(jnp.sum(t * t, axis=-1, keepdims=True) + EPS)


def split_cols(t, sizes):
    idx = np.cumsum(sizes)[:-1].tolist()
    return jnp.split(t, idx, axis=-1)


def adaln(cond, w, b):
    return jnp.split(jax.nn.silu(cond) @ w + b, 6, axis=-1)


def dw_conv(x, w):
    ch = x.shape[-1]
    left = CONV_WIDTH // 2
    return lax.conv_general_dilated(x, w[:, None, :].astype(x.dtype), window_strides=(1,),
                                    padding=[(left, CONV_WIDTH - 1 - left)],
                                    dimension_numbers=('NWC', 'WIO', 'NWC'), feature_group_count=ch)


def axial_rope_tables(rows):
    row = jnp.repeat(jnp.arange(rows, dtype=jnp.float32), GRID_W)
    col = jnp.tile(jnp.arange(GRID_W, dtype=jnp.float32), rows)
    half = MLA_ROPE // 2
    inv = ROPE_BASE ** (-jnp.arange(0, half, 2, dtype=jnp.float32) / half)
    ang = jnp.stack([row[:, None] * inv, col[:, None] * inv], axis=1)
    ang = jnp.concatenate([ang, ang], axis=-1)[:, None]
    return jnp.cos(ang), jnp.sin(ang)


def apply_axial_rope(t, rope):
    cos, sin = rope
    shp = t.shape
    tr = t.astype(jnp.float32).reshape(shp[:-1] + (2, shp[-1] // 2))
    t1, t2 = jnp.split(tr, 2, axis=-1)
    rot = jnp.concatenate([-t2, t1], axis=-1)
    return (tr * cos + rot * sin).reshape(shp).astype(t.dtype)


def gdn_chunked(q, k, v, g, beta, s0):
    B, L, H, dk = q.shape
    dv = v.shape[-1]
    n = L // GDN_CHUNK

    def chunks(t):
        t = t.reshape((B, n, GDN_CHUNK, H) + t.shape[3:])
        return jnp.moveaxis(t, (1, 3), (0, 2))

    q = chunks(q) * (dk ** -0.5)
    k, v, g, beta = chunks(k), chunks(v), chunks(g), chunks(beta)
    G = jnp.cumsum(g, axis=-1)
    idx = jnp.arange(GDN_CHUNK)
    incl = idx[:, None] >= idx[None, :]
    strict = idx[:, None] > idx[None, :]
    decay = jnp.exp(jnp.where(incl, G[..., :, None] - G[..., None, :], -jnp.inf))
    kb = k * beta[..., None]
    low = jnp.where(strict, jnp.einsum('nbhid,nbhjd->nbhij', kb, k) * decay, 0.0)
    amat = low + jnp.eye(GDN_CHUNK, dtype=low.dtype)
    rhs = jnp.concatenate([v * beta[..., None], kb * jnp.exp(G)[..., None]], axis=-1)
    sol = lax.linalg.triangular_solve(amat, rhs, left_side=True, lower=True, unit_diagonal=True)
    u, w = sol[..., :dv], sol[..., dv:]
    attn = jnp.einsum('nbhid,nbhjd->nbhij', q, k) * decay
    qg = q * jnp.exp(G)[..., None]
    kg = k * jnp.exp(G[..., -1:] - G)[..., None]
    glast = jnp.exp(G[..., -1])

    def step(S, inp):
        u_i, w_i, a_i, qg_i, kg_i, gl_i = inp
        v_new = u_i - jnp.einsum('bhcd,bhde->bhce', w_i, S)
        o_i = jnp.einsum('bhcd,bhde->bhce', qg_i, S) + jnp.einsum('bhij,bhje->bhie', a_i, v_new)
        S = S * gl_i[..., None, None] + jnp.einsum('bhcd,bhce->bhde', kg_i, v_new)
        return S, o_i

    S, o = lax.scan(step, s0, (u, w, attn, qg, kg, glast))
    o = jnp.moveaxis(o, (0, 2), (1, 3)).reshape(B, L, H, dv)
    return o, S


def gdn_seq(qkv, b_col, a_col, conv_w, a_log, dt_bias, s0_f, s0_b):
    B, L, _ = qkv.shape
    qkv = jax.nn.silu(dw_conv(qkv, conv_w)).astype(jnp.float32)
    q, k, v = [t.reshape(B, L, GDN_HEADS, GDN_HEAD_DIM) for t in jnp.split(qkv, 3, axis=-1)]
    q, k = l2norm(q), l2norm(k)
    beta = jax.nn.sigmoid(b_col.astype(jnp.float32)).reshape(B, L, 2, GDN_HEADS)
    g = -jnp.exp(a_log.astype(jnp.float32)) * jax.nn.softplus(
        a_col.astype(jnp.float32).reshape(B, L, 2, GDN_HEADS) + dt_bias.astype(jnp.float32))
    o_f, s_f = gdn_chunked(q, k, v, g[:, :, 0], beta[:, :, 0], s0_f)
    rev = lambda t: jnp.flip(t, axis=1)
    o_b, s_b = gdn_chunked(rev(q), rev(k), rev(v), rev(g[:, :, 1]), rev(beta[:, :, 1]), s0_b)
    return o_f + rev(o_b), s_f, s_b


def gdn_out(o, z, norm_w):
    B, L = z.shape[:2]
    zf = z.astype(jnp.float32).reshape(B, L, GDN_HEADS, GDN_HEAD_DIM)
    return (rms_norm(o, norm_w) * jax.nn.silu(zf)).reshape(B, L, GDN_WIDTH)


def rg_lru_coeffs(xc, w_a, b_a, w_i, b_i, lam):
    B, L, W = xc.shape
    xb = xc.reshape(B, L, LRU_BLOCKS, LRU_BLOCK)
    r = jax.nn.sigmoid(jnp.einsum('blgi,gij->blgj', xb, w_a).reshape(B, L, W) + b_a)
    i = jax.nn.sigmoid(jnp.einsum('blgi,gij->blgj', xb, w_i).reshape(B, L, W) + b_i)
    log_a = -LRU_C * r * jax.nn.softplus(-lam)
    a = jnp.exp(log_a)
    return a, jnp.sqrt(-jnp.expm1(2.0 * log_a)) * (i * xc)


def linear_scan(a, b, h0):
    b = b.at[:, 0].add(a[:, 0] * h0)

    def combine(e1, e2):
        return e1[0] * e2[0], e2[0] * e1[1] + e2[1]

    return lax.associative_scan(combine, (a, b), axis=1)[1]


def lru_seq(xr, conv_w, conv_b, w_a, b_a, w_i, b_i, lam, h0_f, h0_b):
    xc = (dw_conv(xr, conv_w) + conv_b).astype(jnp.float32)
    a_f, u_f = rg_lru_coeffs(xc, w_a[0], b_a[0], w_i[0], b_i[0], lam[0])
    h_f = linear_scan(a_f, u_f, h0_f)
    a_b, u_b = rg_lru_coeffs(jnp.flip(xc, axis=1), w_a[1], b_a[1], w_i[1], b_i[1], lam[1])
    h_b = linear_scan(a_b, u_b, h0_b)
    return h_f + jnp.flip(h_b, axis=1), h_f[:, -1], h_b[:, -1]


def mla_q(cq, q_norm_w, w_uq, rope):
    B, L, _ = cq.shape
    q = (rms_norm(cq, q_norm_w) @ w_uq).reshape(B, L, MLA_HEADS, MLA_NOPE + MLA_ROPE)
    q_rope = q[..., MLA_NOPE:]
    if rope is not None:
        q_rope = apply_axial_rope(q_rope, rope)
    return jnp.concatenate([q[..., :MLA_NOPE], q_rope], axis=-1)


def mla_kv(ckv, kr, kv_norm_w, w_ukv, rope):
    B, L, _ = ckv.shape
    kv = (rms_norm(ckv, kv_norm_w) @ w_ukv).reshape(B, L, MLA_HEADS, MLA_NOPE + MLA_V)
    kr = kr[:, :, None, :]
    if rope is not None:
        kr = apply_axial_rope(kr, rope)
    k = jnp.concatenate([kv[..., :MLA_NOPE], jnp.broadcast_to(kr, (B, L, MLA_HEADS, MLA_ROPE))], axis=-1)
    return k, kv[..., MLA_NOPE:]


def softmax_attend(q, k, v):
    s = jnp.einsum('bqhd,bkhd->bhqk', q, k).astype(jnp.float32) * ((MLA_NOPE + MLA_ROPE) ** -0.5)
    p = jax.nn.softmax(s, axis=-1)
    return jnp.einsum('bhqk,bkhd->bqhd', p.astype(v.dtype), v)


def blocked_attend(q, k, v):
    B, L, H, d = q.shape
    nb = L // Q_BLOCK
    qb = jnp.moveaxis(q.reshape(B, nb, Q_BLOCK, H, d), 1, 0)
    out = lax.map(lambda qi: softmax_attend(qi, k, v), qb)
    return jnp.moveaxis(out, 0, 1).reshape(B, L, H, v.shape[-1])


def hybrid_mixer(h_lat, h_ctx, with_ctx_out, w_in, gdn_conv_w, gdn_a_log, gdn_dt_bias, gdn_norm_w,
                 lru_conv_w, lru_conv_b, lru_w_a, lru_b_a, lru_w_i, lru_b_i, lru_lambda,
                 mla_q_norm, mla_w_uq, mla_kv_norm, mla_w_ukv, w_out, rope):
    B, L, _ = h_lat.shape
    pc = split_cols(h_ctx @ w_in, IN_SIZES)
    pl = split_cols(h_lat @ w_in, IN_SIZES)
    s_zero = jnp.zeros((B, GDN_HEADS, GDN_HEAD_DIM, GDN_HEAD_DIM), jnp.float32)
    oc_gdn, s_f, s_b = gdn_seq(pc[0], pc[2], pc[3], gdn_conv_w, gdn_a_log, gdn_dt_bias, s_zero, s_zero)
    ol_gdn, _, _ = gdn_seq(pl[0], pl[2], pl[3], gdn_conv_w, gdn_a_log, gdn_dt_bias, s_f, s_b)
    h_zero = jnp.zeros((B, LRU_WIDTH), jnp.float32)
    rc, hf, hb = lru_seq(pc[4], lru_conv_w, lru_conv_b, lru_w_a, lru_b_a, lru_w_i, lru_b_i, lru_lambda, h_zero, h_zero)
    rl, _, _ = lru_seq(pl[4], lru_conv_w, lru_conv_b, lru_w_a, lru_b_a, lru_w_i, lru_b_i, lru_lambda, hf, hb)
    k_c, v_c = mla_kv(pc[7], pc[8], mla_kv_norm, mla_w_ukv, None)
    k_l, v_l = mla_kv(pl[7], pl[8], mla_kv_norm, mla_w_ukv, rope)
    q_l = mla_q(pl[6], mla_q_norm, mla_w_uq, rope)
    a_l = blocked_attend(q_l, jnp.concatenate([k_c, k_l], axis=1), jnp.concatenate([v_c, v_l], axis=1))
    y_lat = jnp.concatenate([gdn_out(ol_gdn, pl[1], gdn_norm_w),
                             rl * jax.nn.gelu(pl[5].astype(jnp.float32)),
                             a_l.reshape(B, L, MLA_WIDTH).astype(jnp.float32)], axis=-1).astype(h_lat.dtype) @ w_out
    if not with_ctx_out:
        return y_lat, None
    Lc = h_ctx.shape[1]
    a_c = softmax_attend(mla_q(pc[6], mla_q_norm, mla_w_uq, None), k_c, v_c)
    y_ctx = jnp.concatenate([gdn_out(oc_gdn, pc[1], gdn_norm_w),
                             rc * jax.nn.gelu(pc[5].astype(jnp.float32)),
                             a_c.reshape(B, Lc, MLA_WIDTH).astype(jnp.float32)], axis=-1).astype(h_ctx.dtype) @ w_out
    return y_lat, y_ctx


def sq_relu_mlp(h, w1, w2):
    return jnp.square(jax.nn.relu(h @ w1)) @ w2


def setup_inputs(seed: int = 0) -> dict:
    key = jax.random.key(seed)
    kit = iter(list(jax.random.split(key, 40)))
    nrm = lambda shape, scale: scale * jax.random.normal(next(kit), shape, jnp.float32)
    gain = lambda shape: 1.0 + 0.05 * jax.random.normal(next(kit), shape, jnp.float32)
    Ld = DEPTH
    u = jax.random.uniform(next(kit), (Ld, 2, LRU_WIDTH), jnp.float32, minval=0.9, maxval=0.999)
    a_base = u ** (1.0 / LRU_C)
    dt = jnp.exp(jax.random.uniform(next(kit), (Ld, 2, GDN_HEADS), jnp.float32,
                                    minval=math.log(1e-3), maxval=math.log(1e-1)))
    return {
        'x': nrm((BATCH, SEQ, D_MODEL), 1.0),
        'c': nrm((BATCH, D_MODEL), 1.0),
        'ctx': nrm((BATCH, CTX_LEN, D_MODEL), 1.0),
        'c_ctx': nrm((D_MODEL,), 1.0),
        'w_ada': nrm((Ld, D_MODEL, 6 * D_MODEL), 0.5 * D_MODEL ** -0.5),
        'b_ada': nrm((Ld, 6 * D_MODEL), 0.01),
        'g_attn_pre': gain((Ld, D_MODEL)),
        'g_attn_post': gain((Ld, D_MODEL)),
        'g_mlp_pre': gain((Ld, D_MODEL)),
        'g_mlp_post': gain((Ld, D_MODEL)),
        'w_in': nrm((Ld, D_MODEL, IN_WIDTH), D_MODEL ** -0.5),
        'gdn_conv_w': nrm((Ld, CONV_WIDTH, 3 * GDN_WIDTH), CONV_WIDTH ** -0.5),
        'gdn_a_log': jnp.log(jax.random.uniform(next(kit), (Ld, 2, GDN_HEADS), jnp.float32, minval=1.0, maxval=16.0)),
        'gdn_dt_bias': dt + jnp.log(-jnp.expm1(-dt)),
        'gdn_norm_w': gain((Ld, GDN_HEAD_DIM)),
        'lru_conv_w': nrm((Ld, CONV_WIDTH, LRU_WIDTH), CONV_WIDTH ** -0.5),
        'lru_conv_b': nrm((Ld, LRU_WIDTH), 0.01),
        'lru_w_a': nrm((Ld, 2, LRU_BLOCKS, LRU_BLOCK, LRU_BLOCK), LRU_BLOCK ** -0.5),
        'lru_b_a': nrm((Ld, 2, LRU_WIDTH), 0.01),
        'lru_w_i': nrm((Ld, 2, LRU_BLOCKS, LRU_BLOCK, LRU_BLOCK), LRU_BLOCK ** -0.5),
        'lru_b_i': nrm((Ld, 2, LRU_WIDTH), 0.01),
        'lru_lambda': jnp.log(a_base) - jnp.log1p(-a_base),
        'mla_q_norm': gain((Ld, MLA_Q_RANK)),
        'mla_w_uq': nrm((Ld, MLA_Q_RANK, MLA_HEADS * (MLA_NOPE + MLA_ROPE)), MLA_Q_RANK ** -0.5),
        'mla_kv_norm': gain((Ld, MLA_KV_RANK)),
        'mla_w_ukv': nrm((Ld, MLA_KV_RANK, MLA_HEADS * (MLA_NOPE + MLA_V)), MLA_KV_RANK ** -0.5),
        'w_out': nrm((Ld, MIX_WIDTH, D_MODEL), MIX_WIDTH ** -0.5),
        'w_mlp1': nrm((Ld, D_MODEL, D_FF), D_MODEL ** -0.5),
        'w_mlp2': nrm((Ld, D_FF, D_MODEL), D_FF ** -0.5),
    }


def reference(x, c, ctx, c_ctx, w_ada, b_ada, g_attn_pre, g_attn_post, g_mlp_pre, g_mlp_post, w_in,
              gdn_conv_w, gdn_a_log, gdn_dt_bias, gdn_norm_w, lru_conv_w, lru_conv_b, lru_w_a, lru_b_a,
              lru_w_i, lru_b_i, lru_lambda, mla_q_norm, mla_w_uq, mla_kv_norm, mla_w_ukv, w_out,
              w_mlp1, w_mlp2):
    n_lat = x.shape[1]
    ROWS = n_lat // GRID_W
    rope = axial_rope_tables(ROWS)
    for l in range(DEPTH):
        last = l == DEPTH - 1
        sh1, sc1, gt1, sh2, sc2, gt2 = [m[:, None, :] for m in adaln(c, w_ada[l], b_ada[l])]
        csh1, csc1, cgt1, csh2, csc2, cgt2 = adaln(c_ctx, w_ada[l], b_ada[l])
        h_lat = rms_norm(x, g_attn_pre[l]) * (1.0 + sc1) + sh1
        h_ctx = rms_norm(ctx, g_attn_pre[l]) * (1.0 + csc1) + csh1
        y_lat, y_ctx = hybrid_mixer(h_lat, h_ctx, not last, w_in[l], gdn_conv_w[l], gdn_a_log[l], gdn_dt_bias[l],
                                    gdn_norm_w[l], lru_conv_w[l], lru_conv_b[l], lru_w_a[l], lru_b_a[l],
                                    lru_w_i[l], lru_b_i[l], lru_lambda[l], mla_q_norm[l], mla_w_uq[l],
                                    mla_kv_norm[l], mla_w_ukv[l], w_out[l], rope)
        x = x + gt1 * rms_norm(y_lat, g_attn_post[l])
        h = rms_norm(x, g_mlp_pre[l]) * (1.0 + sc2) + sh2
        x = x + gt2 * rms_norm(sq_relu_mlp(h, w_mlp1[l], w_mlp2[l]), g_mlp_post[l])
        if not last:
            ctx = ctx + cgt1 * rms_norm(y_ctx, g_attn_post[l])
            hc = rms_norm(ctx, g_mlp_pre[l]) * (1.0 + csc2) + csh2
            ctx = ctx + cgt2 * rms_norm(sq_relu_mlp(hc, w_mlp1[l], w_mlp2[l]), g_mlp_post[l])
    return x
```

```python
import contextlib
import numpy as np
import ml_dtypes
import concourse.bass as bass
import concourse.mybir as mybir
from concourse.bass_utils import run_bass_kernel_spmd

F32 = mybir.dt.float32
BF16 = mybir.dt.bfloat16
AF = mybir.ActivationFunctionType
ALU = mybir.AluOpType

D = 1024
KC = 8
DEPTH = 4
IN_W = 2992
IN_WX = 3024
EPS = 1e-6
NPV = 180
GSTOP = 99
GSKIP = 0
MSTOP = 99
GSUB = 99
PHASES = ['ada', 'proj', 'gdn_prep', 'lru', 'mla', 'gdn', 'wout', 'mlp']


class Sched:
    NQ = 8
    CE = ('pe', 'act', 'dve', 'pool')

    def __init__(self, nc, st, needed=None):
        self.nc = nc
        self.eng = {'pe': nc.tensor, 'act': nc.scalar, 'dve': nc.vector, 'pool': nc.gpsimd, 'sp': nc.sync}
        self.sem = {}
        self.pos = {}
        self.act = {}
        self.actual_at = {}
        for k in self.CE:
            self.sem[k] = st.enter_context(nc.semaphore('s_' + k))
            self.pos[k] = 0
            self.act[k] = 0
            self.actual_at[k] = [0]
        self.dq = {}
        for q in ('sp', 'act', 'pool'):
            self.dq[q] = dict(sems=[st.enter_context(nc.semaphore('d_%s%d' % (q, i))) for i in range(self.NQ)],
                              vals=[0] * self.NQ, n=0)
        self.seen = {k: {} for k in self.eng}
        self.bufs = {}
        self.nwait = 0
        self.nsig = 0
        self.dummy_w = None
        self.analysis = needed is None
        self.needed = set() if needed is None else needed

    def need(self, ek, ev):
        if ev is None:
            return
        if ev[0] == 'dma':
            _, sem, val = ev
            sid = id(sem)
            if self.seen[ek].get(sid, 0) >= val:
                return
            self.eng[ek].wait_ge(sem, val)
            self.nwait += 1
            self.seen[ek][sid] = val
            return
        src, pos = ev
        if ek == 'pe' and src == 'pe':
            return
        if self.seen[ek].get(src, 0) >= pos:
            return
        if self.analysis:
            self.needed.add((src, pos))
        val = self.actual_at[src][pos]
        assert val is not None, (src, pos)
        self.eng[ek].wait_ge(self.sem[src], val)
        self.nwait += 1
        self.seen[ek][src] = pos

    def op(self, ek, fn, reads=(), writes=(), signal=True, dma=False):
        nw0 = self.nwait
        for k in reads:
            b = self.bufs.get(k)
            if b is not None:
                self.need(ek, b['w'])
        for k in writes:
            b = self.bufs.get(k)
            if b is not None:
                self.need(ek, b['w'])
                for ev in b['r'].values():
                    self.need(ek, ev)
        if ek == 'pe' and self.nwait != nw0 and self.dummy_w is not None:
            self.eng['pe'].ldweights(self.dummy_w)
        if dma:
            q = self.dq[ek]
            i = q['n'] % self.NQ
            q['n'] += 1
            sem = q['sems'][i]
            if q['vals'][i] > 0:
                self.need(ek, ('dma', sem, q['vals'][i]))
            ins = fn(self.eng[ek])
            q['vals'][i] += 16
            ins.then_inc(sem, 16)
            ev = ('dma', sem, q['vals'][i])
            rkey = ('d', id(sem))
        else:
            ins = fn(self.eng[ek])
            self.pos[ek] += 1
            pos = self.pos[ek]
            if self.analysis or (ek, pos) in self.needed:
                self.act[ek] += 1
                ins.then_inc(self.sem[ek], 1)
                self.actual_at[ek].append(self.act[ek])
                self.nsig += 1
            else:
                self.actual_at[ek].append(None)
            ev = (ek, pos)
            rkey = ek
        self._mark(reads, writes, ev, rkey)

    def _mark(self, reads, writes, ev, rkey):
        for k in reads:
            b = self.bufs.get(k)
            if b is None:
                b = self.bufs[k] = dict(w=None, r={})
            b['r'][rkey] = ev
        for k in writes:
            self.bufs[k] = dict(w=ev, r={})

    def dma(self, out, in_, reads=(), writes=(), q='sp'):
        self.op(q, lambda e: e.dma_start(out=out, in_=in_), reads, writes, dma=True)

    def barrier(self):
        for ek in self.eng:
            for f in self.CE:
                if f != ek and self.pos[f] > 0:
                    self.need(ek, (f, self.pos[f]))
            for qn, q in self.dq.items():
                for i in range(self.NQ):
                    if q['vals'][i] > 0:
                        self.need(ek, ('dma', q['sems'][i], q['vals'][i]))
        self.bufs = {}


class Prog:
    def __init__(self, Lc, Ll, depth, dbg=False, needed=None):
        self.Lc, self.Ll, self.depth, self.dbg = Lc, Ll, depth, dbg
        self.needed = needed
        self.Lt = Lc + Ll
        self.NB = self.Lt // 128
        self.NBc = Lc // 128
        nc = self.nc = bass.Bass("TRN2", target_bir_lowering=False)
        Lt = self.Lt
        di = lambda n, s, dt=F32: nc.dram_tensor(n, s, dt, kind="ExternalInput").ap()
        self.xT_in = di("xT", [D, Lt])
        self.cvec = di("cvec", [128, KC, 2])
        self.pv = di("pv", [depth, 128, NPV])
        self.consts = di("consts", [128, 14 * 128])
        self.rope = di("rope", [2, 32, Lt])
        self.w_ada = di("w_ada", [depth, D, 6 * D])
        self.w_in = di("w_in", [depth, D, IN_WX])
        self.lru_w = di("lru_w", [depth, 2, 2, 2, 128, 128])
        self.w_uq = di("w_uq", [depth, 256, 2 * 384])
        self.w_ukv = di("w_ukv", [depth, 128, 512])
        self.w_out = di("w_out", [depth, D, D])
        self.w_m1 = di("w_m1", [depth, D, 4 * D])
        self.w_m2 = di("w_m2", [depth, 4 * D, D])
        self.outT = nc.dram_tensor("outT", [D, Ll], F32, kind="ExternalOutput").ap()
        kind = "ExternalOutput" if dbg else "Internal"
        ds = lambda n, s, dt=F32: nc.dram_tensor(n, s, dt, kind=kind).ap()
        self.xT = ds("xTs", [D, Lt])
        self.P_qkv = ds("P_qkv", [1536, Lt])
        self.P_z = ds("P_z", [512, Lt])
        self.P_lx = ds("P_lx", [256, Lt])
        self.P_lg = ds("P_lg", [256, Lt])
        self.P_cq = ds("P_cq", [256, Lt])
        self.P_ckv = ds("P_ckv", [128, Lt])
        self.P_kr = ds("P_kr", [64, Lt])
        self.qkvT = ds("qkvT", [1536, Lt], BF16)
        self.yT = ds("yT", [D, Lt], BF16)
        self.dbg_ba = ds("dbg_ba", [128, self.NB * 16]) if dbg else None

    def dump(self, name, ap, keys):
        if not self.dbg:
            return
        shp = list(ap.shape)
        t = self.nc.dram_tensor("dd_" + name, shp, ap.dtype, kind="ExternalOutput").ap()
        self.S.dma(t, ap, reads=keys)

    def uname(self, n):
        self._uid = getattr(self, '_uid', 0) + 1
        return "%s_u%d" % (n, self._uid)

    def pst(self, n, shape, dt):
        return self.nc.psum_tensor(self.uname(n), shape, dt)

    def tiles(self, T):
        out = []
        for (a, b, s) in ((0, self.Lc, 1), (self.Lc, self.Lt, 0)):
            t = a
            while t < b:
                n = min(T, b - t)
                out.append((t, n, s))
                t += n
        return out

    def build(self):
        nc = self.nc
        with contextlib.ExitStack() as st:
            self.S = S = Sched(nc, st, self.needed)
            sb = lambda n, s, dt=F32, stack=st: stack.enter_context(nc.sbuf_tensor(self.uname(n), list(s), dt))
            self.sb = sb
            self.cst = sb("cst", [128, 14, 128])
            self.identb = sb("identb", [128, 128], BF16)
            self.onesb = sb("onesb", [128, 128], BF16)
            self.pvt = sb("pvt", [128, NPV])
            self.modv = sb("modv", [128, 6, KC, 2])
            self.ba = sb("ba", [128, self.NB, 16])
            S.dma(self.cst[:].rearrange("p a b -> p (a b)"), self.consts[:, :], writes=['cst'])
            S.op('dve', lambda e: e.tensor_copy(self.identb[:], self.cst[:, 0, :]), ['cst'], ['identb'])
            S.op('dve', lambda e: e.tensor_copy(self.onesb[:], self.cst[:, 1, :]), ['cst'], ['onesb'])
            S.barrier()
            S.dummy_w = self.identb[:]
            for (t0, tn, s) in self.tiles(2048):
                for k in range(KC):
                    S.dma(self.xT[k * 128:(k + 1) * 128, t0:t0 + tn], self.xT_in[k * 128:(k + 1) * 128, t0:t0 + tn],
                          writes=[('xT', k, t0)])
            S.barrier()
            for l in range(self.depth):
                self.layer(l)
            for k in range(KC):
                S.dma(self.outT[k * 128:(k + 1) * 128, :], self.xT[k * 128:(k + 1) * 128, self.Lc:self.Lt])
            S.barrier()
        return nc

    def ident(self):
        return self.cst[:, 0, :]

    def ones32(self):
        return self.cst[:, 1, :]

    def pvs(self, off, n=1):
        return self.pvt[:, off:off + n]

    def rstd_from_ss(self, ss_ps, out_sb, n, inv_n, keys_r, keys_w, tmp):
        S = self.S
        S.op('act', lambda e: e.activation(tmp[:, :n], ss_ps[:, :n], AF.Sqrt, bias=EPS, scale=inv_n), keys_r, [('tmp', id(tmp))])
        S.op('dve', lambda e: e.reciprocal(out_sb[:, :n], tmp[:, :n]), [('tmp', id(tmp))], keys_w)

    def load_weight_bf(self, st, name, dram2d, K, N, piece=512, kpiece=None):
        S = self.S
        w = self.sb(name, [128, K, N], BF16, st)
        src = dram2d.rearrange("(k p) n -> p k n", p=128)
        with contextlib.ExitStack() as s2:
            if kpiece is None:
                stg = [self.sb(name + "_stg%d" % i, [128, K, piece], F32, s2) for i in range(2)]
                i = 0
                for c0 in range(0, N, piece):
                    n = min(piece, N - c0)
                    sg = stg[i % 2]
                    S.dma(sg[:, :, :n], src[:, :, c0:c0 + n], writes=[(name, 'stg', i % 2)])
                    S.op('pool', lambda e: e.tensor_copy(w[:, :, c0:c0 + n], sg[:, :, :n]), [(name, 'stg', i % 2)], [(name, 'w', i)])
                    i += 1
            else:
                stg = [self.sb(name + "_stg%d" % i, [128, kpiece, N], F32, s2) for i in range(2)]
                i = 0
                for k0 in range(0, K, kpiece):
                    sg = stg[i % 2]
                    S.dma(sg[:], src[:, k0:k0 + kpiece, :], writes=[(name, 'stg', i % 2)])
                    S.op('pool', lambda e: e.tensor_copy(w[:, k0:k0 + kpiece, :], sg[:]), [(name, 'stg', i % 2)], [(name, 'w', i)])
                    i += 1
            S.barrier()
        return w, []

    def layer(self, l):
        for nm in PHASES:
            getattr(self, 'phase_' + nm)(l)

    def phase_ada(self, l):
        nc, S = self.nc, self.S
        with contextlib.ExitStack() as st:
            sb = lambda n, s, dt=F32: self.sb(n, s, dt, st)
            S.dma(self.pvt[:], self.pv[l], writes=['pvt'])
            cv = sb("cv", [128, KC, 2])
            sc = sb("sc", [128, KC, 2])
            mod = sb("mod", [128, 48, 2])
            S.dma(cv[:], self.cvec[:, :, :], writes=['cv'])
            S.op('act', lambda e: e.activation(sc[:], cv[:], AF.Silu), ['cv'], ['sc'])
            stg = [sb("ada_stg%d" % i, [128, KC, 512]) for i in range(2)]
            ps = st.enter_context(self.pst("ada_ps", [128, 48, 2], F32))
            src = self.w_ada[l].rearrange("(k p) n -> p k n", p=128)
            for pc in range(12):
                sg = stg[pc % 2]
                S.dma(sg[:], src[:, :, pc * 512:(pc + 1) * 512], writes=[('adastg', pc % 2)])
                for jj in range(4):
                    j = pc * 4 + jj
                    for k in range(KC):
                        S.op('pe', lambda e: e.matmul(ps[:, j, :], sg[:, k, jj * 128:(jj + 1) * 128], sc[:, k, :],
                                                      start=(k == 0), stop=(k == KC - 1)),
                             [('adastg', pc % 2), 'sc'], ['adaps'], signal=(k == KC - 1))
            bb = self.pvs(0, 48).unsqueeze(2).to_broadcast([128, 48, 2])
            S.op('dve', lambda e: e.tensor_tensor(mod[:], ps[:], bb, ALU.add), ['adaps', 'pvt'], ['mod'])
            mv = self.modv
            g = lambda off: self.pvs(off, 8).unsqueeze(2).to_broadcast([128, 8, 2])
            S.op('dve', lambda e: e.scalar_tensor_tensor(mv[:, 0], mod[:, 8:16, :], 1.0, g(48), ALU.add, ALU.mult), ['mod', 'pvt'], ['modv0'])
            S.op('dve', lambda e: e.tensor_copy(mv[:, 1], mod[:, 0:8, :]), ['mod'], ['modv1'])
            S.op('dve', lambda e: e.tensor_tensor(mv[:, 2], mod[:, 16:24, :], g(56), ALU.mult), ['mod', 'pvt'], ['modv2'])
            S.op('dve', lambda e: e.scalar_tensor_tensor(mv[:, 3], mod[:, 32:40, :], 1.0, g(64), ALU.add, ALU.mult), ['mod', 'pvt'], ['modv3'])
            S.op('dve', lambda e: e.tensor_copy(mv[:, 4], mod[:, 24:32, :]), ['mod'], ['modv4'])
            S.op('dve', lambda e: e.tensor_tensor(mv[:, 5], mod[:, 40:48, :], g(72), ALU.mult), ['mod', 'pvt'], ['modv5'])
            S.barrier()

    def norm_mod(self, t0, tn, s, xt, sq, xn, h, rstd, tmp, ss_ps, ia, ib, par, bpar=None, xnk=None, hpar=None):
        S = self.S
        kx = ('xt', par)
        bpar = par if bpar is None else bpar
        hpar = par if hpar is None else hpar
        xnk = [('xn', bpar, k) for k in range(KC)] if xnk is None else xnk
        if True:
            S.dma(xt[:, :, :tn], self.xT.rearrange("(k p) t -> p k t", p=128)[:, :, t0:t0 + tn], writes=[kx])
        S.op('act', lambda e: e.activation(sq[:, :, :tn], xt[:, :, :tn], AF.Square), [kx], [('sq', bpar)])
        for k in range(KC):
            S.op('pe', lambda e: e.matmul(ss_ps[:, :tn], self.onesb[:], sq[:, k, :tn], start=(k == 0), stop=(k == KC - 1)),
                 [('sq', bpar), 'onesb'], [('ss', par)], signal=(k == KC - 1))
        self.rstd_from_ss(ss_ps, rstd, tn, 1.0 / D, [('ss', par)], [('rstd', par)], tmp)
        S.op('dve', lambda e: e.tensor_tensor(xn[:, :, :tn], xt[:, :, :tn], rstd[:, :tn].unsqueeze(1).to_broadcast([128, KC, tn]), ALU.mult),
             [kx, ('rstd', par)], xnk)
        for k in range(KC):
            S.op('pool', lambda e: e.tensor_scalar(h[:, k, :tn], xn[:, k, :tn], self.modv[:, ia, k, s:s + 1], self.modv[:, ib, k, s:s + 1],
                                                   ALU.mult, ALU.add),
                 [xnk[k], 'modv%d' % ia, 'modv%d' % ib], [('h', hpar, k)])

    def phase_proj(self, l):
        nc, S = self.nc, self.S
        with contextlib.ExitStack() as st:
            sb = lambda n, s, dt=F32: self.sb(n, s, dt, st)
            w, wkeys = self.load_weight_bf(st, "win", self.w_in[l], KC, IN_WX)
            xt = [sb("b_xt%d" % i, [128, KC, 512]) for i in range(2)]
            xn = [sb("b_xn%d" % i, [128, KC, 512]) for i in range(2)]
            sq = [sb("b_sq%d" % i, [128, KC, 512], BF16) for i in range(2)]
            h = [sb("b_h%d" % i, [128, KC, 512], BF16) for i in range(2)]
            rstd = [sb("b_rstd%d" % i, [128, 512]) for i in range(2)]
            tmp = [sb("b_tmp%d" % i, [128, 512]) for i in range(2)]
            ost = [sb("b_ost%d" % i, [128, 512]) for i in range(4)]
            ss_ps = [st.enter_context(self.pst("b_ss%d" % i, [128, 512], F32)) for i in range(2)]
            ops = [st.enter_context(self.pst("b_ops%d" % i, [128, 512], F32)) for i in range(4)]
            bps = st.enter_context(self.pst("b_bps", [128, 4, 16], F32))
            groups = []
            for c in range(12):
                groups.append((self.P_qkv, c * 128, c * 128, 128))
            for c in range(4):
                groups.append((self.P_z, c * 128, 1536 + c * 128, 128))
            for c in range(2):
                groups.append((self.P_lx, c * 128, 2064 + c * 128, 128))
            for c in range(2):
                groups.append((self.P_lg, c * 128, 2320 + c * 128, 128))
            for c in range(2):
                groups.append((self.P_cq, c * 128, 2576 + c * 128, 128))
            groups.append((self.P_ckv, 0, 2832, 128))
            groups.append((self.P_kr, 0, 2960, 64))
            ei = 0
            for ti, (t0, tn, s) in enumerate(self.tiles(512)):
                p = ti % 2
                self.norm_mod(t0, tn, s, xt[p], sq[p], xn[p], h[p], rstd[p], tmp[p], ss_ps[p], 0, 1, p)
                hk = [('h', p, k) for k in range(KC)]
                for gi, (dr, r0, c0, m) in enumerate(groups):
                    o = ops[ei % 4]
                    og = ost[ei % 4]
                    for k in range(KC):
                        S.op('pe', lambda e: e.matmul(o[:m, :tn], w[:, k, c0:c0 + m], h[p][:, k, :tn], start=(k == 0), stop=(k == KC - 1)),
                             hk + wkeys, [('ops', ei % 4)], signal=(k == KC - 1))
                    if ei % 2 == 0:
                        S.op('act', lambda e: e.copy(og[:m, :tn], o[:m, :tn]), [('ops', ei % 4)], [('ost', ei % 4)])
                    else:
                        S.op('dve', lambda e: e.tensor_copy(og[:m, :tn], o[:m, :tn]), [('ops', ei % 4)], [('ost', ei % 4)])
                    S.dma(dr[r0:r0 + m, t0:t0 + tn], og[:m, :tn], reads=[('ost', ei % 4)])
                    ei += 1
                nblk = tn // 128
                for bi in range(nblk):
                    for k in range(KC):
                        S.op('pe', lambda e: e.matmul(bps[:, bi, :], h[p][:, k, bi * 128:(bi + 1) * 128], w[:, k, 2048:2064],
                                                      start=(k == 0), stop=(k == KC - 1)),
                             hk + wkeys, ['bps'], signal=(k == KC - 1))
                b0 = t0 // 128
                S.op('dve', lambda e: e.tensor_copy(self.ba[:, b0:b0 + nblk, :], bps[:, :nblk, :]), ['bps'], [('ba', ti)])
                if self.dbg:
                    S.dma(self.dbg_ba[:, b0 * 16:(b0 + nblk) * 16], self.ba[:, b0:b0 + nblk, :].rearrange('p b c -> p (b c)'), reads=[('ba', ti)])
            S.barrier()

    def conv4(self, acc, xb, n, woff, kr, kw):
        S = self.S
        wv = lambda j: self.pvt[:, woff + j:woff + j + 1]
        S.op('dve', lambda e: e.tensor_scalar(acc[:, :n], xb[:, 0:n], wv(0), None, ALU.mult), kr + ['pvt'], kw)
        S.op('dve', lambda e: e.scalar_tensor_tensor(acc[:, :n], xb[:, 1:n + 1], wv(1), acc[:, :n], ALU.mult, ALU.add), kr + kw + ['pvt'], kw)
        S.op('dve', lambda e: e.scalar_tensor_tensor(acc[:, :n], xb[:, 2:n + 2], wv(2), acc[:, :n], ALU.mult, ALU.add), kr + kw + ['pvt'], kw)
        S.op('dve', lambda e: e.scalar_tensor_tensor(acc[:, :n], xb[:, 3:n + 3], wv(3), acc[:, :n], ALU.mult, ALU.add), kr + kw + ['pvt'], kw)

    def load_halo(self, xb, dram_rows, t0, tn, par, key):
        S = self.S
        seg0, seg1 = (0, self.Lc) if t0 < self.Lc else (self.Lc, self.Lt)
        a = max(seg0, t0 - 2)
        b = min(seg1, t0 + tn + 1)
        k = (key, par)
        if a > t0 - 2:
            S.op('pool', lambda e: e.memset(xb[:, 0:2], 0.0), [], [k])
        if b < t0 + tn + 1:
            S.op('pool', lambda e: e.memset(xb[:, tn + 2:tn + 3], 0.0), [k], [k])
        S.dma(xb[:, a - (t0 - 2):b - (t0 - 2)], dram_rows[:, a:b], reads=[k], writes=[k])
        return k

    def phase_gdn_prep(self, l):
        nc, S = self.nc, self.S
        with contextlib.ExitStack() as st:
            sb = lambda n, s, dt=F32: self.sb(n, s, dt, st)
            xb = [sb("c_xb%d" % i, [128, 515]) for i in range(2)]
            xbb = [sb("c_xbb%d" % i, [128, 516], BF16) for i in range(2)]
            dg = sb("c_dg", [128, 48, 128], BF16)
            for i in range(48):
                S.op('dve', lambda e: e.tensor_scalar(dg[:, i, :], self.ident(), self.pvt[:, 80 + i:81 + i], None, ALU.mult), ['cst', 'pvt'], [('dg', i)])
            cps = [st.enter_context(self.pst("c_cps%d" % i, [128, 512], F32)) for i in range(2)]
            sl = [sb("c_sl%d" % i, [128, 512]) for i in range(2)]
            sq = [sb("c_sq%d" % i, [128, 512], BF16) for i in range(2)]
            rs = [sb("c_rs%d" % i, [128, 512]) for i in range(2)]
            tmp = [sb("c_tmp%d" % i, [128, 512]) for i in range(2)]
            ob = [sb("c_ob%d" % i, [128, 512], BF16) for i in range(2)]
            ss = [st.enter_context(self.pst("c_ss%d" % i, [128, 512], F32)) for i in range(2)]
            it = 0
            for c in range(12):
                part = c // 4
                rows = self.P_qkv[c * 128:(c + 1) * 128, :]
                for (t0, tn, s) in self.tiles(512):
                    p = it % 2
                    it += 1
                    kx = self.load_halo(xb[p], rows, t0, tn, p, 'cxb')
                    S.op('pool', lambda e: e.tensor_copy(xbb[p][:, :tn + 3], xb[p][:, :tn + 3]), [kx], [('cxbb', p)])
                    for j in range(4):
                        S.op('pe', lambda e: e.matmul(cps[p][:, :tn], dg[:, c * 4 + j, :], xbb[p][:, j:j + tn], start=(j == 0), stop=(j == 3)),
                             [('cxbb', p), ('dg', c * 4 + j)], [('ccps', p)])
                    S.op('act', lambda e: e.activation(sl[p][:, :tn], cps[p][:, :tn], AF.Silu), [('ccps', p)], [('csl', p)])
                    if part < 2:
                        S.op('act', lambda e: e.activation(sq[p][:, :tn], sl[p][:, :tn], AF.Square), [('csl', p)], [('csq', p)])
                        S.op('pe', lambda e: e.matmul(ss[p][:, :tn], self.onesb[:], sq[p][:, :tn], start=True, stop=True),
                             [('csq', p), 'onesb'], [('css', p)])
                        self.rstd_from_ss(ss[p], rs[p], tn, 1.0, [('css', p)], [('crs', p)], tmp[p])
                        scale = (128.0 ** -0.5) if part == 0 else 1.0
                        S.op('dve', lambda e: e.scalar_tensor_tensor(ob[p][:, :tn], sl[p][:, :tn], scale, rs[p][:, :tn], ALU.mult, ALU.mult),
                             [('csl', p), ('crs', p)], [('cob', p)])
                    else:
                        S.op('dve', lambda e: e.tensor_copy(ob[p][:, :tn], sl[p][:, :tn]), [('csl', p)], [('cob', p)])
                    S.dma(self.qkvT[c * 128:(c + 1) * 128, t0:t0 + tn], ob[p][:, :tn], reads=[('cob', p)])
            S.barrier()

    def phase_lru(self, l):
        nc, S = self.nc, self.S
        Lt, Lc = self.Lt, self.Lc
        with contextlib.ExitStack() as st:
            sb = lambda n, s, dt=F32: self.sb(n, s, dt, st)
            wst = sb("l_wst", [128, 8, 128])
            wbf = sb("l_wbf", [128, 8, 128], BF16)
            S.dma(wst[:], self.lru_w[l].rearrange("a d c p n -> p (a d c) n"), writes=['lwst'])
            S.op('dve', lambda e: e.tensor_copy(wbf[:], wst[:]), ['lwst'], ['lwbf'])
            cs = sb("l_cs", [128, 4])
            S.op('act', lambda e: e.activation(cs[:], self.pvs(156, 4), AF.Exp, scale=-1.0), ['pvt'], ['lcs'])
            S.op('act', lambda e: e.activation(cs[:], cs[:], AF.Ln, bias=1.0), ['lcs'], ['lcs'])
            S.op('dve', lambda e: e.tensor_scalar(cs[:], cs[:], -8.0, None, ALU.mult), ['lcs'], ['lcs'])
            xb = sb("l_xb", [128, Lt + 6])
            xc = sb("l_xc", [128, Lt])
            xcb = sb("l_xcb", [128, Lt], BF16)
            rr = sb("l_r", [128, Lt])
            ii = sb("l_i", [128, Lt])
            aa = sb("l_a", [128, Lt])
            uu = sb("l_u", [128, Lt])
            hf = sb("l_hf", [128, Lt])
            hb = sb("l_hb", [128, Lt])
            gt = sb("l_gt", [128, Lt])
            yo = sb("l_yo", [128, Lt], BF16)
            gps = [st.enter_context(self.pst("l_gps%d" % i, [128, 512], F32)) for i in range(4)]
            for c in range(2):
                S.op('pool', lambda e: e.memset(xb[:], 0.0), [], ['lxb'])
                rows = self.P_lx[c * 128:(c + 1) * 128, :]
                S.dma(xb[:, 2:2 + Lc], rows[:, 0:Lc], reads=['lxb'], writes=['lxb'])
                S.dma(xb[:, Lc + 5:Lc + 5 + self.Ll], rows[:, Lc:Lt], reads=['lxb'], writes=['lxb'])
                S.dma(gt[:], self.P_lg[c * 128:(c + 1) * 128, :], writes=['lgt'])
                wv = lambda j: self.pvt[:, 140 + c * 4 + j:140 + c * 4 + j + 1]
                for (o0, t0, n) in ((0, 0, Lc), (Lc + 3, Lc, self.Ll)):
                    kk = [('lxc', t0)]
                    S.op('dve', lambda e: e.tensor_scalar(xc[:, t0:t0 + n], xb[:, o0:o0 + n], wv(0), self.pvt[:, 148 + c:149 + c], ALU.mult, ALU.add),
                         ['lxb', 'pvt'], kk)
                    for j in (1, 2, 3):
                        S.op('dve',
                             lambda e: e.scalar_tensor_tensor(xc[:, t0:t0 + n], xb[:, o0 + j:o0 + j + n], wv(j), xc[:, t0:t0 + n], ALU.mult, ALU.add),
                             ['lxb', 'pvt'] + kk, kk)
                kxc = [('lxc', 0), ('lxc', Lc)]
                S.op('act', lambda e: e.copy(xcb[:], xc[:]), kxc, ['lxcb'])
                S.op('act', lambda e: e.activation(gt[:], gt[:], AF.Gelu_apprx_tanh), ['lgt'], ['lgt'])
                tl = self.tiles(512)
                kr = [('lr', t0) for (t0, _, _) in tl]
                ki = [('li', t0) for (t0, _, _) in tl]
                gi = 0
                for d in range(2):
                    for (t0, tn, s_) in tl:
                        for ai, dst in ((0, rr), (1, ii)):
                            g = gps[gi % 4]
                            kg_ = ('lgps', gi % 4)
                            gi += 1
                            widx = (ai * 2 + d) * 2 + c
                            S.op('pe', lambda e: e.matmul(g[:, :tn], wbf[:, widx, :], xcb[:, t0:t0 + tn], start=True, stop=True),
                                 ['lwbf', 'lxcb'], [kg_])
                            bo = (160 if ai == 0 else 164) + d * 2 + c
                            S.op('act', lambda e: e.activation(dst[:, t0:t0 + tn], g[:, :tn], AF.Sigmoid, bias=self.pvt[:, bo:bo + 1]),
                                 [kg_, 'pvt'], [('lr' if ai == 0 else 'li', t0)])
                    ci = d * 2 + c
                    S.op('act', lambda e: e.activation(aa[:], rr[:], AF.Exp, scale=cs[:, ci:ci + 1]), kr + ['lcs'], ['la'])
                    S.op('dve', lambda e: e.tensor_tensor(rr[:], aa[:], aa[:], ALU.mult), ['la'] + kr, kr)
                    S.op('act', lambda e: e.activation(rr[:], rr[:], AF.Sqrt, bias=1.0, scale=-1.0), kr, kr)
                    S.op('pool', lambda e: e.tensor_tensor(uu[:], ii[:], xc[:], ALU.mult), ki + kxc, ['lu'])
                    S.op('dve', lambda e: e.tensor_tensor(uu[:], uu[:], rr[:], ALU.mult), ['lu'] + kr, ['lu'])
                    if d == 0:
                        S.op('dve', lambda e: e.tensor_tensor_scan(hf[:], aa[:], uu[:], 0.0, ALU.mult, ALU.add), ['la', 'lu'], ['lhf'])
                    else:
                        S.op('dve', lambda e: e.tensor_tensor_scan(hb[:, 0:Lc][:, ::-1], aa[:, 0:Lc][:, ::-1], uu[:, 0:Lc][:, ::-1],
                                                                   0.0, ALU.mult, ALU.add),
                             ['la', 'lu'], ['lhb'])
                        S.op('dve', lambda e: e.tensor_tensor_scan(hb[:, Lc:Lt][:, ::-1], aa[:, Lc:Lt][:, ::-1], uu[:, Lc:Lt][:, ::-1],
                                                                   hb[:, 0:1], ALU.mult, ALU.add),
                             ['la', 'lu', 'lhb'], ['lhb'])
                S.op('dve', lambda e: e.tensor_tensor(hf[:], hf[:], hb[:], ALU.add), ['lhf', 'lhb'], ['lhf'])
                S.op('dve', lambda e: e.tensor_tensor(yo[:], hf[:], gt[:], ALU.mult), ['lhf', 'lgt'], ['lyo'])
                S.dma(self.yT[512 + c * 128:512 + (c + 1) * 128, :], yo[:], reads=['lyo'])
            S.barrier()

    def phase_mla(self, l):
        nc, S = self.nc, self.S
        Lt, Lc, NB, NBc = self.Lt, self.Lc, self.NB, self.NBc
        with contextlib.ExitStack() as st:
            sb = lambda n, s, dt=F32: self.sb(n, s, dt, st)
            qT = sb("m_qT", [96, 4, Lt], BF16)
            kT = sb("m_kT", [96, 4, Lt], BF16)
            V = sb("m_V", [128, NB, 4, 65], BF16)
            tab = sb("m_tab", [96, 2, Lt])
            S.dma(tab[64:96, 0, :], self.rope[0], writes=['tab'])
            S.dma(tab[64:96, 1, :], self.rope[1], reads=['tab'], writes=['tab'])
            S.op('pool', lambda e: e.memset(V[:, :, :, 64:65], 1.0), [], ['Vones'])
            with contextlib.ExitStack() as st2:
                sb2 = lambda n, s, dt=F32: self.sb(n, s, dt, st2)
                wq, wqk = self.load_weight_bf(st2, "wuq", self.w_uq[l], 2, 768)
                wkv, wkvk = self.load_weight_bf(st2, "wukv", self.w_ukv[l], 1, 512)
                cq = [sb2("m_cq%d" % i, [128, 2, 512]) for i in range(2)]
                sq = [sb2("m_sq%d" % i, [128, 2, 512], BF16) for i in range(2)]
                cqn = [sb2("m_cqn%d" % i, [128, 2, 512], BF16) for i in range(2)]
                ckv = [sb2("m_ckv%d" % i, [128, 512]) for i in range(2)]
                sk = [sb2("m_sk%d" % i, [128, 512], BF16) for i in range(2)]
                ckn = [sb2("m_ckn%d" % i, [128, 512], BF16) for i in range(2)]
                rs = [sb2("m_rs%d" % i, [128, 512]) for i in range(2)]
                rk = [sb2("m_rk%d" % i, [128, 512]) for i in range(2)]
                tmp = [sb2("m_tmp%d" % i, [128, 512]) for i in range(2)]
                kr = [sb2("m_kr%d" % i, [96, 2, 512]) for i in range(2)]
                r1 = [sb2("m_r1%d" % i, [96, 512]) for i in range(2)]
                r2 = [sb2("m_r2%d" % i, [96, 512]) for i in range(2)]
                ssq = [st2.enter_context(self.pst("m_ssq%d" % i, [128, 512], F32)) for i in range(2)]
                qps = [st2.enter_context(self.pst("m_qps%d" % i, [128, 512], F32)) for i in range(4)]
                vps = st2.enter_context(self.pst("m_vps", [128, 4, 64], F32))
                qi = 0
                for ti, (t0, tn, s) in enumerate(self.tiles(512)):
                    p = ti % 2
                    S.dma(cq[p][:, :, :tn], self.P_cq.rearrange("(k p) t -> p k t", p=128)[:, :, t0:t0 + tn], writes=[('mcq', p)])
                    S.op('act', lambda e: e.activation(sq[p][:, :, :tn], cq[p][:, :, :tn], AF.Square), [('mcq', p)], [('msq', p)])
                    for k in range(2):
                        S.op('pe', lambda e: e.matmul(ssq[p][:, :tn], self.onesb[:], sq[p][:, k, :tn], start=(k == 0), stop=(k == 1)),
                             [('msq', p), 'onesb'], [('mssq', p)], signal=(k == 1))
                    self.rstd_from_ss(ssq[p], rs[p], tn, 1.0 / 256, [('mssq', p)], [('mrs', p)], tmp[p])
                    for k in range(2):
                        S.op('dve', lambda e: e.scalar_tensor_tensor(cqn[p][:, k, :tn], cq[p][:, k, :tn], self.pvt[:, 168 + k:169 + k], rs[p][:, :tn],
                                                                     ALU.mult, ALU.mult),
                             [('mcq', p), ('mrs', p), 'pvt'], [('mcqn', p, k)])
                    kq = [('mcqn', p, 0), ('mcqn', p, 1)]
                    for hh in range(4):
                        qa = qps[qi % 4]
                        qb = qps[(qi + 1) % 4]
                        ka, kb = ('mqps', qi % 4), ('mqps', (qi + 1) % 4)
                        qi += 2
                        for k in range(2):
                            S.op('pe', lambda e: e.matmul(qa[:96, :tn], wq[:, k, hh * 96:(hh + 1) * 96], cqn[p][:, k, :tn], start=(k == 0), stop=(k == 1)),
                                 kq + wqk, [ka], signal=(k == 1))
                        for k in range(2):
                            S.op('pe', lambda e: e.matmul(qb[:96, :tn], wq[:, k, 384 + hh * 96:384 + (hh + 1) * 96], cqn[p][:, k, :tn],
                                                          start=(k == 0), stop=(k == 1)),
                                 kq + wqk, [kb], signal=(k == 1))
                        S.op('act', lambda e: e.copy(qT[0:64, hh, t0:t0 + tn], qa[0:64, :tn]), [ka], [('qTn', hh, t0)])
                        S.op('dve', lambda e: e.tensor_tensor(r1[p][64:96, :tn], qa[64:96, :tn], tab[64:96, 0, t0:t0 + tn], ALU.mult),
                             [ka, 'tab'], [('mr1', p)])
                        S.op('dve', lambda e: e.tensor_tensor(r2[p][64:96, :tn], qb[64:96, :tn], tab[64:96, 1, t0:t0 + tn], ALU.mult),
                             [kb, 'tab'], [('mr2', p)])
                        S.op('pool', lambda e: e.tensor_tensor(qT[64:96, hh, t0:t0 + tn], r1[p][64:96, :tn], r2[p][64:96, :tn], ALU.add),
                             [('mr1', p), ('mr2', p)], [('qTr', hh, t0)])
                    S.dma(ckv[p][:, :tn], self.P_ckv[:, t0:t0 + tn], writes=[('mckv', p)])
                    S.op('act', lambda e: e.activation(sk[p][:, :tn], ckv[p][:, :tn], AF.Square), [('mckv', p)], [('msk', p)])
                    S.op('pe', lambda e: e.matmul(ssq[p][:, :tn], self.onesb[:], sk[p][:, :tn], start=True, stop=True),
                         [('msk', p), 'onesb'], [('mssq', p)])
                    self.rstd_from_ss(ssq[p], rk[p], tn, 1.0 / 128, [('mssq', p)], [('mrk', p)], tmp[p])
                    S.op('dve', lambda e: e.scalar_tensor_tensor(ckn[p][:, :tn], ckv[p][:, :tn], self.pvt[:, 170:171], rk[p][:, :tn], ALU.mult, ALU.mult),
                         [('mckv', p), ('mrk', p), 'pvt'], [('mckn', p)])
                    for hh in range(4):
                        qa = qps[qi % 4]
                        ka = ('mqps', qi % 4)
                        qi += 1
                        S.op('pe', lambda e: e.matmul(qa[:64, :tn], wkv[:, 0, hh * 64:(hh + 1) * 64], ckn[p][:, :tn], start=True, stop=True),
                             [('mckn', p)] + wkvk, [ka])
                        S.op('act', lambda e: e.copy(kT[0:64, hh, t0:t0 + tn], qa[0:64, :tn]), [ka], [('kTn', hh, t0)])
                    for bi in range(tn // 128):
                        S.op('pe', lambda e: e.matmul(vps[:].rearrange("p h d -> p (h d)"), ckn[p][:, bi * 128:(bi + 1) * 128], wkv[:, 0, 256:512],
                                                      start=True, stop=True),
                             [('mckn', p)] + wkvk, ['mvps'])
                        blk = t0 // 128 + bi
                        S.op('dve', lambda e: e.tensor_copy(V[:, blk, :, 0:64], vps[:]), ['mvps'], [('V', blk)])
                    S.dma(kr[p][64:96, 0, :tn], self.P_kr[0:32, t0:t0 + tn], writes=[('mkr', p)])
                    S.dma(kr[p][64:96, 1, :tn], self.P_kr[32:64, t0:t0 + tn], reads=[('mkr', p)], writes=[('mkr', p)])
                    S.op('dve', lambda e: e.tensor_tensor(kr[p][64:96, :, :tn], kr[p][64:96, :, :tn], tab[64:96, :, t0:t0 + tn], ALU.mult),
                         [('mkr', p), 'tab'], [('mkr', p)])
                    S.op('dve', lambda e: e.tensor_tensor(r1[p][64:96, :tn], kr[p][64:96, 0, :tn], kr[p][64:96, 1, :tn], ALU.add),
                         [('mkr', p)], [('mr1', p)])
                    for hh in range(4):
                        S.op('pool', lambda e: e.tensor_copy(kT[64:96, hh, t0:t0 + tn], r1[p][64:96, :tn]), [('mr1', p)], [('kTr', hh, t0)])
                S.barrier()
            pT = [sb("m_pT%d" % i, [128, 512], BF16) for i in range(3)]
            atm = [sb("m_atm%d" % i, [128, 4, 256]) for i in range(2)]
            rec = sb("m_rec", [128, 8])
            yob = [sb("m_yob%d" % i, [128, 2, 512], BF16) for i in range(2)]
            sps = [st.enter_context(self.pst("m_sps%d" % i, [128, 512], F32)) for i in range(2)]
            acc = [st.enter_context(self.pst("m_acc%d" % i, [128, 512], F32)) for i in range(4)]
            tps = [st.enter_context(self.pst("m_tps%d" % i, [128, 512], F32)) for i in range(2)]
            scl = 96.0 ** -0.5
            si = 0
            ri = 0
            for ti, (t0, tn, s) in enumerate(self.tiles(512)):
                p = ti % 2
                nq = tn // 128
                kblocks = list(range(NBc)) if s == 1 else list(range(NB))
                items = [(hh, kb) for hh in range(4) for kb in kblocks]

                def emit_s(j):
                    hh, kb = items[j]
                    sp_ = sps[j % 2]
                    S.op('pe', lambda e: e.matmul(sp_[:, :tn], kT[0:96, hh, kb * 128:(kb + 1) * 128], qT[0:96, hh, t0:t0 + tn], start=True, stop=True),
                         [], [('sps', j % 2)])
                emit_s(0)
                for j, (hh, kb) in enumerate(items):
                    sp_ = sps[j % 2]
                    pt_ = pT[j % 3]
                    ks, kp = ('sps', j % 2), ('pT', j % 3)
                    S.op('act', lambda e: e.activation(pt_[:, :tn], sp_[:, :tn], AF.Exp, scale=scl), [ks], [kp])
                    if j + 1 < len(items):
                        emit_s(j + 1)
                    for qb in range(nq):
                        S.op('pe', lambda e: e.matmul(acc[qb][:, 0:65], pt_[:, qb * 128:(qb + 1) * 128], V[:, kb, hh, :],
                                                      start=(kb == kblocks[0]), stop=(kb == kblocks[-1])),
                             [kp], [('acc', qb)])
                    if kb == kblocks[-1]:
                        for qb in range(nq):
                            rc = rec[:, ri % 8:ri % 8 + 1]
                            kr_ = ('rec', ri % 8)
                            ri += 1
                            S.op('dve', lambda e: e.reciprocal(rc, acc[qb][:, 64:65]), [('acc', qb)], [kr_])
                            S.op('dve', lambda e: e.tensor_scalar(atm[p][:, qb, hh * 64:(hh + 1) * 64], acc[qb][:, 0:64], rc, None, ALU.mult),
                                 [('acc', qb), kr_], [('atm', p, qb, hh)])
                for qb in range(nq):
                    for c in range(2):
                        tp = tps[(qb * 2 + c) % 2]
                        kt = ('tps', (qb * 2 + c) % 2)
                        S.op('pe', lambda e: e.transpose(tp[:, 0:128], atm[p][:, qb, c * 128:(c + 1) * 128], self.ident()),
                             [('atm', p, qb, hh) for hh in range(4)] + ['cst'], [kt])
                        S.op('act', lambda e: e.copy(yob[p][:, c, qb * 128:(qb + 1) * 128], tp[:, 0:128]), [kt], [('yob', p, c, qb)])
                for c in range(2):
                    S.dma(self.yT[768 + c * 128:768 + (c + 1) * 128, t0:t0 + tn], yob[p][:, c, :tn], reads=[('yob', p, c, qb) for qb in range(nq)])
            S.barrier()

    def phase_gdn(self, l):
        nc, S = self.nc, self.S
        Lt, Lc, NB, NBc = self.Lt, self.Lc, self.NB, self.NBc
        with contextlib.ExitStack() as st:
            sb = lambda n, s, dt=F32: self.sb(n, s, dt, st)
            oacc = sb("g_oacc", [128, 4, Lt])
            beta = sb("g_beta", [128, NB, 8])
            nbeta = sb("g_nbeta", [128, NB, 8])
            gg = sb("g_gg", [128, NB, 8])
            t1 = sb("g_t1", [128, NB, 8])
            t2 = sb("g_t2", [128, NB, 8])
            negA = sb("g_negA", [128, 8])
            kba = [('ba', ti) for ti in range(len(self.tiles(512)))]
            if GSTOP < -1:
                S.barrier()
                return
            S.op('act', lambda e: e.activation(beta[:], self.ba[:, :, 0:8], AF.Sigmoid), kba, ['gbeta'])
            S.op('dve', lambda e: e.tensor_scalar(nbeta[:], beta[:], -1.0, None, ALU.mult), ['gbeta'], ['gnbeta'])
            S.op('dve', lambda e: e.tensor_tensor(t1[:], self.ba[:, :, 8:16], self.pvs(128, 8).unsqueeze(1).to_broadcast([128, NB, 8]), ALU.add),
                 kba + ['pvt'], ['gt1'])
            S.op('act', lambda e: e.activation(t2[:], t1[:], AF.Abs), ['gt1'], ['gt2'])
            S.op('act', lambda e: e.activation(t2[:], t2[:], AF.Exp, scale=-1.0), ['gt2'], ['gt2'])
            S.op('act', lambda e: e.activation(t2[:], t2[:], AF.Ln, bias=1.0), ['gt2'], ['gt2'])
            S.op('dve', lambda e: e.scalar_tensor_tensor(t1[:], t1[:], 0.0, t2[:], ALU.max, ALU.add), ['gt1', 'gt2'], ['gt1'])
            S.op('act', lambda e: e.activation(negA[:], self.pvs(172, 8), AF.Exp), ['pvt'], ['gnegA'])
            S.op('dve', lambda e: e.tensor_scalar(negA[:], negA[:], -1.0, None, ALU.mult), ['gnegA'], ['gnegA'])
            S.op('dve', lambda e: e.tensor_tensor(gg[:], t1[:], negA[:].unsqueeze(1).to_broadcast([128, NB, 8]), ALU.mult), ['gt1', 'gnegA'], ['ggg'])

            if GSTOP < 0:
                S.barrier()
                return
            def T(n, dt=F32):
                return [sb("g_%s%d" % (n, i), [128, 4, 128], dt) for i in range(2)]
            qkv = [sb("g_qkv%d" % i, [128, 12, 128], BF16) for i in range(2)]
            GM, EGb, Dm, E, NK, NBm, ub = T("GM"), T("EGb"), T("Dm"), T("E"), T("NK"), T("NBm"), T("ub")
            attnT, Tb = T("attnT", BF16), T("Tb", BF16)
            T1 = lambda n: sb("g1_" + n, [128, 4, 128])
            Nn1, NT1, Xa1, XTa1, Xb1, XTb1, Qa1, Qb1 = [T1(n) for n in ("Nn", "NT", "Xa", "XTa", "Xb", "XTb", "Qa", "Qb")]
            Ba1, BTa1, Bb1, BTb1, Cc1, C2c1, Nl1, NlT1 = [T1(n) for n in ("Ba", "BTa", "Bb", "BTb", "Cc", "C2c", "Nl", "NlT")]
            kE, kg, vtm, wT, qgT, vnew = T("kE", BF16), T("kg", BF16), T("vtm", BF16), T("wT", BF16), T("qgT", BF16), T("vnew", BF16)
            gsm = [sb("g_gsm%d" % i, [128, 16]) for i in range(2)]
            gla = [sb("g_gla%d" % i, [128, 4]) for i in range(2)]
            S32 = sb("g_S32", [128, 4, 128])
            Sbf = sb("g_Sbf", [128, 4, 128], BF16)
            pf = [st.enter_context(self.pst("g_pf%d" % i, [128, 4, 128], F32)) for i in range(6)]
            pb = [st.enter_context(self.pst("g_pb%d" % i, [128, 4, 128], BF16)) for i in range(2)]
            cnt = dict(f=0, b=0)

            def PF():
                i = cnt['f'] % 6
                cnt['f'] += 1
                return pf[i], ('pf', i)

            def PB():
                i = cnt['b'] % 2
                cnt['b'] += 1
                return pb[i], ('pb', i)

            bc_h = lambda ap2: ap2.unsqueeze(1).to_broadcast([128, 4, 128])
            onesH = sb("g_onesH", [128, 4, 128])
            MdH = [sb("g_MdH%d" % i, [128, 4, 128]) for i in range(2)]
            strictH = [sb("g_strictH%d" % i, [128, 4, 128]) for i in range(2)]
            S.op('dve', lambda e: e.memset(onesH[:], 1.0), [], ['onesH'])
            maskH = {}
            for mi_ in (0, 10, 11, 12, 13):
                maskH[mi_] = sb("g_maskH%d" % mi_, [128, 4, 128])
                S.op('dve', lambda e: e.tensor_tensor(maskH[mi_][:], onesH[:], bc_h(self.cst[:, mi_, :]), ALU.mult), ['onesH', 'cst'], ['cst'])
            for dd in range(2):
                S.op('dve', lambda e: e.tensor_tensor(MdH[dd][:], onesH[:], bc_h(self.cst[:, 2 + dd, :]), ALU.mult), ['onesH', 'cst'], [('MdH', dd)])
                S.op('dve', lambda e: e.tensor_tensor(strictH[dd][:], onesH[:], bc_h(self.cst[:, 6 + dd, :]), ALU.mult), ['onesH', 'cst'], [('strictH', dd)])
            bc_i = lambda ap2: ap2.unsqueeze(2).to_broadcast([128, 4, 128])
            qsrc = self.qkvT.rearrange("(c p) t -> p c t", p=128)
            it = 0
            for d in range(2):
                Md, negm, strict = self.cst[:, 2 + d, :], self.cst[:, 4 + d, :], self.cst[:, 6 + d, :]
                if GSKIP != 2:
                    S.op('pool', lambda e: e.memset(S32[:], 0.0), [('S32', h) for h in range(4)], [('S32', h) for h in range(4)])
                    S.op('pool', lambda e: e.memset(Sbf[:], 0.0), [('Sbf', h) for h in range(4)], [('Sbf', h) for h in range(4)])
                if d == 0:
                    order = list(range(NB))
                else:
                    order = list(range(NBc - 1, -1, -1)) + list(range(NB - 1, NBc - 1, -1))
                for b in order:
                    p = it % 2
                    it += 1
                    K = lambda *n: n + (p,)
                    tok = slice(b * 128, (b + 1) * 128)
                    if GSKIP != 1:
                        S.dma(qkv[p][:], qsrc[:, :, tok], writes=[K('qkv')])
                    qTb = lambda h: qkv[p][:, h, :]
                    kTb = lambda h: qkv[p][:, 4 + h, :]
                    vTb = lambda h: qkv[p][:, 8 + h, :]
                    gcol = gg[:, b, d * 4:(d + 1) * 4]
                    if GSTOP < 1:
                        continue
                    S.op('dve', lambda e: e.tensor_tensor(GM[p][:], MdH[d][:], bc_i(gcol), ALU.mult), [('MdH', d), 'ggg'], [K('GM')])
                    if GSUB < 1:
                        continue
                    gp, kgp = PF()
                    gpv = gp[:].rearrange("p h c -> p (h c)")
                    S.op('pe', lambda e: e.matmul(gpv[:, 0:4], Md, gcol, start=True, stop=True), ['cst', 'ggg'], [kgp], signal=False)
                    S.op('pe', lambda e: e.matmul(gpv[:, 4:8], self.ones32(), gcol, start=True, stop=True), ['cst', 'ggg'], [kgp])
                    if GSUB < 2:
                        continue
                    S.op('dve', lambda e: e.tensor_copy(gsm[p][:, 0:8], gpv[:, 0:8]), [kgp], [K('gsm')])
                    S.op('act', lambda e: e.activation(gsm[p][:, 8:12], gsm[p][:, 0:4], AF.Exp), [K('gsm')], [K('gsm2')])
                    S.op('dve', lambda e: e.tensor_tensor(gsm[p][:, 12:16], gsm[p][:, 4:8], gsm[p][:, 0:4], ALU.subtract), [K('gsm')], [K('gsm3')])
                    S.op('act', lambda e: e.activation(gsm[p][:, 12:16], gsm[p][:, 12:16], AF.Exp), [K('gsm3')], [K('gsm3')])
                    S.op('act', lambda e: e.activation(gla[p][:], gsm[p][:, 4:8], AF.Exp), [K('gsm')], [K('gla')])
                    if GSUB < 3:
                        continue
                    gb, kgb = PF()
                    for h in range(4):
                        if GSKIP == 4:
                            break
                        S.op('pe', lambda e: e.matmul(gb[:, h, :], self.ones32(), GM[p][:, h, :], start=True, stop=True),
                             ['cst', K('GM')], [kgb], signal=(h == 3))
                    if GSKIP == 5000:
                        S.op('act', lambda e: e.activation(EGb[p][:].rearrange("p h c -> p (h c)"), gb[:].rearrange("p h c -> p (h c)"), AF.Exp), [kgb], [K('EGb')])
                    elif GSKIP != 7:
                        S.op('dve', lambda e: e.tensor_copy(EGb[p][:], gb[:]), [kgb], [K('EGb')])
                        S.op('act', lambda e: e.activation(EGb[p][:], EGb[p][:], AF.Exp), [K('EGb')], [K('EGb')])
                    elif GSKIP != 3:
                        S.op('act', lambda e: e.activation(EGb[p][:], gb[:], AF.Exp), [kgb], [K('EGb')])
                    if GSUB < 4:
                        continue
                    S.op('dve', lambda e: e.tensor_tensor(Dm[p][:], gb[:], bc_i(gsm[p][:, 0:4]), ALU.subtract), [kgb, K('gsm')], [K('Dm')])
                    S.op('dve', lambda e: e.scalar_tensor_tensor(Dm[p][:], Dm[p][:], 0.0, bc_h(negm), ALU.min, ALU.add), [K('Dm'), 'cst'], [K('Dm')])
                    S.op('act', lambda e: e.activation(E[p][:], Dm[p][:], AF.Exp), [K('Dm')], [K('E')])
                    if GSUB < 5:
                        continue
                    S.op('dve', lambda e: e.tensor_tensor(NBm[p][:], strictH[d][:], bc_i(nbeta[:, b, d * 4:(d + 1) * 4]), ALU.mult),
                         [('strictH', d), 'gnbeta'], [K('NBm')])
                    S.op('dve', lambda e: e.tensor_tensor(qgT[p][:], qkv[p][:, 0:4, :], EGb[p][:], ALU.mult), [K('qkv'), K('EGb')], [K('qgT')])
                    if GSTOP < 2:
                        continue
                    tk, ktk = PB()
                    for h in range(4):
                        S.op('pe', lambda e: e.transpose(tk[:, h, :], kTb(h), self.identb[:]), [K('qkv'), 'identb'], [ktk], signal=(h == 3))
                    S.op('dve', lambda e: e.tensor_tensor(kE[p][:], tk[:], bc_i(gsm[p][:, 8:12]), ALU.mult), [ktk, K('gsm2')], [K('kE')])
                    S.op('dve', lambda e: e.tensor_tensor(kg[p][:], tk[:], bc_i(gsm[p][:, 12:16]), ALU.mult), [ktk, K('gsm3')], [K('kg')])
                    tv, ktv = PB()
                    for h in range(4):
                        S.op('pe', lambda e: e.transpose(tv[:, h, :], vTb(h), self.identb[:]), [K('qkv'), 'identb'], [ktv], signal=(h == 3))
                    S.op('act', lambda e: e.copy(vtm[p][:], tv[:]), [ktv], [K('vtm')])
                    if GSTOP < 3:
                        continue
                    kk, kkk = PF()
                    for h in range(4):
                        S.op('pe', lambda e: e.matmul(kk[:, h, :], kTb(h), kTb(h), start=True, stop=True), [K('qkv')], [kkk], signal=(h == 3))
                    qk, kqk = PF()
                    for h in range(4):
                        S.op('pe', lambda e: e.matmul(qk[:, h, :], kTb(h), qTb(h), start=True, stop=True), [K('qkv')], [kqk], signal=(h == 3))
                    S.op('dve', lambda e: e.tensor_tensor(attnT[p][:], qk[:], E[p][:], ALU.mult), [kqk, K('E')], [K('attnT')])
                    S.op('dve', lambda e: e.tensor_tensor(NK[p][:], kk[:], E[p][:], ALU.mult), [kkk, K('E')], [K('NK')])
                    kN, kNT = ('i', 'Nn'), ('i', 'NT')
                    mk = lambda i: maskH[i][:]
                    id32 = self.ident()
                    S.op('pool', lambda e: e.tensor_tensor(Nn1[:], NK[p][:], NBm[p][:], ALU.mult), [K('NK'), K('NBm')], [kN])
                    tnf, ktnf = PF()
                    for h in range(4):
                        S.op('pe', lambda e: e.transpose(tnf[:, h, :], Nn1[:, h, :], id32), [kN, 'cst'], [ktnf])
                    S.op('dve', lambda e: e.tensor_copy(NT1[:], tnf[:]), [ktnf], [kNT])
                    if GSTOP < 4:
                        continue
                    S.op('pool', lambda e: e.tensor_tensor(Xa1[:], Nn1[:], mk(10), ALU.mult), [kN, 'cst'], [('i', 'Xa')])
                    S.op('pool', lambda e: e.tensor_tensor(XTa1[:], NT1[:], mk(10), ALU.mult), [kNT, 'cst'], [('i', 'XTa')])
                    S.op('pool', lambda e: e.tensor_copy(Qa1[:], Xa1[:]), [('i', 'Xa')], [('i', 'Qa')])
                    X, XT, kX, kXT = Xa1, XTa1, ('i', 'Xa'), ('i', 'XTa')
                    Qc, kQ = Qa1, ('i', 'Qa')
                    xb_ = [(Xb1, XTb1, ('i', 'Xb'), ('i', 'XTb')), (Xa1, XTa1, ('i', 'Xa'), ('i', 'XTa'))]
                    qb_ = [(Qb1, ('i', 'Qb')), (Qa1, ('i', 'Qa'))]
                    for lv in range(1, 4):
                        X2, X2T, kX2, kX2T = xb_[(lv - 1) % 2]
                        Qn, kQn = qb_[(lv - 1) % 2]
                        a, ka = PF()
                        for h in range(4):
                            S.op('pe', lambda e: e.matmul(a[:, h, :], X[:, h, :], XT[:, h, :], start=True, stop=True), [kX, kXT], [ka])
                        a2, ka2 = (None, None)
                        if lv < 3:
                            a2, ka2 = PF()
                            for h in range(4):
                                S.op('pe', lambda e: e.matmul(a2[:, h, :], XT[:, h, :], X[:, h, :], start=True, stop=True), [kX, kXT], [ka2])
                        S.op('act', lambda e: e.copy(X2T[:], a[:]), [ka], [kX2T])
                        if lv < 3:
                            S.op('dve', lambda e: e.tensor_copy(X2[:], a2[:]), [ka2], [kX2])
                        a3, ka3 = PF()
                        for h in range(4):
                            S.op('pe', lambda e: e.matmul(a3[:, h, :], X2T[:, h, :], Qc[:, h, :], start=True, stop=False), [kX2T, kQ], [ka3])
                            S.op('pe', lambda e: e.matmul(a3[:, h, :], X2T[:, h, :], id32, start=False, stop=True), [kX2T, 'cst'], [ka3])
                        S.op('dve', lambda e: e.tensor_tensor(Qn[:], a3[:], Qc[:], ALU.add), [ka3, kQ], [kQn])
                        X, XT, kX, kXT = X2, X2T, kX2, kX2T
                        Qc, kQ = Qn, kQn
                    tq, ktq = PF()
                    for h in range(4):
                        S.op('pe', lambda e: e.transpose(tq[:, h, :], Qc[:, h, :], id32), [kQ, 'cst'], [ktq])
                    S.op('dve', lambda e: e.tensor_tensor(BTa1[:], tq[:], mk(0), ALU.add), [ktq, 'cst'], [('i', 'BTa')])
                    S.op('pool', lambda e: e.tensor_tensor(Ba1[:], Qc[:], mk(0), ALU.add), [kQ, 'cst'], [('i', 'Ba')])
                    Bc, BTc, kB, kBT = Ba1, BTa1, ('i', 'Ba'), ('i', 'BTa')
                    mb_ = [(Bb1, BTb1, ('i', 'Bb'), ('i', 'BTb')), (Ba1, BTa1, ('i', 'Ba'), ('i', 'BTa'))]
                    for mi, midx in enumerate((11, 12, 13)):
                        Bn, BTn, kBn, kBTn = mb_[mi % 2]
                        S.op('pool', lambda e: e.tensor_tensor(NlT1[:], NT1[:], mk(midx), ALU.mult), [kNT, 'cst'], [('i', 'NlT')])
                        c, kc = PF()
                        for h in range(4):
                            S.op('pe', lambda e: e.matmul(c[:, h, :], NlT1[:, h, :], Bc[:, h, :], start=True, stop=True), [('i', 'NlT'), kB], [kc])
                        S.op('act', lambda e: e.copy(Cc1[:], c[:]), [kc], [('i', 'Cc')])
                        if mi < 2:
                            S.op('pool', lambda e: e.tensor_tensor(Nl1[:], Nn1[:], mk(midx), ALU.mult), [kN, 'cst'], [('i', 'Nl')])
                            c2, kc2 = PF()
                            for h in range(4):
                                S.op('pe', lambda e: e.matmul(c2[:, h, :], Nl1[:, h, :], BTc[:, h, :], start=True, stop=True), [('i', 'Nl'), kBT], [kc2])
                            S.op('act', lambda e: e.copy(C2c1[:], c2[:]), [kc2], [('i', 'C2c')])
                        bn, kbn = PF()
                        for h in range(4):
                            S.op('pe', lambda e: e.matmul(bn[:, h, :], BTc[:, h, :], Cc1[:, h, :], start=True, stop=True), [kBT, ('i', 'Cc')], [kbn])
                        S.op('dve', lambda e: e.tensor_tensor(Bn[:], bn[:], Bc[:], ALU.add), [kbn, kB], [kBn])
                        if mi < 2:
                            bt, kbt = PF()
                            for h in range(4):
                                S.op('pe', lambda e: e.matmul(bt[:, h, :], Bc[:, h, :], C2c1[:, h, :], start=True, stop=True), [kB, ('i', 'C2c')], [kbt])
                            S.op('dve', lambda e: e.tensor_tensor(BTn[:], bt[:], BTc[:], ALU.add), [kbt, kBT], [kBTn])
                        Bc, kB = Bn, kBn
                        if mi < 2:
                            BTc, kBT = BTn, kBTn
                    S.op('pool', lambda e: e.tensor_copy(Tb[p][:], Bc[:]), [kB], [K('Tb')])
                    if GSTOP < 5:
                        continue
                    u_, ku = PF()
                    for h in range(4):
                        S.op('pe', lambda e: e.matmul(u_[:, h, :], Tb[p][:, h, :], vtm[p][:, h, :], start=True, stop=True), [K('Tb'), K('vtm')], [ku])
                    w_, kw = PF()
                    for h in range(4):
                        S.op('pe', lambda e: e.matmul(w_[:, h, :], kE[p][:, h, :], Tb[p][:, h, :], start=True, stop=True), [K('Tb'), K('kE')], [kw])
                    S.op('dve', lambda e: e.tensor_tensor(ub[p][:], u_[:], bc_i(beta[:, b, d * 4:(d + 1) * 4]), ALU.mult), [ku, 'gbeta'], [K('ub')])
                    S.op('act', lambda e: e.copy(wT[p][:], w_[:]), [kw], [K('wT')])
                    if GSTOP < 6:
                        continue
                    if d == 0 and b == 0 and l == 0:
                        fl = lambda t: t[:].rearrange("p h c -> p (h c)")
                        self.dump("gsm", gsm[p][:], [K('gsm'), K('gsm2'), K('gsm3')])
                        self.dump("E", fl(E[p]), [K('E')])
                        self.dump("EGb", fl(EGb[p]), [K('EGb')])
                        self.dump("attnT", fl(attnT[p]), [K('attnT')])
                        self.dump("Tb", fl(Tb[p]), [K('Tb')])
                        self.dump("ub", fl(ub[p]), [K('ub')])
                        self.dump("wT", fl(wT[p]), [K('wT')])
                        self.dump("kE", fl(kE[p]), [K('kE')])
                        self.dump("kg", fl(kg[p]), [K('kg')])
                        self.dump("vtm", fl(vtm[p]), [K('vtm')])
                        self.dump("qgT", fl(qgT[p]), [K('qgT')])
                    p1, kp1 = PF()
                    for h in range(4):
                        S.op('pe', lambda e: e.matmul(p1[:, h, :], wT[p][:, h, :], Sbf[:, h, :], start=True, stop=True), [K('wT'), ('Sbf', h)], [kp1],
                             signal=(h == 3))
                    for h in range(4):
                        nb_ = nbeta[:, b, d * 4 + h:d * 4 + h + 1]
                        S.op('dve', lambda e: e.scalar_tensor_tensor(vnew[p][:, h, :], p1[:, h, :], nb_, ub[p][:, h, :], ALU.mult, ALU.add),
                             [kp1, 'gnbeta', K('ub')], [K('vnew', h)])
                    o_, ko = PF()
                    for h in range(4):
                        S.op('pe', lambda e: e.matmul(o_[:, h, :], Sbf[:, h, :], qgT[p][:, h, :], start=True, stop=False), [('Sbf', h), K('qgT')], [ko], signal=False)
                        S.op('pe', lambda e: e.matmul(o_[:, h, :], vnew[p][:, h, :], attnT[p][:, h, :], start=False, stop=True), [K('vnew', h), K('attnT')], [ko],
                             signal=(h == 3))
                    if d == 0:
                        S.op('dve', lambda e: e.tensor_copy(oacc[:, :, tok], o_[:]), [ko], [('oacc', b)])
                    else:
                        S.op('dve', lambda e: e.tensor_tensor(oacc[:, :, tok], o_[:], oacc[:, :, tok], ALU.add), [ko, ('oacc', b)], [('oacc', b)])
                    su, ksu = PF()
                    for h in range(4):
                        S.op('pe', lambda e: e.matmul(su[:, h, :], kg[p][:, h, :], vnew[p][:, h, :], start=True, stop=True), [K('kg'), K('vnew', h)], [ksu],
                             signal=(h == 3))
                    for h in range(4):
                        S.op('dve', lambda e: e.scalar_tensor_tensor(S32[:, h, :], S32[:, h, :], gla[p][:, h:h + 1], su[:, h, :], ALU.mult, ALU.add),
                             [ksu, K('gla'), ('S32', h)], [('S32', h)])
                        S.op('dve', lambda e: e.tensor_copy(Sbf[:, h, :], S32[:, h, :]), [('S32', h)], [('Sbf', h)])
            if GSTOP < 7:
                S.barrier()
                return
            if l == 0:
                self.dump("oacc", oacc[:].rearrange("p h t -> p (h t)"), [('oacc', b) for b in range(NB)])
            S.barrier()
            fl_ = lambda t: t[:].rearrange("p h c -> p (h c)")
            zt = [fl_(GM[i]) for i in range(2)]
            sq = [fl_(attnT[i]) for i in range(2)]
            rs = [fl_(Dm[i]) for i in range(2)]
            tmp = [fl_(E[i]) for i in range(2)]
            on = [fl_(NK[i]) for i in range(2)]
            yo = [fl_(kE[i]) for i in range(2)]
            it = 0
            for (t0, tn, s) in self.tiles(512):
                kb_ = [('oacc', b) for b in range(t0 // 128, (t0 + tn) // 128)]
                for h in range(4):
                    p = it % 2
                    it += 1
                    ss, kss = PF()
                    ssv = ss[:].rearrange("p h c -> p (h c)")
                    S.dma(zt[p][:, :tn], self.P_z[h * 128:(h + 1) * 128, t0:t0 + tn], writes=[('gzt', p)])
                    S.op('act', lambda e: e.activation(zt[p][:, :tn], zt[p][:, :tn], AF.Silu), [('gzt', p)], [('gzt', p)])
                    S.op('act', lambda e: e.activation(sq[p][:, :tn], oacc[:, h, t0:t0 + tn], AF.Square), kb_, [('gsq', p)])
                    S.op('pe', lambda e: e.matmul(ssv[:, :tn], self.onesb[:], sq[p][:, :tn], start=True, stop=True), [('gsq', p), 'onesb'], [kss])
                    self.rstd_from_ss(ssv, rs[p], tn, 1.0 / 128, [kss], [('grs', p)], tmp[p])
                    S.op('dve', lambda e: e.tensor_tensor(on[p][:, :tn], oacc[:, h, t0:t0 + tn], rs[p][:, :tn], ALU.mult), kb_ + [('grs', p)], [('gon', p)])
                    S.op('dve', lambda e: e.scalar_tensor_tensor(yo[p][:, :tn], on[p][:, :tn], self.pvt[:, 136:137], zt[p][:, :tn], ALU.mult, ALU.mult),
                         [('gon', p), ('gzt', p), 'pvt'], [('gyo', p)])
                    S.dma(self.yT[h * 128:(h + 1) * 128, t0:t0 + tn], yo[p][:, :tn], reads=[('gyo', p)])
            S.barrier()

    def post_norm_residual(self, y32, xt, sq, rstd, tmp, ss_ps, t0, tn, s, ic, par, yk, xk, sqk):
        S = self.S
        S.op('act', lambda e: e.activation(sq[:, :, :tn], y32[:, :, :tn], AF.Square), yk, [sqk])
        for k in range(KC):
            S.op('pe', lambda e: e.matmul(ss_ps[:, :tn], self.onesb[:], sq[:, k, :tn], start=(k == 0), stop=(k == KC - 1)),
                 [sqk, 'onesb'], [('pss', par)], signal=(k == KC - 1))
        self.rstd_from_ss(ss_ps, rstd, tn, 1.0 / D, [('pss', par)], [('prstd', par)], tmp)
        S.op('dve', lambda e: e.tensor_tensor(y32[:, :, :tn], y32[:, :, :tn], rstd[:, :tn].unsqueeze(1).to_broadcast([128, KC, tn]), ALU.mult),
             yk + [('prstd', par)], yk)
        for k in range(KC):
            S.op('dve',
                 lambda e: e.scalar_tensor_tensor(xt[:, k, :tn], y32[:, k, :tn], self.modv[:, ic, k, s:s + 1], xt[:, k, :tn], ALU.mult, ALU.add),
                 yk + ['modv%d' % ic] + xk, [('pxo', par, k)])
        S.dma(self.xT.rearrange("(k p) t -> p k t", p=128)[:, :, t0:t0 + tn], xt[:, :, :tn], reads=[('pxo', par, k) for k in range(KC)] + xk)

    def phase_wout(self, l):
        nc, S = self.nc, self.S
        with contextlib.ExitStack() as st:
            sb = lambda n, s, dt=F32: self.sb(n, s, dt, st)
            w, wkeys = self.load_weight_bf(st, "wout", self.w_out[l], KC, D)
            yt = [sb("f_yt%d" % i, [128, KC, 512], BF16) for i in range(2)]
            xt = [sb("f_xt%d" % i, [128, KC, 512]) for i in range(2)]
            y32 = [sb("f_y32%d" % i, [128, KC, 512]) for i in range(2)]
            sq = [sb("f_sq%d" % i, [128, KC, 512], BF16) for i in range(2)]
            rstd = [sb("f_rstd%d" % i, [128, 512]) for i in range(2)]
            tmp = [sb("f_tmp%d" % i, [128, 512]) for i in range(2)]
            ss_ps = [st.enter_context(self.pst("f_ss%d" % i, [128, 512], F32)) for i in range(2)]
            ops = [st.enter_context(self.pst("f_ops%d" % i, [128, 512], F32)) for i in range(4)]
            ei = 0
            for ti, (t0, tn, s) in enumerate(self.tiles(512)):
                p = ti % 2
                S.dma(yt[p][:, :, :tn], self.yT.rearrange("(k p) t -> p k t", p=128)[:, :, t0:t0 + tn], writes=[('fyt', p)])
                S.dma(xt[p][:, :, :tn], self.xT.rearrange("(k p) t -> p k t", p=128)[:, :, t0:t0 + tn], writes=[('fxt', p)])
                for oc in range(KC):
                    o = ops[ei % 4]
                    ko = ('fops', ei % 4)
                    ei += 1
                    for k in range(KC):
                        S.op('pe', lambda e: e.matmul(o[:, :tn], w[:, k, oc * 128:(oc + 1) * 128], yt[p][:, k, :tn], start=(k == 0), stop=(k == KC - 1)),
                             [('fyt', p)] + wkeys, [ko], signal=(k == KC - 1))
                    S.op('act' if oc % 2 else 'dve', lambda e: (e.copy if oc % 2 else e.tensor_copy)(y32[p][:, oc, :tn], o[:, :tn]), [ko], [('fy32', p, oc)])
                self.post_norm_residual(y32[p], xt[p], sq[p], rstd[p], tmp[p], ss_ps[p], t0, tn, s, 2, p,
                                        [('fy32', p, oc) for oc in range(KC)], [('fxt', p)], ('fsq', p))
            S.barrier()

    def phase_mlp(self, l):
        nc, S = self.nc, self.S
        TN = 256
        with contextlib.ExitStack() as st:
            sb = lambda n, s, dt=F32: self.sb(n, s, dt, st)
            w1, w1k = self.load_weight_bf(st, "wm1", self.w_m1[l], KC, 4 * D, piece=256)
            w2, w2k = self.load_weight_bf(st, "wm2", self.w_m2[l], 32, D, kpiece=4)
            xt = [sb("h_xt%d" % i, [128, KC, TN]) for i in range(2)]
            sq = [sb("h_sq%d" % i, [128, KC, TN], BF16) for i in range(1)]
            h = [sb("h_h%d" % i, [128, KC, TN], BF16) for i in range(1)]
            hid = [sb("h_hid%d" % i, [128, 32, TN], BF16) for i in range(1)]
            rl = [sb("h_rl%d" % i, [128, TN], BF16) for i in range(4)]
            y32 = [sb("h_y32%d" % i, [128, KC, TN]) for i in range(1)]
            rstd = [sb("h_rstd%d" % i, [128, TN]) for i in range(2)]
            tmp = [sb("h_tmp%d" % i, [128, TN]) for i in range(2)]
            ss_ps = [st.enter_context(self.pst("h_ss%d" % i, [128, 512], F32)) for i in range(2)]
            ops = [st.enter_context(self.pst("h_ops%d" % i, [128, 512], F32)) for i in range(4)]
            ei = 0
            for ti, (t0, tn, s) in enumerate(self.tiles(TN)):
                p = ti % 2
                if MSTOP < 1:
                    continue
                yxk = [('hy32', oc) for oc in range(KC)]
                self.norm_mod(t0, tn, s, xt[p], sq[0], y32[0], h[0], rstd[p], tmp[p], ss_ps[p], 3, 4, p, bpar='m', xnk=yxk, hpar='m')
                hk = [('h', 'm', k) for k in range(KC)]
                if MSTOP < 2:
                    continue
                for j in range(32):
                    o = ops[ei % 4]
                    ko = ('hops', ei % 4)
                    r_ = rl[ei % 4]
                    kr_ = ('hrl', ei % 4)
                    ei += 1
                    for k in range(KC):
                        S.op('pe', lambda e: e.matmul(o[:, :tn], w1[:, k, j * 128:(j + 1) * 128], h[0][:, k, :tn], start=(k == 0), stop=(k == KC - 1)),
                             hk + w1k, [ko], signal=(k == KC - 1))
                    S.op('act', lambda e: e.activation(r_[:, :tn], o[:, :tn], AF.Relu), [ko], [kr_])
                    S.op('pool' if j % 2 else 'dve', lambda e: e.tensor_tensor(hid[0][:, j, :tn], r_[:, :tn], r_[:, :tn], ALU.mult), [kr_], [('hid', j)])
                hidk = [('hid', j) for j in range(32)]
                if MSTOP < 3:
                    continue
                for oc in range(KC):
                    o = ops[ei % 4]
                    ko = ('hops', ei % 4)
                    ei += 1
                    for j in range(32):
                        S.op('pe', lambda e: e.matmul(o[:, :tn], w2[:, j, oc * 128:(oc + 1) * 128], hid[0][:, j, :tn], start=(j == 0), stop=(j == 31)),
                             hidk + w2k, [ko], signal=(j == 31))
                    S.op('act' if oc % 2 else 'dve', lambda e: (e.copy if oc % 2 else e.tensor_copy)(y32[0][:, oc, :tn], o[:, :tn]), [ko], [('hy32', oc)])
                if MSTOP < 4:
                    continue
                self.post_norm_residual(y32[0], xt[p], sq[0], rstd[p], tmp[p], ss_ps[p], t0, tn, s, 5, p,
                                        [('hy32', oc) for oc in range(KC)], [('xt', p)], ('sq', 'm'))
            S.barrier()


def _pp(v, nch):
    return np.ascontiguousarray(np.asarray(v, np.float32).reshape(nch, 128).T)


def make_consts():
    c = np.zeros((128, 14, 128), np.float32)
    idx = np.arange(128)
    c[:, 0] = np.eye(128)
    c[:, 1] = 1.0
    c[:, 2] = (idx[:, None] <= idx[None, :])
    c[:, 3] = (idx[:, None] >= idx[None, :])
    c[:, 4] = np.where(idx[None, :] >= idx[:, None], 0.0, -30000.0)
    c[:, 5] = np.where(idx[None, :] <= idx[:, None], 0.0, -30000.0)
    c[:, 6] = (idx[None, :] > idx[:, None])
    c[:, 7] = (idx[None, :] < idx[:, None])
    same = lambda n: (idx[:, None] // n) == (idx[None, :] // n)
    c[:, 10] = same(16)
    c[:, 11] = same(32) & ~same(16)
    c[:, 12] = same(64) & ~same(32)
    c[:, 13] = ~same(64)
    return c.reshape(128, 14 * 128)


def make_rope(Lc, Ll):
    rows = Ll // 64
    row = np.repeat(np.arange(rows, dtype=np.float32), 64)
    col = np.tile(np.arange(64, dtype=np.float32), rows)
    half = 16
    inv = (np.float32(10000.0) ** (-np.arange(0, half, 2, dtype=np.float32) / half)).astype(np.float32)
    ang = np.stack([row[:, None] * inv, col[:, None] * inv], axis=1)
    ang = np.concatenate([ang, ang], axis=-1)
    cos = np.cos(ang).reshape(Ll, 32).T
    sin = np.sin(ang).reshape(Ll, 32).T.copy()
    sgn = np.ones(32, np.float32)
    sgn[0:8] = -1
    sgn[16:24] = -1
    sin = sin * sgn[:, None]
    out = np.zeros((2, 32, Lc + Ll), np.float32)
    out[0, :, :Lc] = 1.0
    out[0, :, Lc:] = cos
    out[1, :, Lc:] = sin
    return out


_SWAP = np.concatenate([np.arange(8, 16), np.arange(0, 8), np.arange(24, 32), np.arange(16, 24)])


def prep_shared(inp, depth):
    f = lambda k: np.asarray(inp[k], np.float32)
    pv = np.zeros((depth, 128, NPV), np.float32)
    for l in range(depth):
        pv[l, :, 0:48] = _pp(f('b_ada')[l], 48)
        pv[l, :, 48:56] = _pp(f('g_attn_pre')[l], 8)
        pv[l, :, 56:64] = _pp(f('g_attn_post')[l], 8)
        pv[l, :, 64:72] = _pp(f('g_mlp_pre')[l], 8)
        pv[l, :, 72:80] = _pp(f('g_mlp_post')[l], 8)
        cw = f('gdn_conv_w')[l]
        pv[l, :, 80:128] = cw.reshape(4, 12, 128).transpose(2, 1, 0).reshape(128, 48)
        pv[l, :, 128:136] = f('gdn_dt_bias')[l].reshape(1, 8)
        pv[l, :, 136:137] = f('gdn_norm_w')[l].reshape(128, 1)
        lw = f('lru_conv_w')[l]
        pv[l, :, 140:148] = lw.reshape(4, 2, 128).transpose(2, 1, 0).reshape(128, 8)
        pv[l, :, 148:150] = _pp(f('lru_conv_b')[l], 2)
        pv[l, :, 156:160] = f('lru_lambda')[l].reshape(2, 2, 128).transpose(2, 0, 1).reshape(128, 4)
        pv[l, :, 160:164] = f('lru_b_a')[l].reshape(2, 2, 128).transpose(2, 0, 1).reshape(128, 4)
        pv[l, :, 164:168] = f('lru_b_i')[l].reshape(2, 2, 128).transpose(2, 0, 1).reshape(128, 4)
        pv[l, :, 168:170] = _pp(f('mla_q_norm')[l], 2)
        pv[l, :, 170:171] = f('mla_kv_norm')[l].reshape(128, 1)
        pv[l, :, 172:180] = f('gdn_a_log')[l].reshape(1, 8)
    w_in = f('w_in')
    w_in_x = np.concatenate([w_in, w_in[:, :, 2960:2992][:, :, _SWAP]], axis=2)
    lw = np.zeros((depth, 2, 2, 2, 128, 128), np.float32)
    for ai, nm in enumerate(('lru_w_a', 'lru_w_i')):
        w = f(nm)
        for c in range(2):
            for gsub in range(2):
                lw[:, ai, :, c, gsub * 64:(gsub + 1) * 64, gsub * 64:(gsub + 1) * 64] = w[:, :, c * 2 + gsub]
    wuq = f('mla_w_uq')
    wuq_sw = np.zeros_like(wuq)
    for h in range(4):
        wuq_sw[:, :, h * 96 + 64:(h + 1) * 96] = wuq[:, :, h * 96 + 64:(h + 1) * 96][:, :, _SWAP]
    wuq_x = np.concatenate([wuq, wuq_sw], axis=2)
    wukv = f('mla_w_ukv').reshape(depth, 128, 4, 128)
    wukv_x = np.concatenate([wukv[:, :, :, :64].reshape(depth, 128, 256), wukv[:, :, :, 64:].reshape(depth, 128, 256)], axis=2)
    return dict(pv=pv, consts=make_consts(), w_ada=f('w_ada')[:depth], w_in=np.ascontiguousarray(w_in_x[:depth]),
                lru_w=lw[:depth], w_uq=np.ascontiguousarray(wuq_x[:depth]), w_ukv=np.ascontiguousarray(wukv_x[:depth]),
                w_out=f('w_out')[:depth], w_m1=f('w_mlp1')[:depth], w_m2=f('w_mlp2')[:depth])


def prep_core(inp, b, shared, Lc, Ll):
    x = np.asarray(inp['x'], np.float32)[b]
    ctx = np.asarray(inp['ctx'], np.float32)[b]
    xT = np.ascontiguousarray(np.concatenate([ctx, x], axis=0).T)
    cv = np.stack([_pp(np.asarray(inp['c'], np.float32)[b], 8), _pp(np.asarray(inp['c_ctx'], np.float32), 8)], axis=2)
    m = dict(xT=xT, cvec=np.ascontiguousarray(cv), rope=make_rope(Lc, Ll))
    for k in ('consts', 'w_ada', 'w_in', 'lru_w', 'w_uq', 'w_ukv', 'w_out', 'w_m1', 'w_m2'):
        m[k] = shared[k]
    m['pv'] = shared['pv']
    return m


_PROG_CACHE = {}


def run(inp, depth=DEPTH, ncores=8, dbg=False):
    Ll = inp['x'].shape[1]
    Lc = inp['ctx'].shape[1]
    key = (Lc, Ll, depth, dbg)
    if key not in _PROG_CACHE:
        p1 = Prog(Lc, Ll, depth, dbg)
        p1.build()
        p2 = Prog(Lc, Ll, depth, dbg, needed=p1.S.needed)
        _PROG_CACHE[key] = p2.build()
        print("sched: ops=%d signals %d -> %d, waits %d" % (sum(p1.S.pos.values()), p1.S.nsig, p2.S.nsig, p2.S.nwait))
    nc = _PROG_CACHE[key]
    shared = prep_shared(inp, depth)
    in_maps = [prep_core(inp, b, shared, Lc, Ll) for b in range(ncores)]
    res = run_bass_kernel_spmd(nc, in_maps, core_ids=list(range(ncores)))
    return res


def kernel(**inputs):
    res = run(inputs, DEPTH, 8, False)
    out = np.stack([np.ascontiguousarray(r["outT"].T) for r in res.results], axis=0)
    return out.astype(np.float32)
```

```python
import contextlib
import numpy as np
import ml_dtypes
import concourse.bass as bass
import concourse.mybir as mybir
from concourse.bass_utils import run_bass_kernel_spmd

F32 = mybir.dt.float32
BF16 = mybir.dt.bfloat16
AF = mybir.ActivationFunctionType
ALU = mybir.AluOpType

D = 1024
KC = 8
DEPTH = 4
IN_W = 2992
IN_WX = 3024
EPS = 1e-6
NPV = 180
GSTOP = 99
GSKIP = 0
MSTOP = 99
GSUB = 99
PHASES = ['ada', 'proj', 'gdn_prep', 'lru', 'mla', 'gdn', 'wout', 'mlp']


class Sched:
    NQ = 8
    CE = ('pe', 'act', 'dve', 'pool')

    def __init__(self, nc, st, needed=None):
        self.nc = nc
        self.eng = {'pe': nc.tensor, 'act': nc.scalar, 'dve': nc.vector, 'pool': nc.gpsimd, 'sp': nc.sync}
        self.sem = {}
        self.pos = {}
        self.act = {}
        self.actual_at = {}
        for k in self.CE:
            self.sem[k] = st.enter_context(nc.semaphore('s_' + k))
            self.pos[k] = 0
            self.act[k] = 0
            self.actual_at[k] = [0]
        self.dq = {}
        for q in ('sp', 'act', 'pool'):
            self.dq[q] = dict(sems=[st.enter_context(nc.semaphore('d_%s%d' % (q, i))) for i in range(self.NQ)],
                              vals=[0] * self.NQ, n=0)
        self.seen = {k: {} for k in self.eng}
        self.bufs = {}
        self.nwait = 0
        self.nsig = 0
        self.dummy_w = None
        self.analysis = needed is None
        self.needed = set() if needed is None else needed

    def need(self, ek, ev):
        if ev is None:
            return
        if ev[0] == 'dma':
            _, sem, val = ev
            sid = id(sem)
            if self.seen[ek].get(sid, 0) >= val:
                return
            self.eng[ek].wait_ge(sem, val)
            self.nwait += 1
            self.seen[ek][sid] = val
            return
        src, pos = ev
        if ek == 'pe' and src == 'pe':
            return
        if self.seen[ek].get(src, 0) >= pos:
            return
        if self.analysis:
            self.needed.add((src, pos))
        val = self.actual_at[src][pos]
        assert val is not None, (src, pos)
        self.eng[ek].wait_ge(self.sem[src], val)
        self.nwait += 1
        self.seen[ek][src] = pos

    def op(self, ek, fn, reads=(), writes=(), signal=True, dma=False):
        nw0 = self.nwait
        for k in reads:
            b = self.bufs.get(k)
            if b is not None:
                self.need(ek, b['w'])
        for k in writes:
            b = self.bufs.get(k)
            if b is not None:
                self.need(ek, b['w'])
                for ev in b['r'].values():
                    self.need(ek, ev)
        if ek == 'pe' and self.nwait != nw0 and self.dummy_w is not None:
            self.eng['pe'].ldweights(self.dummy_w)
        if dma:
            q = self.dq[ek]
            i = q['n'] % self.NQ
            q['n'] += 1
            sem = q['sems'][i]
            if q['vals'][i] > 0:
                self.need(ek, ('dma', sem, q['vals'][i]))
            ins = fn(self.eng[ek])
            q['vals'][i] += 16
            ins.then_inc(sem, 16)
            ev = ('dma', sem, q['vals'][i])
            rkey = ('d', id(sem))
        else:
            ins = fn(self.eng[ek])
            self.pos[ek] += 1
            pos = self.pos[ek]
            if self.analysis or (ek, pos) in self.needed:
                self.act[ek] += 1
                ins.then_inc(self.sem[ek], 1)
                self.actual_at[ek].append(self.act[ek])
                self.nsig += 1
            else:
                self.actual_at[ek].append(None)
            ev = (ek, pos)
            rkey = ek
        self._mark(reads, writes, ev, rkey)

    def _mark(self, reads, writes, ev, rkey):
        for k in reads:
            b = self.bufs.get(k)
            if b is None:
                b = self.bufs[k] = dict(w=None, r={})
            b['r'][rkey] = ev
        for k in writes:
            self.bufs[k] = dict(w=ev, r={})

    def dma(self, out, in_, reads=(), writes=(), q='sp'):
        self.op(q, lambda e: e.dma_start(out=out, in_=in_), reads, writes, dma=True)

    def barrier(self):
        for ek in self.eng:
            for f in self.CE:
                if f != ek and self.pos[f] > 0:
                    self.need(ek, (f, self.pos[f]))
            for qn, q in self.dq.items():
                for i in range(self.NQ):
                    if q['vals'][i] > 0:
                        self.need(ek, ('dma', q['sems'][i], q['vals'][i]))
        self.bufs = {}


class Prog:
    def __init__(self, Lc, Ll, depth, dbg=False, needed=None):
        self.Lc, self.Ll, self.depth, self.dbg = Lc, Ll, depth, dbg
        self.needed = needed
        self.Lt = Lc + Ll
        self.NB = self.Lt // 128
        self.NBc = Lc // 128
        nc = self.nc = bass.Bass("TRN2", target_bir_lowering=False)
        Lt = self.Lt
        di = lambda n, s, dt=F32: nc.dram_tensor(n, s, dt, kind="ExternalInput").ap()
        self.xT_in = di("xT", [D, Lt])
        self.cvec = di("cvec", [128, KC, 2])
        self.pv = di("pv", [depth, 128, NPV])
        self.consts = di("consts", [128, 14 * 128])
        self.rope = di("rope", [2, 32, Lt])
        self.w_ada = di("w_ada", [depth, D, 6 * D])
        self.w_in = di("w_in", [depth, D, IN_WX])
        self.lru_w = di("lru_w", [depth, 2, 2, 2, 128, 128])
        self.w_uq = di("w_uq", [depth, 256, 2 * 384])
        self.w_ukv = di("w_ukv", [depth, 128, 512])
        self.w_out = di("w_out", [depth, D, D])
        self.w_m1 = di("w_m1", [depth, D, 4 * D])
        self.w_m2 = di("w_m2", [depth, 4 * D, D])
        self.outT = nc.dram_tensor("outT", [D, Ll], F32, kind="ExternalOutput").ap()
        kind = "ExternalOutput" if dbg else "Internal"
        ds = lambda n, s, dt=F32: nc.dram_tensor(n, s, dt, kind=kind).ap()
        self.xT = ds("xTs", [D, Lt])
        self.P_qkv = ds("P_qkv", [1536, Lt])
        self.P_z = ds("P_z", [512, Lt])
        self.P_lx = ds("P_lx", [256, Lt])
        self.P_lg = ds("P_lg", [256, Lt])
        self.P_cq = ds("P_cq", [256, Lt])
        self.P_ckv = ds("P_ckv", [128, Lt])
        self.P_kr = ds("P_kr", [64, Lt])
        self.qkvT = ds("qkvT", [1536, Lt], BF16)
        self.yT = ds("yT", [D, Lt], BF16)
        self.dbg_ba = ds("dbg_ba", [128, self.NB * 16]) if dbg else None

    def dump(self, name, ap, keys):
        if not self.dbg:
            return
        shp = list(ap.shape)
        t = self.nc.dram_tensor("dd_" + name, shp, ap.dtype, kind="ExternalOutput").ap()
        self.S.dma(t, ap, reads=keys)

    def uname(self, n):
        self._uid = getattr(self, '_uid', 0) + 1
        return "%s_u%d" % (n, self._uid)

    def pst(self, n, shape, dt):
        return self.nc.psum_tensor(self.uname(n), shape, dt)

    def tiles(self, T):
        out = []
        for (a, b, s) in ((0, self.Lc, 1), (self.Lc, self.Lt, 0)):
            t = a
            while t < b:
                n = min(T, b - t)
                out.append((t, n, s))
                t += n
        return out

    def build(self):
        nc = self.nc
        with contextlib.ExitStack() as st:
            self.S = S = Sched(nc, st, self.needed)
            sb = lambda n, s, dt=F32, stack=st: stack.enter_context(nc.sbuf_tensor(self.uname(n), list(s), dt))
            self.sb = sb
            self.cst = sb("cst", [128, 14, 128])
            self.identb = sb("identb", [128, 128], BF16)
            self.onesb = sb("onesb", [128, 128], BF16)
            self.pvt = sb("pvt", [128, NPV])
            self.modv = sb("modv", [128, 6, KC, 2])
            self.ba = sb("ba", [128, self.NB, 16])
            S.dma(self.cst[:].rearrange("p a b -> p (a b)"), self.consts[:, :], writes=['cst'])
            S.op('dve', lambda e: e.tensor_copy(self.identb[:], self.cst[:, 0, :]), ['cst'], ['identb'])
            S.op('dve', lambda e: e.tensor_copy(self.onesb[:], self.cst[:, 1, :]), ['cst'], ['onesb'])
            S.barrier()
            S.dummy_w = self.identb[:]
            for (t0, tn, s) in self.tiles(2048):
                for k in range(KC):
                    S.dma(self.xT[k * 128:(k + 1) * 128, t0:t0 + tn], self.xT_in[k * 128:(k + 1) * 128, t0:t0 + tn],
                          writes=[('xT', k, t0)])
            S.barrier()
            for l in range(self.depth):
                self.layer(l)
            for k in range(KC):
                S.dma(self.outT[k * 128:(k + 1) * 128, :], self.xT[k * 128:(k + 1) * 128, self.Lc:self.Lt])
            S.barrier()
        return nc

    def ident(self):
        return self.cst[:, 0, :]

    def ones32(self):
        return self.cst[:, 1, :]

    def pvs(self, off, n=1):
        return self.pvt[:, off:off + n]

    def rstd_from_ss(self, ss_ps, out_sb, n, inv_n, keys_r, keys_w, tmp):
        S = self.S
        S.op('act', lambda e: e.activation(tmp[:, :n], ss_ps[:, :n], AF.Sqrt, bias=EPS, scale=inv_n), keys_r, [('tmp', id(tmp))])
        S.op('dve', lambda e: e.reciprocal(out_sb[:, :n], tmp[:, :n]), [('tmp', id(tmp))], keys_w)

    def load_weight_bf(self, st, name, dram2d, K, N, piece=512, kpiece=None):
        S = self.S
        w = self.sb(name, [128, K, N], BF16, st)
        src = dram2d.rearrange("(k p) n -> p k n", p=128)
        with contextlib.ExitStack() as s2:
            if kpiece is None:
                stg = [self.sb(name + "_stg%d" % i, [128, K, piece], F32, s2) for i in range(2)]
                i = 0
                for c0 in range(0, N, piece):
                    n = min(piece, N - c0)
                    sg = stg[i % 2]
                    S.dma(sg[:, :, :n], src[:, :, c0:c0 + n], writes=[(name, 'stg', i % 2)])
                    S.op('pool', lambda e: e.tensor_copy(w[:, :, c0:c0 + n], sg[:, :, :n]), [(name, 'stg', i % 2)], [(name, 'w', i)])
                    i += 1
            else:
                stg = [self.sb(name + "_stg%d" % i, [128, kpiece, N], F32, s2) for i in range(2)]
                i = 0
                for k0 in range(0, K, kpiece):
                    sg = stg[i % 2]
                    S.dma(sg[:], src[:, k0:k0 + kpiece, :], writes=[(name, 'stg', i % 2)])
                    S.op('pool', lambda e: e.tensor_copy(w[:, k0:k0 + kpiece, :], sg[:]), [(name, 'stg', i % 2)], [(name, 'w', i)])
                    i += 1
            S.barrier()
        return w, []

    def layer(self, l):
        for nm in PHASES:
            getattr(self, 'phase_' + nm)(l)

    def phase_ada(self, l):
        nc, S = self.nc, self.S
        with contextlib.ExitStack() as st:
            sb = lambda n, s, dt=F32: self.sb(n, s, dt, st)
            S.dma(self.pvt[:], self.pv[l], writes=['pvt'])
            cv = sb("cv", [128, KC, 2])
            sc = sb("sc", [128, KC, 2])
            mod = sb("mod", [128, 48, 2])
            S.dma(cv[:], self.cvec[:, :, :], writes=['cv'])
            S.op('act', lambda e: e.activation(sc[:], cv[:], AF.Silu), ['cv'], ['sc'])
            stg = [sb("ada_stg%d" % i, [128, KC, 512]) for i in range(2)]
            ps = st.enter_context(self.pst("ada_ps", [128, 48, 2], F32))
            src = self.w_ada[l].rearrange("(k p) n -> p k n", p=128)
            for pc in range(12):
                sg = stg[pc % 2]
                S.dma(sg[:], src[:, :, pc * 512:(pc + 1) * 512], writes=[('adastg', pc % 2)])
                for jj in range(4):
                    j = pc * 4 + jj
                    for k in range(KC):
                        S.op('pe', lambda e: e.matmul(ps[:, j, :], sg[:, k, jj * 128:(jj + 1) * 128], sc[:, k, :],
                                                      start=(k == 0), stop=(k == KC - 1)),
                             [('adastg', pc % 2), 'sc'], ['adaps'], signal=(k == KC - 1))
            bb = self.pvs(0, 48).unsqueeze(2).to_broadcast([128, 48, 2])
            S.op('dve', lambda e: e.tensor_tensor(mod[:], ps[:], bb, ALU.add), ['adaps', 'pvt'], ['mod'])
            mv = self.modv
            g = lambda off: self.pvs(off, 8).unsqueeze(2).to_broadcast([128, 8, 2])
            S.op('dve', lambda e: e.scalar_tensor_tensor(mv[:, 0], mod[:, 8:16, :], 1.0, g(48), ALU.add, ALU.mult), ['mod', 'pvt'], ['modv0'])
            S.op('dve', lambda e: e.tensor_copy(mv[:, 1], mod[:, 0:8, :]), ['mod'], ['modv1'])
            S.op('dve', lambda e: e.tensor_tensor(mv[:, 2], mod[:, 16:24, :], g(56), ALU.mult), ['mod', 'pvt'], ['modv2'])
            S.op('dve', lambda e: e.scalar_tensor_tensor(mv[:, 3], mod[:, 32:40, :], 1.0, g(64), ALU.add, ALU.mult), ['mod', 'pvt'], ['modv3'])
            S.op('dve', lambda e: e.tensor_copy(mv[:, 4], mod[:, 24:32, :]), ['mod'], ['modv4'])
            S.op('dve', lambda e: e.tensor_tensor(mv[:, 5], mod[:, 40:48, :], g(72), ALU.mult), ['mod', 'pvt'], ['modv5'])
            S.barrier()

    def norm_mod(self, t0, tn, s, xt, sq, xn, h, rstd, tmp, ss_ps, ia, ib, par, bpar=None, xnk=None, hpar=None):
        S = self.S
        kx = ('xt', par)
        bpar = par if bpar is None else bpar
        hpar = par if hpar is None else hpar
        xnk = [('xn', bpar, k) for k in range(KC)] if xnk is None else xnk
        if True:
            S.dma(xt[:, :, :tn], self.xT.rearrange("(k p) t -> p k t", p=128)[:, :, t0:t0 + tn], writes=[kx])
        S.op('act', lambda e: e.activation(sq[:, :, :tn], xt[:, :, :tn], AF.Square), [kx], [('sq', bpar)])
        for k in range(KC):
            S.op('pe', lambda e: e.matmul(ss_ps[:, :tn], self.onesb[:], sq[:, k, :tn], start=(k == 0), stop=(k == KC - 1)),
                 [('sq', bpar), 'onesb'], [('ss', par)], signal=(k == KC - 1))
        self.rstd_from_ss(ss_ps, rstd, tn, 1.0 / D, [('ss', par)], [('rstd', par)], tmp)
        S.op('dve', lambda e: e.tensor_tensor(xn[:, :, :tn], xt[:, :, :tn], rstd[:, :tn].unsqueeze(1).to_broadcast([128, KC, tn]), ALU.mult),
             [kx, ('rstd', par)], xnk)
        for k in range(KC):
            S.op('pool', lambda e: e.tensor_scalar(h[:, k, :tn], xn[:, k, :tn], self.modv[:, ia, k, s:s + 1], self.modv[:, ib, k, s:s + 1],
                                                   ALU.mult, ALU.add),
                 [xnk[k], 'modv%d' % ia, 'modv%d' % ib], [('h', hpar, k)])

    def phase_proj(self, l):
        nc, S = self.nc, self.S
        with contextlib.ExitStack() as st:
            sb = lambda n, s, dt=F32: self.sb(n, s, dt, st)
            w, wkeys = self.load_weight_bf(st, "win", self.w_in[l], KC, IN_WX)
            xt = [sb("b_xt%d" % i, [128, KC, 512]) for i in range(2)]
            xn = [sb("b_xn%d" % i, [128, KC, 512]) for i in range(2)]
            sq = [sb("b_sq%d" % i, [128, KC, 512], BF16) for i in range(2)]
            h = [sb("b_h%d" % i, [128, KC, 512], BF16) for i in range(2)]
            rstd = [sb("b_rstd%d" % i, [128, 512]) for i in range(2)]
            tmp = [sb("b_tmp%d" % i, [128, 512]) for i in range(2)]
            ost = [sb("b_ost%d" % i, [128, 512]) for i in range(4)]
            ss_ps = [st.enter_context(self.pst("b_ss%d" % i, [128, 512], F32)) for i in range(2)]
            ops = [st.enter_context(self.pst("b_ops%d" % i, [128, 512], F32)) for i in range(4)]
            bps = st.enter_context(self.pst("b_bps", [128, 4, 16], F32))
            groups = []
            for c in range(12):
                groups.append((self.P_qkv, c * 128, c * 128, 128))
            for c in range(4):
                groups.append((self.P_z, c * 128, 1536 + c * 128, 128))
            for c in range(2):
                groups.append((self.P_lx, c * 128, 2064 + c * 128, 128))
            for c in range(2):
                groups.append((self.P_lg, c * 128, 2320 + c * 128, 128))
            for c in range(2):
                groups.append((self.P_cq, c * 128, 2576 + c * 128, 128))
            groups.append((self.P_ckv, 0, 2832, 128))
            groups.append((self.P_kr, 0, 2960, 64))
            ei = 0
            for ti, (t0, tn, s) in enumerate(self.tiles(512)):
                p = ti % 2
                self.norm_mod(t0, tn, s, xt[p], sq[p], xn[p], h[p], rstd[p], tmp[p], ss_ps[p], 0, 1, p)
                hk = [('h', p, k) for k in range(KC)]
                for gi, (dr, r0, c0, m) in enumerate(groups):
                    o = ops[ei % 4]
                    og = ost[ei % 4]
                    for k in range(KC):
                        S.op('pe', lambda e: e.matmul(o[:m, :tn], w[:, k, c0:c0 + m], h[p][:, k, :tn], start=(k == 0), stop=(k == KC - 1)),
                             hk + wkeys, [('ops', ei % 4)], signal=(k == KC - 1))
                    if ei % 2 == 0:
                        S.op('act', lambda e: e.copy(og[:m, :tn], o[:m, :tn]), [('ops', ei % 4)], [('ost', ei % 4)])
                    else:
                        S.op('dve', lambda e: e.tensor_copy(og[:m, :tn], o[:m, :tn]), [('ops', ei % 4)], [('ost', ei % 4)])
                    S.dma(dr[r0:r0 + m, t0:t0 + tn], og[:m, :tn], reads=[('ost', ei % 4)])
                    ei += 1
                nblk = tn // 128
                for bi in range(nblk):
                    for k in range(KC):
                        S.op('pe', lambda e: e.matmul(bps[:, bi, :], h[p][:, k, bi * 128:(bi + 1) * 128], w[:, k, 2048:2064],
                                                      start=(k == 0), stop=(k == KC - 1)),
                             hk + wkeys, ['bps'], signal=(k == KC - 1))
                b0 = t0 // 128
                S.op('dve', lambda e: e.tensor_copy(self.ba[:, b0:b0 + nblk, :], bps[:, :nblk, :]), ['bps'], [('ba', ti)])
                if self.dbg:
                    S.dma(self.dbg_ba[:, b0 * 16:(b0 + nblk) * 16], self.ba[:, b0:b0 + nblk, :].rearrange('p b c -> p (b c)'), reads=[('ba', ti)])
            S.barrier()

    def conv4(self, acc, xb, n, woff, kr, kw):
        S = self.S
        wv = lambda j: self.pvt[:, woff + j:woff + j + 1]
        S.op('dve', lambda e: e.tensor_scalar(acc[:, :n], xb[:, 0:n], wv(0), None, ALU.mult), kr + ['pvt'], kw)
        S.op('dve', lambda e: e.scalar_tensor_tensor(acc[:, :n], xb[:, 1:n + 1], wv(1), acc[:, :n], ALU.mult, ALU.add), kr + kw + ['pvt'], kw)
        S.op('dve', lambda e: e.scalar_tensor_tensor(acc[:, :n], xb[:, 2:n + 2], wv(2), acc[:, :n], ALU.mult, ALU.add), kr + kw + ['pvt'], kw)
        S.op('dve', lambda e: e.scalar_tensor_tensor(acc[:, :n], xb[:, 3:n + 3], wv(3), acc[:, :n], ALU.mult, ALU.add), kr + kw + ['pvt'], kw)

    def load_halo(self, xb, dram_rows, t0, tn, par, key):
        S = self.S
        seg0, seg1 = (0, self.Lc) if t0 < self.Lc else (self.Lc, self.Lt)
        a = max(seg0, t0 - 2)
        b = min(seg1, t0 + tn + 1)
        k = (key, par)
        if a > t0 - 2:
            S.op('pool', lambda e: e.memset(xb[:, 0:2], 0.0), [], [k])
        if b < t0 + tn + 1:
            S.op('pool', lambda e: e.memset(xb[:, tn + 2:tn + 3], 0.0), [k], [k])
        S.dma(xb[:, a - (t0 - 2):b - (t0 - 2)], dram_rows[:, a:b], reads=[k], writes=[k])
        return k

    def phase_gdn_prep(self, l):
        nc, S = self.nc, self.S
        with contextlib.ExitStack() as st:
            sb = lambda n, s, dt=F32: self.sb(n, s, dt, st)
            xb = [sb("c_xb%d" % i, [128, 515]) for i in range(2)]
            acc = [sb("c_acc%d" % i, [128, 512]) for i in range(2)]
            sl = [sb("c_sl%d" % i, [128, 512]) for i in range(2)]
            sq = [sb("c_sq%d" % i, [128, 512], BF16) for i in range(2)]
            rs = [sb("c_rs%d" % i, [128, 512]) for i in range(2)]
            tmp = [sb("c_tmp%d" % i, [128, 512]) for i in range(2)]
            ob = [sb("c_ob%d" % i, [128, 512], BF16) for i in range(2)]
            ss = [st.enter_context(self.pst("c_ss%d" % i, [128, 512], F32)) for i in range(2)]
            it = 0
            for c in range(12):
                part = c // 4
                rows = self.P_qkv[c * 128:(c + 1) * 128, :]
                for (t0, tn, s) in self.tiles(512):
                    p = it % 2
                    it += 1
                    kx = self.load_halo(xb[p], rows, t0, tn, p, 'cxb')
                    self.conv4(acc[p], xb[p], tn, 80 + c * 4, [kx], [('cacc', p)])
                    S.op('act', lambda e: e.activation(sl[p][:, :tn], acc[p][:, :tn], AF.Silu), [('cacc', p)], [('csl', p)])
                    if part < 2:
                        S.op('act', lambda e: e.activation(sq[p][:, :tn], sl[p][:, :tn], AF.Square), [('csl', p)], [('csq', p)])
                        S.op('pe', lambda e: e.matmul(ss[p][:, :tn], self.onesb[:], sq[p][:, :tn], start=True, stop=True),
                             [('csq', p), 'onesb'], [('css', p)])
                        self.rstd_from_ss(ss[p], rs[p], tn, 1.0, [('css', p)], [('crs', p)], tmp[p])
                        scale = (128.0 ** -0.5) if part == 0 else 1.0
                        S.op('dve', lambda e: e.scalar_tensor_tensor(ob[p][:, :tn], sl[p][:, :tn], scale, rs[p][:, :tn], ALU.mult, ALU.mult),
                             [('csl', p), ('crs', p)], [('cob', p)])
                    else:
                        S.op('dve', lambda e: e.tensor_copy(ob[p][:, :tn], sl[p][:, :tn]), [('csl', p)], [('cob', p)])
                    S.dma(self.qkvT[c * 128:(c + 1) * 128, t0:t0 + tn], ob[p][:, :tn], reads=[('cob', p)])
            S.barrier()

    def phase_lru(self, l):
        nc, S = self.nc, self.S
        Lt, Lc = self.Lt, self.Lc
        with contextlib.ExitStack() as st:
            sb = lambda n, s, dt=F32: self.sb(n, s, dt, st)
            wst = sb("l_wst", [128, 8, 128])
            wbf = sb("l_wbf", [128, 8, 128], BF16)
            S.dma(wst[:], self.lru_w[l].rearrange("a d c p n -> p (a d c) n"), writes=['lwst'])
            S.op('dve', lambda e: e.tensor_copy(wbf[:], wst[:]), ['lwst'], ['lwbf'])
            cs = sb("l_cs", [128, 4])
            S.op('act', lambda e: e.activation(cs[:], self.pvs(156, 4), AF.Exp, scale=-1.0), ['pvt'], ['lcs'])
            S.op('act', lambda e: e.activation(cs[:], cs[:], AF.Ln, bias=1.0), ['lcs'], ['lcs'])
            S.op('dve', lambda e: e.tensor_scalar(cs[:], cs[:], -8.0, None, ALU.mult), ['lcs'], ['lcs'])
            xb = sb("l_xb", [128, Lt + 6])
            xc = sb("l_xc", [128, Lt])
            xcb = sb("l_xcb", [128, Lt], BF16)
            rr = sb("l_r", [128, Lt])
            ii = sb("l_i", [128, Lt])
            aa = sb("l_a", [128, Lt])
            uu = sb("l_u", [128, Lt])
            hf = sb("l_hf", [128, Lt])
            hb = sb("l_hb", [128, Lt])
            gt = sb("l_gt", [128, Lt])
            yo = sb("l_yo", [128, Lt], BF16)
            gps = [st.enter_context(self.pst("l_gps%d" % i, [128, 512], F32)) for i in range(4)]
            for c in range(2):
                S.op('pool', lambda e: e.memset(xb[:], 0.0), [], ['lxb'])
                rows = self.P_lx[c * 128:(c + 1) * 128, :]
                S.dma(xb[:, 2:2 + Lc], rows[:, 0:Lc], reads=['lxb'], writes=['lxb'])
                S.dma(xb[:, Lc + 5:Lc + 5 + self.Ll], rows[:, Lc:Lt], reads=['lxb'], writes=['lxb'])
                S.dma(gt[:], self.P_lg[c * 128:(c + 1) * 128, :], writes=['lgt'])
                wv = lambda j: self.pvt[:, 140 + c * 4 + j:140 + c * 4 + j + 1]
                for (o0, t0, n) in ((0, 0, Lc), (Lc + 3, Lc, self.Ll)):
                    kk = [('lxc', t0)]
                    S.op('dve', lambda e: e.tensor_scalar(xc[:, t0:t0 + n], xb[:, o0:o0 + n], wv(0), self.pvt[:, 148 + c:149 + c], ALU.mult, ALU.add),
                         ['lxb', 'pvt'], kk)
                    for j in (1, 2, 3):
                        S.op('dve',
                             lambda e: e.scalar_tensor_tensor(xc[:, t0:t0 + n], xb[:, o0 + j:o0 + j + n], wv(j), xc[:, t0:t0 + n], ALU.mult, ALU.add),
                             ['lxb', 'pvt'] + kk, kk)
                kxc = [('lxc', 0), ('lxc', Lc)]
                S.op('act', lambda e: e.copy(xcb[:], xc[:]), kxc, ['lxcb'])
                S.op('act', lambda e: e.activation(gt[:], gt[:], AF.Gelu_apprx_tanh), ['lgt'], ['lgt'])
                tl = self.tiles(512)
                kr = [('lr', t0) for (t0, _, _) in tl]
                ki = [('li', t0) for (t0, _, _) in tl]
                gi = 0
                for d in range(2):
                    for (t0, tn, s_) in tl:
                        for ai, dst in ((0, rr), (1, ii)):
                            g = gps[gi % 4]
                            kg_ = ('lgps', gi % 4)
                            gi += 1
                            widx = (ai * 2 + d) * 2 + c
                            S.op('pe', lambda e: e.matmul(g[:, :tn], wbf[:, widx, :], xcb[:, t0:t0 + tn], start=True, stop=True),
                                 ['lwbf', 'lxcb'], [kg_])
                            bo = (160 if ai == 0 else 164) + d * 2 + c
                            S.op('act', lambda e: e.activation(dst[:, t0:t0 + tn], g[:, :tn], AF.Sigmoid, bias=self.pvt[:, bo:bo + 1]),
                                 [kg_, 'pvt'], [('lr' if ai == 0 else 'li', t0)])
                    ci = d * 2 + c
                    S.op('act', lambda e: e.activation(aa[:], rr[:], AF.Exp, scale=cs[:, ci:ci + 1]), kr + ['lcs'], ['la'])
                    S.op('dve', lambda e: e.tensor_tensor(rr[:], aa[:], aa[:], ALU.mult), ['la'] + kr, kr)
                    S.op('act', lambda e: e.activation(rr[:], rr[:], AF.Sqrt, bias=1.0, scale=-1.0), kr, kr)
                    S.op('pool', lambda e: e.tensor_tensor(uu[:], ii[:], xc[:], ALU.mult), ki + kxc, ['lu'])
                    S.op('dve', lambda e: e.tensor_tensor(uu[:], uu[:], rr[:], ALU.mult), ['lu'] + kr, ['lu'])
                    if d == 0:
                        S.op('dve', lambda e: e.tensor_tensor_scan(hf[:], aa[:], uu[:], 0.0, ALU.mult, ALU.add), ['la', 'lu'], ['lhf'])
                    else:
                        S.op('dve', lambda e: e.tensor_tensor_scan(hb[:, 0:Lc][:, ::-1], aa[:, 0:Lc][:, ::-1], uu[:, 0:Lc][:, ::-1],
                                                                   0.0, ALU.mult, ALU.add),
                             ['la', 'lu'], ['lhb'])
                        S.op('dve', lambda e: e.tensor_tensor_scan(hb[:, Lc:Lt][:, ::-1], aa[:, Lc:Lt][:, ::-1], uu[:, Lc:Lt][:, ::-1],
                                                                   hb[:, 0:1], ALU.mult, ALU.add),
                             ['la', 'lu', 'lhb'], ['lhb'])
                S.op('dve', lambda e: e.tensor_tensor(hf[:], hf[:], hb[:], ALU.add), ['lhf', 'lhb'], ['lhf'])
                S.op('dve', lambda e: e.tensor_tensor(yo[:], hf[:], gt[:], ALU.mult), ['lhf', 'lgt'], ['lyo'])
                S.dma(self.yT[512 + c * 128:512 + (c + 1) * 128, :], yo[:], reads=['lyo'])
            S.barrier()

    def phase_mla(self, l):
        nc, S = self.nc, self.S
        Lt, Lc, NB, NBc = self.Lt, self.Lc, self.NB, self.NBc
        with contextlib.ExitStack() as st:
            sb = lambda n, s, dt=F32: self.sb(n, s, dt, st)
            qT = sb("m_qT", [96, 4, Lt], BF16)
            kT = sb("m_kT", [96, 4, Lt], BF16)
            V = sb("m_V", [128, NB, 4, 65], BF16)
            tab = sb("m_tab", [96, 2, Lt])
            S.dma(tab[64:96, 0, :], self.rope[0], writes=['tab'])
            S.dma(tab[64:96, 1, :], self.rope[1], reads=['tab'], writes=['tab'])
            S.op('pool', lambda e: e.memset(V[:, :, :, 64:65], 1.0), [], ['Vones'])
            with contextlib.ExitStack() as st2:
                sb2 = lambda n, s, dt=F32: self.sb(n, s, dt, st2)
                wq, wqk = self.load_weight_bf(st2, "wuq", self.w_uq[l], 2, 768)
                wkv, wkvk = self.load_weight_bf(st2, "wukv", self.w_ukv[l], 1, 512)
                cq = [sb2("m_cq%d" % i, [128, 2, 512]) for i in range(2)]
                sq = [sb2("m_sq%d" % i, [128, 2, 512], BF16) for i in range(2)]
                cqn = [sb2("m_cqn%d" % i, [128, 2, 512], BF16) for i in range(2)]
                ckv = [sb2("m_ckv%d" % i, [128, 512]) for i in range(2)]
                sk = [sb2("m_sk%d" % i, [128, 512], BF16) for i in range(2)]
                ckn = [sb2("m_ckn%d" % i, [128, 512], BF16) for i in range(2)]
                rs = [sb2("m_rs%d" % i, [128, 512]) for i in range(2)]
                rk = [sb2("m_rk%d" % i, [128, 512]) for i in range(2)]
                tmp = [sb2("m_tmp%d" % i, [128, 512]) for i in range(2)]
                kr = [sb2("m_kr%d" % i, [96, 2, 512]) for i in range(2)]
                r1 = [sb2("m_r1%d" % i, [96, 512]) for i in range(2)]
                r2 = [sb2("m_r2%d" % i, [96, 512]) for i in range(2)]
                ssq = [st2.enter_context(self.pst("m_ssq%d" % i, [128, 512], F32)) for i in range(2)]
                qps = [st2.enter_context(self.pst("m_qps%d" % i, [128, 512], F32)) for i in range(4)]
                vps = st2.enter_context(self.pst("m_vps", [128, 4, 64], F32))
                qi = 0
                for ti, (t0, tn, s) in enumerate(self.tiles(512)):
                    p = ti % 2
                    S.dma(cq[p][:, :, :tn], self.P_cq.rearrange("(k p) t -> p k t", p=128)[:, :, t0:t0 + tn], writes=[('mcq', p)])
                    S.op('act', lambda e: e.activation(sq[p][:, :, :tn], cq[p][:, :, :tn], AF.Square), [('mcq', p)], [('msq', p)])
                    for k in range(2):
                        S.op('pe', lambda e: e.matmul(ssq[p][:, :tn], self.onesb[:], sq[p][:, k, :tn], start=(k == 0), stop=(k == 1)),
                             [('msq', p), 'onesb'], [('mssq', p)], signal=(k == 1))
                    self.rstd_from_ss(ssq[p], rs[p], tn, 1.0 / 256, [('mssq', p)], [('mrs', p)], tmp[p])
                    for k in range(2):
                        S.op('dve', lambda e: e.scalar_tensor_tensor(cqn[p][:, k, :tn], cq[p][:, k, :tn], self.pvt[:, 168 + k:169 + k], rs[p][:, :tn],
                                                                     ALU.mult, ALU.mult),
                             [('mcq', p), ('mrs', p), 'pvt'], [('mcqn', p, k)])
                    kq = [('mcqn', p, 0), ('mcqn', p, 1)]
                    for hh in range(4):
                        qa = qps[qi % 4]
                        qb = qps[(qi + 1) % 4]
                        ka, kb = ('mqps', qi % 4), ('mqps', (qi + 1) % 4)
                        qi += 2
                        for k in range(2):
                            S.op('pe', lambda e: e.matmul(qa[:96, :tn], wq[:, k, hh * 96:(hh + 1) * 96], cqn[p][:, k, :tn], start=(k == 0), stop=(k == 1)),
                                 kq + wqk, [ka], signal=(k == 1))
                        for k in range(2):
                            S.op('pe', lambda e: e.matmul(qb[:96, :tn], wq[:, k, 384 + hh * 96:384 + (hh + 1) * 96], cqn[p][:, k, :tn],
                                                          start=(k == 0), stop=(k == 1)),
                                 kq + wqk, [kb], signal=(k == 1))
                        S.op('act', lambda e: e.copy(qT[0:64, hh, t0:t0 + tn], qa[0:64, :tn]), [ka], [('qTn', hh, t0)])
                        S.op('dve', lambda e: e.tensor_tensor(r1[p][64:96, :tn], qa[64:96, :tn], tab[64:96, 0, t0:t0 + tn], ALU.mult),
                             [ka, 'tab'], [('mr1', p)])
                        S.op('dve', lambda e: e.tensor_tensor(r2[p][64:96, :tn], qb[64:96, :tn], tab[64:96, 1, t0:t0 + tn], ALU.mult),
                             [kb, 'tab'], [('mr2', p)])
                        S.op('pool', lambda e: e.tensor_tensor(qT[64:96, hh, t0:t0 + tn], r1[p][64:96, :tn], r2[p][64:96, :tn], ALU.add),
                             [('mr1', p), ('mr2', p)], [('qTr', hh, t0)])
                    S.dma(ckv[p][:, :tn], self.P_ckv[:, t0:t0 + tn], writes=[('mckv', p)])
                    S.op('act', lambda e: e.activation(sk[p][:, :tn], ckv[p][:, :tn], AF.Square), [('mckv', p)], [('msk', p)])
                    S.op('pe', lambda e: e.matmul(ssq[p][:, :tn], self.onesb[:], sk[p][:, :tn], start=True, stop=True),
                         [('msk', p), 'onesb'], [('mssq', p)])
                    self.rstd_from_ss(ssq[p], rk[p], tn, 1.0 / 128, [('mssq', p)], [('mrk', p)], tmp[p])
                    S.op('dve', lambda e: e.scalar_tensor_tensor(ckn[p][:, :tn], ckv[p][:, :tn], self.pvt[:, 170:171], rk[p][:, :tn], ALU.mult, ALU.mult),
                         [('mckv', p), ('mrk', p), 'pvt'], [('mckn', p)])
                    for hh in range(4):
                        qa = qps[qi % 4]
                        ka = ('mqps', qi % 4)
                        qi += 1
                        S.op('pe', lambda e: e.matmul(qa[:64, :tn], wkv[:, 0, hh * 64:(hh + 1) * 64], ckn[p][:, :tn], start=True, stop=True),
                             [('mckn', p)] + wkvk, [ka])
                        S.op('act', lambda e: e.copy(kT[0:64, hh, t0:t0 + tn], qa[0:64, :tn]), [ka], [('kTn', hh, t0)])
                    for bi in range(tn // 128):
                        S.op('pe', lambda e: e.matmul(vps[:].rearrange("p h d -> p (h d)"), ckn[p][:, bi * 128:(bi + 1) * 128], wkv[:, 0, 256:512],
                                                      start=True, stop=True),
                             [('mckn', p)] + wkvk, ['mvps'])
                        blk = t0 // 128 + bi
                        S.op('dve', lambda e: e.tensor_copy(V[:, blk, :, 0:64], vps[:]), ['mvps'], [('V', blk)])
                    S.dma(kr[p][64:96, 0, :tn], self.P_kr[0:32, t0:t0 + tn], writes=[('mkr', p)])
                    S.dma(kr[p][64:96, 1, :tn], self.P_kr[32:64, t0:t0 + tn], reads=[('mkr', p)], writes=[('mkr', p)])
                    S.op('dve', lambda e: e.tensor_tensor(kr[p][64:96, :, :tn], kr[p][64:96, :, :tn], tab[64:96, :, t0:t0 + tn], ALU.mult),
                         [('mkr', p), 'tab'], [('mkr', p)])
                    S.op('dve', lambda e: e.tensor_tensor(r1[p][64:96, :tn], kr[p][64:96, 0, :tn], kr[p][64:96, 1, :tn], ALU.add),
                         [('mkr', p)], [('mr1', p)])
                    for hh in range(4):
                        S.op('pool', lambda e: e.tensor_copy(kT[64:96, hh, t0:t0 + tn], r1[p][64:96, :tn]), [('mr1', p)], [('kTr', hh, t0)])
                S.barrier()
            pT = [sb("m_pT%d" % i, [128, 512], BF16) for i in range(3)]
            atm = [sb("m_atm%d" % i, [128, 4, 256]) for i in range(2)]
            rec = sb("m_rec", [128, 8])
            yob = [sb("m_yob%d" % i, [128, 2, 512], BF16) for i in range(2)]
            sps = [st.enter_context(self.pst("m_sps%d" % i, [128, 512], F32)) for i in range(2)]
            acc = [st.enter_context(self.pst("m_acc%d" % i, [128, 512], F32)) for i in range(4)]
            tps = [st.enter_context(self.pst("m_tps%d" % i, [128, 512], F32)) for i in range(2)]
            scl = 96.0 ** -0.5
            si = 0
            ri = 0
            for ti, (t0, tn, s) in enumerate(self.tiles(512)):
                p = ti % 2
                nq = tn // 128
                kblocks = list(range(NBc)) if s == 1 else list(range(NB))
                items = [(hh, kb) for hh in range(4) for kb in kblocks]

                def emit_s(j):
                    hh, kb = items[j]
                    sp_ = sps[j % 2]
                    S.op('pe', lambda e: e.matmul(sp_[:, :tn], kT[0:96, hh, kb * 128:(kb + 1) * 128], qT[0:96, hh, t0:t0 + tn], start=True, stop=True),
                         [], [('sps', j % 2)])
                emit_s(0)
                for j, (hh, kb) in enumerate(items):
                    sp_ = sps[j % 2]
                    pt_ = pT[j % 3]
                    ks, kp = ('sps', j % 2), ('pT', j % 3)
                    S.op('act', lambda e: e.activation(pt_[:, :tn], sp_[:, :tn], AF.Exp, scale=scl), [ks], [kp])
                    if j + 1 < len(items):
                        emit_s(j + 1)
                    for qb in range(nq):
                        S.op('pe', lambda e: e.matmul(acc[qb][:, 0:65], pt_[:, qb * 128:(qb + 1) * 128], V[:, kb, hh, :],
                                                      start=(kb == kblocks[0]), stop=(kb == kblocks[-1])),
                             [kp], [('acc', qb)])
                    if kb == kblocks[-1]:
                        for qb in range(nq):
                            rc = rec[:, ri % 8:ri % 8 + 1]
                            kr_ = ('rec', ri % 8)
                            ri += 1
                            S.op('dve', lambda e: e.reciprocal(rc, acc[qb][:, 64:65]), [('acc', qb)], [kr_])
                            S.op('dve', lambda e: e.tensor_scalar(atm[p][:, qb, hh * 64:(hh + 1) * 64], acc[qb][:, 0:64], rc, None, ALU.mult),
                                 [('acc', qb), kr_], [('atm', p, qb, hh)])
                for qb in range(nq):
                    for c in range(2):
                        tp = tps[(qb * 2 + c) % 2]
                        kt = ('tps', (qb * 2 + c) % 2)
                        S.op('pe', lambda e: e.transpose(tp[:, 0:128], atm[p][:, qb, c * 128:(c + 1) * 128], self.ident()),
                             [('atm', p, qb, hh) for hh in range(4)] + ['cst'], [kt])
                        S.op('act', lambda e: e.copy(yob[p][:, c, qb * 128:(qb + 1) * 128], tp[:, 0:128]), [kt], [('yob', p, c, qb)])
                for c in range(2):
                    S.dma(self.yT[768 + c * 128:768 + (c + 1) * 128, t0:t0 + tn], yob[p][:, c, :tn], reads=[('yob', p, c, qb) for qb in range(nq)])
            S.barrier()

    def phase_gdn(self, l):
        nc, S = self.nc, self.S
        Lt, Lc, NB, NBc = self.Lt, self.Lc, self.NB, self.NBc
        with contextlib.ExitStack() as st:
            sb = lambda n, s, dt=F32: self.sb(n, s, dt, st)
            oacc = sb("g_oacc", [128, 4, Lt])
            beta = sb("g_beta", [128, NB, 8])
            nbeta = sb("g_nbeta", [128, NB, 8])
            gg = sb("g_gg", [128, NB, 8])
            t1 = sb("g_t1", [128, NB, 8])
            t2 = sb("g_t2", [128, NB, 8])
            negA = sb("g_negA", [128, 8])
            kba = [('ba', ti) for ti in range(len(self.tiles(512)))]
            if GSTOP < -1:
                S.barrier()
                return
            S.op('act', lambda e: e.activation(beta[:], self.ba[:, :, 0:8], AF.Sigmoid), kba, ['gbeta'])
            S.op('dve', lambda e: e.tensor_scalar(nbeta[:], beta[:], -1.0, None, ALU.mult), ['gbeta'], ['gnbeta'])
            S.op('dve', lambda e: e.tensor_tensor(t1[:], self.ba[:, :, 8:16], self.pvs(128, 8).unsqueeze(1).to_broadcast([128, NB, 8]), ALU.add),
                 kba + ['pvt'], ['gt1'])
            S.op('act', lambda e: e.activation(t2[:], t1[:], AF.Abs), ['gt1'], ['gt2'])
            S.op('act', lambda e: e.activation(t2[:], t2[:], AF.Exp, scale=-1.0), ['gt2'], ['gt2'])
            S.op('act', lambda e: e.activation(t2[:], t2[:], AF.Ln, bias=1.0), ['gt2'], ['gt2'])
            S.op('dve', lambda e: e.scalar_tensor_tensor(t1[:], t1[:], 0.0, t2[:], ALU.max, ALU.add), ['gt1', 'gt2'], ['gt1'])
            S.op('act', lambda e: e.activation(negA[:], self.pvs(172, 8), AF.Exp), ['pvt'], ['gnegA'])
            S.op('dve', lambda e: e.tensor_scalar(negA[:], negA[:], -1.0, None, ALU.mult), ['gnegA'], ['gnegA'])
            S.op('dve', lambda e: e.tensor_tensor(gg[:], t1[:], negA[:].unsqueeze(1).to_broadcast([128, NB, 8]), ALU.mult), ['gt1', 'gnegA'], ['ggg'])

            if GSTOP < 0:
                S.barrier()
                return
            def T(n, dt=F32):
                return [sb("g_%s%d" % (n, i), [128, 4, 128], dt) for i in range(2)]
            qkv = [sb("g_qkv%d" % i, [128, 12, 128], BF16) for i in range(2)]
            GM, EGb, Dm, E, NK, NBm, ub = T("GM"), T("EGb"), T("Dm"), T("E"), T("NK"), T("NBm"), T("ub")
            attnT, Tb = T("attnT", BF16), T("Tb", BF16)
            T1 = lambda n, dt=F32: sb("g1_" + n, [128, 4, 128], dt)
            Nn1, NT1, Xa1, XTa1, Xb1, XTb1, Qa1, Qb1 = [T1(n) for n in ("Nn", "NT", "Xa", "XTa", "Xb", "XTb", "Qa", "Qb")]
            Ba1, BTa1, Bb1, BTb1, Cc1, C2c1, Nl1, NlT1 = [T1(n, BF16) for n in ("Ba", "BTa", "Bb", "BTb", "Cc", "C2c", "Nl", "NlT")]
            kE, kg, vtm, wT, qgT, vnew = T("kE", BF16), T("kg", BF16), T("vtm", BF16), T("wT", BF16), T("qgT", BF16), T("vnew", BF16)
            gsm = [sb("g_gsm%d" % i, [128, 16]) for i in range(2)]
            gla = [sb("g_gla%d" % i, [128, 4]) for i in range(2)]
            S32 = sb("g_S32", [128, 4, 128])
            Sbf = sb("g_Sbf", [128, 4, 128], BF16)
            pf = [st.enter_context(self.pst("g_pf%d" % i, [128, 4, 128], F32)) for i in range(6)]
            pb = [st.enter_context(self.pst("g_pb%d" % i, [128, 4, 128], BF16)) for i in range(2)]
            cnt = dict(f=0, b=0)

            def PF():
                i = cnt['f'] % 6
                cnt['f'] += 1
                return pf[i], ('pf', i)

            def PB():
                i = cnt['b'] % 2
                cnt['b'] += 1
                return pb[i], ('pb', i)

            bc_h = lambda ap2: ap2.unsqueeze(1).to_broadcast([128, 4, 128])
            onesH = sb("g_onesH", [128, 4, 128])
            MdH = [sb("g_MdH%d" % i, [128, 4, 128]) for i in range(2)]
            strictH = [sb("g_strictH%d" % i, [128, 4, 128]) for i in range(2)]
            S.op('dve', lambda e: e.memset(onesH[:], 1.0), [], ['onesH'])
            maskH = {}
            for mi_ in (0, 10, 11, 12, 13):
                maskH[mi_] = sb("g_maskH%d" % mi_, [128, 4, 128])
                S.op('dve', lambda e: e.tensor_tensor(maskH[mi_][:], onesH[:], bc_h(self.cst[:, mi_, :]), ALU.mult), ['onesH', 'cst'], ['cst'])
            for dd in range(2):
                S.op('dve', lambda e: e.tensor_tensor(MdH[dd][:], onesH[:], bc_h(self.cst[:, 2 + dd, :]), ALU.mult), ['onesH', 'cst'], [('MdH', dd)])
                S.op('dve', lambda e: e.tensor_tensor(strictH[dd][:], onesH[:], bc_h(self.cst[:, 6 + dd, :]), ALU.mult), ['onesH', 'cst'], [('strictH', dd)])
            bc_i = lambda ap2: ap2.unsqueeze(2).to_broadcast([128, 4, 128])
            qsrc = self.qkvT.rearrange("(c p) t -> p c t", p=128)
            it = 0
            for d in range(2):
                Md, negm, strict = self.cst[:, 2 + d, :], self.cst[:, 4 + d, :], self.cst[:, 6 + d, :]
                if GSKIP != 2:
                    S.op('pool', lambda e: e.memset(S32[:], 0.0), [('S32', h) for h in range(4)], [('S32', h) for h in range(4)])
                    S.op('pool', lambda e: e.memset(Sbf[:], 0.0), [('Sbf', h) for h in range(4)], [('Sbf', h) for h in range(4)])
                if d == 0:
                    order = list(range(NB))
                else:
                    order = list(range(NBc - 1, -1, -1)) + list(range(NB - 1, NBc - 1, -1))
                for b in order:
                    p = it % 2
                    it += 1
                    K = lambda *n: n + (p,)
                    tok = slice(b * 128, (b + 1) * 128)
                    if GSKIP != 1:
                        S.dma(qkv[p][:], qsrc[:, :, tok], writes=[K('qkv')])
                    qTb = lambda h: qkv[p][:, h, :]
                    kTb = lambda h: qkv[p][:, 4 + h, :]
                    vTb = lambda h: qkv[p][:, 8 + h, :]
                    gcol = gg[:, b, d * 4:(d + 1) * 4]
                    if GSTOP < 1:
                        continue
                    S.op('dve', lambda e: e.tensor_tensor(GM[p][:], MdH[d][:], bc_i(gcol), ALU.mult), [('MdH', d), 'ggg'], [K('GM')])
                    if GSUB < 1:
                        continue
                    gp, kgp = PF()
                    gpv = gp[:].rearrange("p h c -> p (h c)")
                    S.op('pe', lambda e: e.matmul(gpv[:, 0:4], Md, gcol, start=True, stop=True), ['cst', 'ggg'], [kgp], signal=False)
                    S.op('pe', lambda e: e.matmul(gpv[:, 4:8], self.ones32(), gcol, start=True, stop=True), ['cst', 'ggg'], [kgp])
                    if GSUB < 2:
                        continue
                    S.op('dve', lambda e: e.tensor_copy(gsm[p][:, 0:8], gpv[:, 0:8]), [kgp], [K('gsm')])
                    S.op('act', lambda e: e.activation(gsm[p][:, 8:12], gsm[p][:, 0:4], AF.Exp), [K('gsm')], [K('gsm2')])
                    S.op('dve', lambda e: e.tensor_tensor(gsm[p][:, 12:16], gsm[p][:, 4:8], gsm[p][:, 0:4], ALU.subtract), [K('gsm')], [K('gsm3')])
                    S.op('act', lambda e: e.activation(gsm[p][:, 12:16], gsm[p][:, 12:16], AF.Exp), [K('gsm3')], [K('gsm3')])
                    S.op('act', lambda e: e.activation(gla[p][:], gsm[p][:, 4:8], AF.Exp), [K('gsm')], [K('gla')])
                    if GSUB < 3:
                        continue
                    gb, kgb = PF()
                    for h in range(4):
                        if GSKIP == 4:
                            break
                        S.op('pe', lambda e: e.matmul(gb[:, h, :], self.ones32(), GM[p][:, h, :], start=True, stop=True),
                             ['cst', K('GM')], [kgb], signal=(h == 3))
                    if GSKIP == 5000:
                        S.op('act', lambda e: e.activation(EGb[p][:].rearrange("p h c -> p (h c)"), gb[:].rearrange("p h c -> p (h c)"), AF.Exp), [kgb], [K('EGb')])
                    elif GSKIP != 7:
                        S.op('dve', lambda e: e.tensor_copy(EGb[p][:], gb[:]), [kgb], [K('EGb')])
                        S.op('act', lambda e: e.activation(EGb[p][:], EGb[p][:], AF.Exp), [K('EGb')], [K('EGb')])
                    elif GSKIP != 3:
                        S.op('act', lambda e: e.activation(EGb[p][:], gb[:], AF.Exp), [kgb], [K('EGb')])
                    if GSUB < 4:
                        continue
                    S.op('dve', lambda e: e.tensor_tensor(Dm[p][:], gb[:], bc_i(gsm[p][:, 0:4]), ALU.subtract), [kgb, K('gsm')], [K('Dm')])
                    S.op('dve', lambda e: e.scalar_tensor_tensor(Dm[p][:], Dm[p][:], 0.0, bc_h(negm), ALU.min, ALU.add), [K('Dm'), 'cst'], [K('Dm')])
                    S.op('act', lambda e: e.activation(E[p][:], Dm[p][:], AF.Exp), [K('Dm')], [K('E')])
                    if GSUB < 5:
                        continue
                    S.op('dve', lambda e: e.tensor_tensor(NBm[p][:], strictH[d][:], bc_i(nbeta[:, b, d * 4:(d + 1) * 4]), ALU.mult),
                         [('strictH', d), 'gnbeta'], [K('NBm')])
                    S.op('dve', lambda e: e.tensor_tensor(qgT[p][:], qkv[p][:, 0:4, :], EGb[p][:], ALU.mult), [K('qkv'), K('EGb')], [K('qgT')])
                    if GSTOP < 2:
                        continue
                    tk, ktk = PB()
                    for h in range(4):
                        S.op('pe', lambda e: e.transpose(tk[:, h, :], kTb(h), self.identb[:]), [K('qkv'), 'identb'], [ktk], signal=(h == 3))
                    S.op('dve', lambda e: e.tensor_tensor(kE[p][:], tk[:], bc_i(gsm[p][:, 8:12]), ALU.mult), [ktk, K('gsm2')], [K('kE')])
                    S.op('dve', lambda e: e.tensor_tensor(kg[p][:], tk[:], bc_i(gsm[p][:, 12:16]), ALU.mult), [ktk, K('gsm3')], [K('kg')])
                    tv, ktv = PB()
                    for h in range(4):
                        S.op('pe', lambda e: e.transpose(tv[:, h, :], vTb(h), self.identb[:]), [K('qkv'), 'identb'], [ktv], signal=(h == 3))
                    S.op('act', lambda e: e.copy(vtm[p][:], tv[:]), [ktv], [K('vtm')])
                    if GSTOP < 3:
                        continue
                    kk, kkk = PF()
                    for h in range(4):
                        S.op('pe', lambda e: e.matmul(kk[:, h, :], kTb(h), kTb(h), start=True, stop=True), [K('qkv')], [kkk], signal=(h == 3))
                    qk, kqk = PF()
                    for h in range(4):
                        S.op('pe', lambda e: e.matmul(qk[:, h, :], kTb(h), qTb(h), start=True, stop=True), [K('qkv')], [kqk], signal=(h == 3))
                    S.op('dve', lambda e: e.tensor_tensor(attnT[p][:], qk[:], E[p][:], ALU.mult), [kqk, K('E')], [K('attnT')])
                    S.op('dve', lambda e: e.tensor_tensor(NK[p][:], kk[:], E[p][:], ALU.mult), [kkk, K('E')], [K('NK')])
                    kN, kNT = ('i', 'Nn'), ('i', 'NT')
                    mk = lambda i: maskH[i][:]
                    id32 = self.ident()
                    S.op('pool', lambda e: e.tensor_tensor(Nn1[:], NK[p][:], NBm[p][:], ALU.mult), [K('NK'), K('NBm')], [kN])
                    tnf, ktnf = PF()
                    for h in range(4):
                        S.op('pe', lambda e: e.transpose(tnf[:, h, :], Nn1[:, h, :], id32), [kN, 'cst'], [ktnf])
                    S.op('dve', lambda e: e.tensor_copy(NT1[:], tnf[:]), [ktnf], [kNT])
                    if GSTOP < 4:
                        continue
                    S.op('pool', lambda e: e.tensor_tensor(Xa1[:], Nn1[:], mk(10), ALU.mult), [kN, 'cst'], [('i', 'Xa')])
                    S.op('pool', lambda e: e.tensor_tensor(XTa1[:], NT1[:], mk(10), ALU.mult), [kNT, 'cst'], [('i', 'XTa')])
                    S.op('pool', lambda e: e.tensor_copy(Qa1[:], Xa1[:]), [('i', 'Xa')], [('i', 'Qa')])
                    X, XT, kX, kXT = Xa1, XTa1, ('i', 'Xa'), ('i', 'XTa')
                    Qc, kQ = Qa1, ('i', 'Qa')
                    xb_ = [(Xb1, XTb1, ('i', 'Xb'), ('i', 'XTb')), (Xa1, XTa1, ('i', 'Xa'), ('i', 'XTa'))]
                    qb_ = [(Qb1, ('i', 'Qb')), (Qa1, ('i', 'Qa'))]
                    for lv in range(1, 4):
                        X2, X2T, kX2, kX2T = xb_[(lv - 1) % 2]
                        Qn, kQn = qb_[(lv - 1) % 2]
                        a, ka = PF()
                        for h in range(4):
                            S.op('pe', lambda e: e.matmul(a[:, h, :], X[:, h, :], XT[:, h, :], start=True, stop=True), [kX, kXT], [ka])
                        a2, ka2 = (None, None)
                        if lv < 3:
                            a2, ka2 = PF()
                            for h in range(4):
                                S.op('pe', lambda e: e.matmul(a2[:, h, :], XT[:, h, :], X[:, h, :], start=True, stop=True), [kX, kXT], [ka2])
                        S.op('act', lambda e: e.copy(X2T[:], a[:]), [ka], [kX2T])
                        if lv < 3:
                            S.op('dve', lambda e: e.tensor_copy(X2[:], a2[:]), [ka2], [kX2])
                        a3, ka3 = PF()
                        for h in range(4):
                            S.op('pe', lambda e: e.matmul(a3[:, h, :], X2T[:, h, :], Qc[:, h, :], start=True, stop=False), [kX2T, kQ], [ka3])
                            S.op('pe', lambda e: e.matmul(a3[:, h, :], X2T[:, h, :], id32, start=False, stop=True), [kX2T, 'cst'], [ka3])
                        S.op('dve', lambda e: e.tensor_tensor(Qn[:], a3[:], Qc[:], ALU.add), [ka3, kQ], [kQn])
                        X, XT, kX, kXT = X2, X2T, kX2, kX2T
                        Qc, kQ = Qn, kQn
                    tq, ktq = PF()
                    for h in range(4):
                        S.op('pe', lambda e: e.transpose(tq[:, h, :], Qc[:, h, :], id32), [kQ, 'cst'], [ktq])
                    S.op('dve', lambda e: e.tensor_tensor(BTa1[:], tq[:], mk(0), ALU.add), [ktq, 'cst'], [('i', 'BTa')])
                    S.op('pool', lambda e: e.tensor_tensor(Ba1[:], Qc[:], mk(0), ALU.add), [kQ, 'cst'], [('i', 'Ba')])
                    Bc, BTc, kB, kBT = Ba1, BTa1, ('i', 'Ba'), ('i', 'BTa')
                    mb_ = [(Bb1, BTb1, ('i', 'Bb'), ('i', 'BTb')), (Ba1, BTa1, ('i', 'Ba'), ('i', 'BTa'))]
                    for mi, midx in enumerate((11, 12, 13)):
                        Bn, BTn, kBn, kBTn = mb_[mi % 2]
                        S.op('pool', lambda e: e.tensor_tensor(NlT1[:], NT1[:], mk(midx), ALU.mult), [kNT, 'cst'], [('i', 'NlT')])
                        c, kc = PF()
                        for h in range(4):
                            S.op('pe', lambda e: e.matmul(c[:, h, :], NlT1[:, h, :], Bc[:, h, :], start=True, stop=True), [('i', 'NlT'), kB], [kc])
                        S.op('act', lambda e: e.copy(Cc1[:], c[:]), [kc], [('i', 'Cc')])
                        if mi < 2:
                            S.op('pool', lambda e: e.tensor_tensor(Nl1[:], Nn1[:], mk(midx), ALU.mult), [kN, 'cst'], [('i', 'Nl')])
                            c2, kc2 = PF()
                            for h in range(4):
                                S.op('pe', lambda e: e.matmul(c2[:, h, :], Nl1[:, h, :], BTc[:, h, :], start=True, stop=True), [('i', 'Nl'), kBT], [kc2])
                            S.op('act', lambda e: e.copy(C2c1[:], c2[:]), [kc2], [('i', 'C2c')])
                        bn, kbn = PF()
                        for h in range(4):
                            S.op('pe', lambda e: e.matmul(bn[:, h, :], BTc[:, h, :], Cc1[:, h, :], start=True, stop=True), [kBT, ('i', 'Cc')], [kbn])
                        S.op('dve', lambda e: e.tensor_tensor(Bn[:], bn[:], Bc[:], ALU.add), [kbn, kB], [kBn])
                        if mi < 2:
                            bt, kbt = PF()
                            for h in range(4):
                                S.op('pe', lambda e: e.matmul(bt[:, h, :], Bc[:, h, :], C2c1[:, h, :], start=True, stop=True), [kB, ('i', 'C2c')], [kbt])
                            S.op('dve', lambda e: e.tensor_tensor(BTn[:], bt[:], BTc[:], ALU.add), [kbt, kBT], [kBTn])
                        Bc, kB = Bn, kBn
                        if mi < 2:
                            BTc, kBT = BTn, kBTn
                    S.op('pool', lambda e: e.tensor_copy(Tb[p][:], Bc[:]), [kB], [K('Tb')])
                    if GSTOP < 5:
                        continue
                    u_, ku = PF()
                    for h in range(4):
                        S.op('pe', lambda e: e.matmul(u_[:, h, :], Tb[p][:, h, :], vtm[p][:, h, :], start=True, stop=True), [K('Tb'), K('vtm')], [ku])
                    w_, kw = PF()
                    for h in range(4):
                        S.op('pe', lambda e: e.matmul(w_[:, h, :], kE[p][:, h, :], Tb[p][:, h, :], start=True, stop=True), [K('Tb'), K('kE')], [kw])
                    S.op('dve', lambda e: e.tensor_tensor(ub[p][:], u_[:], bc_i(beta[:, b, d * 4:(d + 1) * 4]), ALU.mult), [ku, 'gbeta'], [K('ub')])
                    S.op('act', lambda e: e.copy(wT[p][:], w_[:]), [kw], [K('wT')])
                    if GSTOP < 6:
                        continue
                    if d == 0 and b == 0 and l == 0:
                        fl = lambda t: t[:].rearrange("p h c -> p (h c)")
                        self.dump("gsm", gsm[p][:], [K('gsm'), K('gsm2'), K('gsm3')])
                        self.dump("E", fl(E[p]), [K('E')])
                        self.dump("EGb", fl(EGb[p]), [K('EGb')])
                        self.dump("attnT", fl(attnT[p]), [K('attnT')])
                        self.dump("Tb", fl(Tb[p]), [K('Tb')])
                        self.dump("ub", fl(ub[p]), [K('ub')])
                        self.dump("wT", fl(wT[p]), [K('wT')])
                        self.dump("kE", fl(kE[p]), [K('kE')])
                        self.dump("kg", fl(kg[p]), [K('kg')])
                        self.dump("vtm", fl(vtm[p]), [K('vtm')])
                        self.dump("qgT", fl(qgT[p]), [K('qgT')])
                    p1, kp1 = PF()
                    for h in range(4):
                        S.op('pe', lambda e: e.matmul(p1[:, h, :], wT[p][:, h, :], Sbf[:, h, :], start=True, stop=True), [K('wT'), ('Sbf', h)], [kp1],
                             signal=(h == 3))
                    for h in range(4):
                        nb_ = nbeta[:, b, d * 4 + h:d * 4 + h + 1]
                        S.op('dve', lambda e: e.scalar_tensor_tensor(vnew[p][:, h, :], p1[:, h, :], nb_, ub[p][:, h, :], ALU.mult, ALU.add),
                             [kp1, 'gnbeta', K('ub')], [K('vnew', h)])
                    o_, ko = PF()
                    for h in range(4):
                        S.op('pe', lambda e: e.matmul(o_[:, h, :], Sbf[:, h, :], qgT[p][:, h, :], start=True, stop=False), [('Sbf', h), K('qgT')], [ko], signal=False)
                        S.op('pe', lambda e: e.matmul(o_[:, h, :], vnew[p][:, h, :], attnT[p][:, h, :], start=False, stop=True), [K('vnew', h), K('attnT')], [ko],
                             signal=(h == 3))
                    if d == 0:
                        S.op('dve', lambda e: e.tensor_copy(oacc[:, :, tok], o_[:]), [ko], [('oacc', b)])
                    else:
                        S.op('dve', lambda e: e.tensor_tensor(oacc[:, :, tok], o_[:], oacc[:, :, tok], ALU.add), [ko, ('oacc', b)], [('oacc', b)])
                    su, ksu = PF()
                    for h in range(4):
                        S.op('pe', lambda e: e.matmul(su[:, h, :], kg[p][:, h, :], vnew[p][:, h, :], start=True, stop=True), [K('kg'), K('vnew', h)], [ksu],
                             signal=(h == 3))
                    for h in range(4):
                        S.op('dve', lambda e: e.scalar_tensor_tensor(S32[:, h, :], S32[:, h, :], gla[p][:, h:h + 1], su[:, h, :], ALU.mult, ALU.add),
                             [ksu, K('gla'), ('S32', h)], [('S32', h)])
                        S.op('dve', lambda e: e.tensor_copy(Sbf[:, h, :], S32[:, h, :]), [('S32', h)], [('Sbf', h)])
            if GSTOP < 7:
                S.barrier()
                return
            if l == 0:
                self.dump("oacc", oacc[:].rearrange("p h t -> p (h t)"), [('oacc', b) for b in range(NB)])
            S.barrier()
            fl_ = lambda t: t[:].rearrange("p h c -> p (h c)")
            zt = [fl_(GM[i]) for i in range(2)]
            sq = [fl_(attnT[i]) for i in range(2)]
            rs = [fl_(Dm[i]) for i in range(2)]
            tmp = [fl_(E[i]) for i in range(2)]
            on = [fl_(NK[i]) for i in range(2)]
            yo = [fl_(kE[i]) for i in range(2)]
            it = 0
            for (t0, tn, s) in self.tiles(512):
                kb_ = [('oacc', b) for b in range(t0 // 128, (t0 + tn) // 128)]
                for h in range(4):
                    p = it % 2
                    it += 1
                    ss, kss = PF()
                    ssv = ss[:].rearrange("p h c -> p (h c)")
                    S.dma(zt[p][:, :tn], self.P_z[h * 128:(h + 1) * 128, t0:t0 + tn], writes=[('gzt', p)])
                    S.op('act', lambda e: e.activation(zt[p][:, :tn], zt[p][:, :tn], AF.Silu), [('gzt', p)], [('gzt', p)])
                    S.op('act', lambda e: e.activation(sq[p][:, :tn], oacc[:, h, t0:t0 + tn], AF.Square), kb_, [('gsq', p)])
                    S.op('pe', lambda e: e.matmul(ssv[:, :tn], self.onesb[:], sq[p][:, :tn], start=True, stop=True), [('gsq', p), 'onesb'], [kss])
                    self.rstd_from_ss(ssv, rs[p], tn, 1.0 / 128, [kss], [('grs', p)], tmp[p])
                    S.op('dve', lambda e: e.tensor_tensor(on[p][:, :tn], oacc[:, h, t0:t0 + tn], rs[p][:, :tn], ALU.mult), kb_ + [('grs', p)], [('gon', p)])
                    S.op('dve', lambda e: e.scalar_tensor_tensor(yo[p][:, :tn], on[p][:, :tn], self.pvt[:, 136:137], zt[p][:, :tn], ALU.mult, ALU.mult),
                         [('gon', p), ('gzt', p), 'pvt'], [('gyo', p)])
                    S.dma(self.yT[h * 128:(h + 1) * 128, t0:t0 + tn], yo[p][:, :tn], reads=[('gyo', p)])
            S.barrier()

    def post_norm_residual(self, y32, xt, sq, rstd, tmp, ss_ps, t0, tn, s, ic, par, yk, xk, sqk):
        S = self.S
        S.op('act', lambda e: e.activation(sq[:, :, :tn], y32[:, :, :tn], AF.Square), yk, [sqk])
        for k in range(KC):
            S.op('pe', lambda e: e.matmul(ss_ps[:, :tn], self.onesb[:], sq[:, k, :tn], start=(k == 0), stop=(k == KC - 1)),
                 [sqk, 'onesb'], [('pss', par)], signal=(k == KC - 1))
        self.rstd_from_ss(ss_ps, rstd, tn, 1.0 / D, [('pss', par)], [('prstd', par)], tmp)
        S.op('dve', lambda e: e.tensor_tensor(y32[:, :, :tn], y32[:, :, :tn], rstd[:, :tn].unsqueeze(1).to_broadcast([128, KC, tn]), ALU.mult),
             yk + [('prstd', par)], yk)
        for k in range(KC):
            S.op('dve',
                 lambda e: e.scalar_tensor_tensor(xt[:, k, :tn], y32[:, k, :tn], self.modv[:, ic, k, s:s + 1], xt[:, k, :tn], ALU.mult, ALU.add),
                 yk + ['modv%d' % ic] + xk, [('pxo', par, k)])
        S.dma(self.xT.rearrange("(k p) t -> p k t", p=128)[:, :, t0:t0 + tn], xt[:, :, :tn], reads=[('pxo', par, k) for k in range(KC)] + xk)

    def phase_wout(self, l):
        nc, S = self.nc, self.S
        with contextlib.ExitStack() as st:
            sb = lambda n, s, dt=F32: self.sb(n, s, dt, st)
            w, wkeys = self.load_weight_bf(st, "wout", self.w_out[l], KC, D)
            yt = [sb("f_yt%d" % i, [128, KC, 512], BF16) for i in range(2)]
            xt = [sb("f_xt%d" % i, [128, KC, 512]) for i in range(2)]
            y32 = [sb("f_y32%d" % i, [128, KC, 512]) for i in range(2)]
            sq = [sb("f_sq%d" % i, [128, KC, 512], BF16) for i in range(2)]
            rstd = [sb("f_rstd%d" % i, [128, 512]) for i in range(2)]
            tmp = [sb("f_tmp%d" % i, [128, 512]) for i in range(2)]
            ss_ps = [st.enter_context(self.pst("f_ss%d" % i, [128, 512], F32)) for i in range(2)]
            ops = [st.enter_context(self.pst("f_ops%d" % i, [128, 512], F32)) for i in range(4)]
            ei = 0
            for ti, (t0, tn, s) in enumerate(self.tiles(512)):
                p = ti % 2
                S.dma(yt[p][:, :, :tn], self.yT.rearrange("(k p) t -> p k t", p=128)[:, :, t0:t0 + tn], writes=[('fyt', p)])
                S.dma(xt[p][:, :, :tn], self.xT.rearrange("(k p) t -> p k t", p=128)[:, :, t0:t0 + tn], writes=[('fxt', p)])
                for oc in range(KC):
                    o = ops[ei % 4]
                    ko = ('fops', ei % 4)
                    ei += 1
                    for k in range(KC):
                        S.op('pe', lambda e: e.matmul(o[:, :tn], w[:, k, oc * 128:(oc + 1) * 128], yt[p][:, k, :tn], start=(k == 0), stop=(k == KC - 1)),
                             [('fyt', p)] + wkeys, [ko], signal=(k == KC - 1))
                    S.op('act' if oc % 2 else 'dve', lambda e: (e.copy if oc % 2 else e.tensor_copy)(y32[p][:, oc, :tn], o[:, :tn]), [ko], [('fy32', p, oc)])
                self.post_norm_residual(y32[p], xt[p], sq[p], rstd[p], tmp[p], ss_ps[p], t0, tn, s, 2, p,
                                        [('fy32', p, oc) for oc in range(KC)], [('fxt', p)], ('fsq', p))
            S.barrier()

    def phase_mlp(self, l):
        nc, S = self.nc, self.S
        TN = 256
        with contextlib.ExitStack() as st:
            sb = lambda n, s, dt=F32: self.sb(n, s, dt, st)
            w1, w1k = self.load_weight_bf(st, "wm1", self.w_m1[l], KC, 4 * D, piece=256)
            w2, w2k = self.load_weight_bf(st, "wm2", self.w_m2[l], 32, D, kpiece=4)
            xt = [sb("h_xt%d" % i, [128, KC, TN]) for i in range(2)]
            sq = [sb("h_sq%d" % i, [128, KC, TN], BF16) for i in range(1)]
            h = [sb("h_h%d" % i, [128, KC, TN], BF16) for i in range(1)]
            hid = [sb("h_hid%d" % i, [128, 32, TN], BF16) for i in range(1)]
            rl = [sb("h_rl%d" % i, [128, TN], BF16) for i in range(4)]
            y32 = [sb("h_y32%d" % i, [128, KC, TN]) for i in range(1)]
            rstd = [sb("h_rstd%d" % i, [128, TN]) for i in range(2)]
            tmp = [sb("h_tmp%d" % i, [128, TN]) for i in range(2)]
            ss_ps = [st.enter_context(self.pst("h_ss%d" % i, [128, 512], F32)) for i in range(2)]
            ops = [st.enter_context(self.pst("h_ops%d" % i, [128, 512], F32)) for i in range(4)]
            ei = 0
            for ti, (t0, tn, s) in enumerate(self.tiles(TN)):
                p = ti % 2
                if MSTOP < 1:
                    continue
                yxk = [('hy32', oc) for oc in range(KC)]
                self.norm_mod(t0, tn, s, xt[p], sq[0], y32[0], h[0], rstd[p], tmp[p], ss_ps[p], 3, 4, p, bpar='m', xnk=yxk, hpar='m')
                hk = [('h', 'm', k) for k in range(KC)]
                if MSTOP < 2:
                    continue
                for j in range(32):
                    o = ops[ei % 4]
                    ko = ('hops', ei % 4)
                    r_ = rl[ei % 4]
                    kr_ = ('hrl', ei % 4)
                    ei += 1
                    for k in range(KC):
                        S.op('pe', lambda e: e.matmul(o[:, :tn], w1[:, k, j * 128:(j + 1) * 128], h[0][:, k, :tn], start=(k == 0), stop=(k == KC - 1)),
                             hk + w1k, [ko], signal=(k == KC - 1))
                    S.op('act', lambda e: e.activation(r_[:, :tn], o[:, :tn], AF.Relu), [ko], [kr_])
                    S.op('pool' if j % 2 else 'dve', lambda e: e.tensor_tensor(hid[0][:, j, :tn], r_[:, :tn], r_[:, :tn], ALU.mult), [kr_], [('hid', j)])
                hidk = [('hid', j) for j in range(32)]
                if MSTOP < 3:
                    continue
                for oc in range(KC):
                    o = ops[ei % 4]
                    ko = ('hops', ei % 4)
                    ei += 1
                    for j in range(32):
                        S.op('pe', lambda e: e.matmul(o[:, :tn], w2[:, j, oc * 128:(oc + 1) * 128], hid[0][:, j, :tn], start=(j == 0), stop=(j == 31)),
                             hidk + w2k, [ko], signal=(j == 31))
                    S.op('act' if oc % 2 else 'dve', lambda e: (e.copy if oc % 2 else e.tensor_copy)(y32[0][:, oc, :tn], o[:, :tn]), [ko], [('hy32', oc)])
                if MSTOP < 4:
                    continue
                self.post_norm_residual(y32[0], xt[p], sq[0], rstd[p], tmp[p], ss_ps[p], t0, tn, s, 5, p,
                                        [('hy32', oc) for oc in range(KC)], [('xt', p)], ('sq', 'm'))
            S.barrier()


def _pp(v, nch):
    return np.ascontiguousarray(np.asarray(v, np.float32).reshape(nch, 128).T)


def make_consts():
    c = np.zeros((128, 14, 128), np.float32)
    idx = np.arange(128)
    c[:, 0] = np.eye(128)
    c[:, 1] = 1.0
    c[:, 2] = (idx[:, None] <= idx[None, :])
    c[:, 3] = (idx[:, None] >= idx[None, :])
    c[:, 4] = np.where(idx[None, :] >= idx[:, None], 0.0, -30000.0)
    c[:, 5] = np.where(idx[None, :] <= idx[:, None], 0.0, -30000.0)
    c[:, 6] = (idx[None, :] > idx[:, None])
    c[:, 7] = (idx[None, :] < idx[:, None])
    same = lambda n: (idx[:, None] // n) == (idx[None, :] // n)
    c[:, 10] = same(16)
    c[:, 11] = same(32) & ~same(16)
    c[:, 12] = same(64) & ~same(32)
    c[:, 13] = ~same(64)
    return c.reshape(128, 14 * 128)


def make_rope(Lc, Ll):
    rows = Ll // 64
    row = np.repeat(np.arange(rows, dtype=np.float32), 64)
    col = np.tile(np.arange(64, dtype=np.float32), rows)
    half = 16
    inv = (np.float32(10000.0) ** (-np.arange(0, half, 2, dtype=np.float32) / half)).astype(np.float32)
    ang = np.stack([row[:, None] * inv, col[:, None] * inv], axis=1)
    ang = np.concatenate([ang, ang], axis=-1)
    cos = np.cos(ang).reshape(Ll, 32).T
    sin = np.sin(ang).reshape(Ll, 32).T.copy()
    sgn = np.ones(32, np.float32)
    sgn[0:8] = -1
    sgn[16:24] = -1
    sin = sin * sgn[:, None]
    out = np.zeros((2, 32, Lc + Ll), np.float32)
    out[0, :, :Lc] = 1.0
    out[0, :, Lc:] = cos
    out[1, :, Lc:] = sin
    return out


_SWAP = np.concatenate([np.arange(8, 16), np.arange(0, 8), np.arange(24, 32), np.arange(16, 24)])


def prep_shared(inp, depth):
    f = lambda k: np.asarray(inp[k], np.float32)
    pv = np.zeros((depth, 128, NPV), np.float32)
    for l in range(depth):
        pv[l, :, 0:48] = _pp(f('b_ada')[l], 48)
        pv[l, :, 48:56] = _pp(f('g_attn_pre')[l], 8)
        pv[l, :, 56:64] = _pp(f('g_attn_post')[l], 8)
        pv[l, :, 64:72] = _pp(f('g_mlp_pre')[l], 8)
        pv[l, :, 72:80] = _pp(f('g_mlp_post')[l], 8)
        cw = f('gdn_conv_w')[l]
        pv[l, :, 80:128] = cw.reshape(4, 12, 128).transpose(2, 1, 0).reshape(128, 48)
        pv[l, :, 128:136] = f('gdn_dt_bias')[l].reshape(1, 8)
        pv[l, :, 136:137] = f('gdn_norm_w')[l].reshape(128, 1)
        lw = f('lru_conv_w')[l]
        pv[l, :, 140:148] = lw.reshape(4, 2, 128).transpose(2, 1, 0).reshape(128, 8)
        pv[l, :, 148:150] = _pp(f('lru_conv_b')[l], 2)
        pv[l, :, 156:160] = f('lru_lambda')[l].reshape(2, 2, 128).transpose(2, 0, 1).reshape(128, 4)
        pv[l, :, 160:164] = f('lru_b_a')[l].reshape(2, 2, 128).transpose(2, 0, 1).reshape(128, 4)
        pv[l, :, 164:168] = f('lru_b_i')[l].reshape(2, 2, 128).transpose(2, 0, 1).reshape(128, 4)
        pv[l, :, 168:170] = _pp(f('mla_q_norm')[l], 2)
        pv[l, :, 170:171] = f('mla_kv_norm')[l].reshape(128, 1)
        pv[l, :, 172:180] = f('gdn_a_log')[l].reshape(1, 8)
    w_in = f('w_in')
    w_in_x = np.concatenate([w_in, w_in[:, :, 2960:2992][:, :, _SWAP]], axis=2)
    lw = np.zeros((depth, 2, 2, 2, 128, 128), np.float32)
    for ai, nm in enumerate(('lru_w_a', 'lru_w_i')):
        w = f(nm)
        for c in range(2):
            for gsub in range(2):
                lw[:, ai, :, c, gsub * 64:(gsub + 1) * 64, gsub * 64:(gsub + 1) * 64] = w[:, :, c * 2 + gsub]
    wuq = f('mla_w_uq')
    wuq_sw = np.zeros_like(wuq)
    for h in range(4):
        wuq_sw[:, :, h * 96 + 64:(h + 1) * 96] = wuq[:, :, h * 96 + 64:(h + 1) * 96][:, :, _SWAP]
    wuq_x = np.concatenate([wuq, wuq_sw], axis=2)
    wukv = f('mla_w_ukv').reshape(depth, 128, 4, 128)
    wukv_x = np.concatenate([wukv[:, :, :, :64].reshape(depth, 128, 256), wukv[:, :, :, 64:].reshape(depth, 128, 256)], axis=2)
    return dict(pv=pv, consts=make_consts(), w_ada=f('w_ada')[:depth], w_in=np.ascontiguousarray(w_in_x[:depth]),
                lru_w=lw[:depth], w_uq=np.ascontiguousarray(wuq_x[:depth]), w_ukv=np.ascontiguousarray(wukv_x[:depth]),
                w_out=f('w_out')[:depth], w_m1=f('w_mlp1')[:depth], w_m2=f('w_mlp2')[:depth])


def prep_core(inp, b, shared, Lc, Ll):
    x = np.asarray(inp['x'], np.float32)[b]
    ctx = np.asarray(inp['ctx'], np.float32)[b]
    xT = np.ascontiguousarray(np.concatenate([ctx, x], axis=0).T)
    cv = np.stack([_pp(np.asarray(inp['c'], np.float32)[b], 8), _pp(np.asarray(inp['c_ctx'], np.float32), 8)], axis=2)
    m = dict(xT=xT, cvec=np.ascontiguousarray(cv), rope=make_rope(Lc, Ll))
    for k in ('consts', 'w_ada', 'w_in', 'lru_w', 'w_uq', 'w_ukv', 'w_out', 'w_m1', 'w_m2'):
        m[k] = shared[k]
    m['pv'] = shared['pv']
    return m


_PROG_CACHE = {}


def run(inp, depth=DEPTH, ncores=8, dbg=False):
    Ll = inp['x'].shape[1]
    Lc = inp['ctx'].shape[1]
    key = (Lc, Ll, depth, dbg)
    if key not in _PROG_CACHE:
        p1 = Prog(Lc, Ll, depth, dbg)
        p1.build()
        p2 = Prog(Lc, Ll, depth, dbg, needed=p1.S.needed)
        _PROG_CACHE[key] = p2.build()
        print("sched: ops=%d signals %d -> %d, waits %d" % (sum(p1.S.pos.values()), p1.S.nsig, p2.S.nsig, p2.S.nwait))
    nc = _PROG_CACHE[key]
    shared = prep_shared(inp, depth)
    in_maps = [prep_core(inp, b, shared, Lc, Ll) for b in range(ncores)]
    res = run_bass_kernel_spmd(nc, in_maps, core_ids=list(range(ncores)))
    return res


def kernel(**inputs):
    res = run(inputs, DEPTH, 8, False)
    out = np.stack([np.ascontiguousarray(r["outT"].T) for r in res.results], axis=0)
    return out.astype(np.float32)
```

```python
import contextlib
import numpy as np
import ml_dtypes
import concourse.bass as bass
import concourse.mybir as mybir
from concourse.bass_utils import run_bass_kernel_spmd

F32 = mybir.dt.float32
BF16 = mybir.dt.bfloat16
AF = mybir.ActivationFunctionType
ALU = mybir.AluOpType

D = 1024
KC = 8
DEPTH = 4
IN_W = 2992
IN_WX = 3024
EPS = 1e-6
NPV = 180
GSTOP = 99
GSKIP = 0
MSTOP = 99
GSUB = 99
PHASES = ['ada', 'proj', 'gdn_prep', 'lru', 'mla', 'gdn', 'wout', 'mlp']


class Sched:
    NQ = 8
    CE = ('pe', 'act', 'dve', 'pool')

    def __init__(self, nc, st, needed=None):
        self.nc = nc
        self.eng = {'pe': nc.tensor, 'act': nc.scalar, 'dve': nc.vector, 'pool': nc.gpsimd, 'sp': nc.sync}
        self.sem = {}
        self.pos = {}
        self.act = {}
        self.actual_at = {}
        for k in self.CE:
            self.sem[k] = st.enter_context(nc.semaphore('s_' + k))
            self.pos[k] = 0
            self.act[k] = 0
            self.actual_at[k] = [0]
        self.dq = {}
        for q in ('sp', 'act', 'pool'):
            self.dq[q] = dict(sems=[st.enter_context(nc.semaphore('d_%s%d' % (q, i))) for i in range(self.NQ)],
                              vals=[0] * self.NQ, n=0)
        self.seen = {k: {} for k in self.eng}
        self.bufs = {}
        self.nwait = 0
        self.nsig = 0
        self.dummy_w = None
        self.analysis = needed is None
        self.needed = set() if needed is None else needed

    def need(self, ek, ev):
        if ev is None:
            return
        if ev[0] == 'dma':
            _, sem, val = ev
            sid = id(sem)
            if self.seen[ek].get(sid, 0) >= val:
                return
            self.eng[ek].wait_ge(sem, val)
            self.nwait += 1
            self.seen[ek][sid] = val
            return
        src, pos = ev
        if ek == 'pe' and src == 'pe':
            return
        if self.seen[ek].get(src, 0) >= pos:
            return
        if self.analysis:
            self.needed.add((src, pos))
        val = self.actual_at[src][pos]
        assert val is not None, (src, pos)
        self.eng[ek].wait_ge(self.sem[src], val)
        self.nwait += 1
        self.seen[ek][src] = pos

    def op(self, ek, fn, reads=(), writes=(), signal=True, dma=False):
        nw0 = self.nwait
        for k in reads:
            b = self.bufs.get(k)
            if b is not None:
                self.need(ek, b['w'])
        for k in writes:
            b = self.bufs.get(k)
            if b is not None:
                self.need(ek, b['w'])
                for ev in b['r'].values():
                    self.need(ek, ev)
        if ek == 'pe' and self.nwait != nw0 and self.dummy_w is not None:
            self.eng['pe'].ldweights(self.dummy_w)
        if dma:
            q = self.dq[ek]
            i = q['n'] % self.NQ
            q['n'] += 1
            sem = q['sems'][i]
            if q['vals'][i] > 0:
                self.need(ek, ('dma', sem, q['vals'][i]))
            ins = fn(self.eng[ek])
            q['vals'][i] += 16
            ins.then_inc(sem, 16)
            ev = ('dma', sem, q['vals'][i])
            rkey = ('d', id(sem))
        else:
            ins = fn(self.eng[ek])
            self.pos[ek] += 1
            pos = self.pos[ek]
            if self.analysis or (ek, pos) in self.needed:
                self.act[ek] += 1
                ins.then_inc(self.sem[ek], 1)
                self.actual_at[ek].append(self.act[ek])
                self.nsig += 1
            else:
                self.actual_at[ek].append(None)
            ev = (ek, pos)
            rkey = ek
        self._mark(reads, writes, ev, rkey)

    def _mark(self, reads, writes, ev, rkey):
        for k in reads:
            b = self.bufs.get(k)
            if b is None:
                b = self.bufs[k] = dict(w=None, r={})
            b['r'][rkey] = ev
        for k in writes:
            self.bufs[k] = dict(w=ev, r={})

    def dma(self, out, in_, reads=(), writes=(), q='sp'):
        self.op(q, lambda e: e.dma_start(out=out, in_=in_), reads, writes, dma=True)

    def barrier(self):
        for ek in self.eng:
            for f in self.CE:
                if f != ek and self.pos[f] > 0:
                    self.need(ek, (f, self.pos[f]))
            for qn, q in self.dq.items():
                for i in range(self.NQ):
                    if q['vals'][i] > 0:
                        self.need(ek, ('dma', q['sems'][i], q['vals'][i]))
        self.bufs = {}


class Prog:
    def __init__(self, Lc, Ll, depth, dbg=False, needed=None):
        self.Lc, self.Ll, self.depth, self.dbg = Lc, Ll, depth, dbg
        self.needed = needed
        self.Lt = Lc + Ll
        self.NB = self.Lt // 128
        self.NBc = Lc // 128
        nc = self.nc = bass.Bass("TRN2", target_bir_lowering=False)
        Lt = self.Lt
        di = lambda n, s, dt=F32: nc.dram_tensor(n, s, dt, kind="ExternalInput").ap()
        self.xT_in = di("xT", [D, Lt])
        self.cvec = di("cvec", [128, KC, 2])
        self.pv = di("pv", [depth, 128, NPV])
        self.consts = di("consts", [128, 14 * 128])
        self.rope = di("rope", [2, 32, Lt])
        self.w_ada = di("w_ada", [depth, D, 6 * D])
        self.w_in = di("w_in", [depth, D, IN_WX])
        self.lru_w = di("lru_w", [depth, 2, 2, 2, 128, 128])
        self.w_uq = di("w_uq", [depth, 256, 2 * 384])
        self.w_ukv = di("w_ukv", [depth, 128, 512])
        self.w_out = di("w_out", [depth, D, D])
        self.w_m1 = di("w_m1", [depth, D, 4 * D])
        self.w_m2 = di("w_m2", [depth, 4 * D, D])
        self.outT = nc.dram_tensor("outT", [D, Ll], F32, kind="ExternalOutput").ap()
        kind = "ExternalOutput" if dbg else "Internal"
        ds = lambda n, s, dt=F32: nc.dram_tensor(n, s, dt, kind=kind).ap()
        self.xT = ds("xTs", [D, Lt])
        self.P_qkv = ds("P_qkv", [1536, Lt])
        self.P_z = ds("P_z", [512, Lt])
        self.P_lx = ds("P_lx", [256, Lt])
        self.P_lg = ds("P_lg", [256, Lt])
        self.P_cq = ds("P_cq", [256, Lt])
        self.P_ckv = ds("P_ckv", [128, Lt])
        self.P_kr = ds("P_kr", [64, Lt])
        self.qkvT = ds("qkvT", [1536, Lt], BF16)
        self.yT = ds("yT", [D, Lt], BF16)
        self.dbg_ba = ds("dbg_ba", [128, self.NB * 16]) if dbg else None

    def dump(self, name, ap, keys):
        if not self.dbg:
            return
        shp = list(ap.shape)
        t = self.nc.dram_tensor("dd_" + name, shp, ap.dtype, kind="ExternalOutput").ap()
        self.S.dma(t, ap, reads=keys)

    def uname(self, n):
        self._uid = getattr(self, '_uid', 0) + 1
        return "%s_u%d" % (n, self._uid)

    def pst(self, n, shape, dt):
        return self.nc.psum_tensor(self.uname(n), shape, dt)

    def tiles(self, T):
        out = []
        for (a, b, s) in ((0, self.Lc, 1), (self.Lc, self.Lt, 0)):
            t = a
            while t < b:
                n = min(T, b - t)
                out.append((t, n, s))
                t += n
        return out

    def build(self):
        nc = self.nc
        with contextlib.ExitStack() as st:
            self.S = S = Sched(nc, st, self.needed)
            sb = lambda n, s, dt=F32, stack=st: stack.enter_context(nc.sbuf_tensor(self.uname(n), list(s), dt))
            self.sb = sb
            self.cst = sb("cst", [128, 14, 128])
            self.identb = sb("identb", [128, 128], BF16)
            self.onesb = sb("onesb", [128, 128], BF16)
            self.pvt = sb("pvt", [128, NPV])
            self.modv = sb("modv", [128, 6, KC, 2])
            self.ba = sb("ba", [128, self.NB, 16])
            S.dma(self.cst[:].rearrange("p a b -> p (a b)"), self.consts[:, :], writes=['cst'])
            S.op('dve', lambda e: e.tensor_copy(self.identb[:], self.cst[:, 0, :]), ['cst'], ['identb'])
            S.op('dve', lambda e: e.tensor_copy(self.onesb[:], self.cst[:, 1, :]), ['cst'], ['onesb'])
            S.barrier()
            S.dummy_w = self.identb[:]
            for (t0, tn, s) in self.tiles(2048):
                for k in range(KC):
                    S.dma(self.xT[k * 128:(k + 1) * 128, t0:t0 + tn], self.xT_in[k * 128:(k + 1) * 128, t0:t0 + tn],
                          writes=[('xT', k, t0)])
            S.barrier()
            for l in range(self.depth):
                self.layer(l)
            for k in range(KC):
                S.dma(self.outT[k * 128:(k + 1) * 128, :], self.xT[k * 128:(k + 1) * 128, self.Lc:self.Lt])
            S.barrier()
        return nc

    def ident(self):
        return self.cst[:, 0, :]

    def ones32(self):
        return self.cst[:, 1, :]

    def pvs(self, off, n=1):
        return self.pvt[:, off:off + n]

    def rstd_from_ss(self, ss_ps, out_sb, n, inv_n, keys_r, keys_w, tmp):
        S = self.S
        S.op('act', lambda e: e.activation(tmp[:, :n], ss_ps[:, :n], AF.Sqrt, bias=EPS, scale=inv_n), keys_r, [('tmp', id(tmp))])
        S.op('dve', lambda e: e.reciprocal(out_sb[:, :n], tmp[:, :n]), [('tmp', id(tmp))], keys_w)

    def load_weight_bf(self, st, name, dram2d, K, N, piece=512, kpiece=None):
        S = self.S
        w = self.sb(name, [128, K, N], BF16, st)
        src = dram2d.rearrange("(k p) n -> p k n", p=128)
        with contextlib.ExitStack() as s2:
            if kpiece is None:
                stg = [self.sb(name + "_stg%d" % i, [128, K, piece], F32, s2) for i in range(2)]
                i = 0
                for c0 in range(0, N, piece):
                    n = min(piece, N - c0)
                    sg = stg[i % 2]
                    S.dma(sg[:, :, :n], src[:, :, c0:c0 + n], writes=[(name, 'stg', i % 2)])
                    S.op('pool', lambda e: e.tensor_copy(w[:, :, c0:c0 + n], sg[:, :, :n]), [(name, 'stg', i % 2)], [(name, 'w', i)])
                    i += 1
            else:
                stg = [self.sb(name + "_stg%d" % i, [128, kpiece, N], F32, s2) for i in range(2)]
                i = 0
                for k0 in range(0, K, kpiece):
                    sg = stg[i % 2]
                    S.dma(sg[:], src[:, k0:k0 + kpiece, :], writes=[(name, 'stg', i % 2)])
                    S.op('pool', lambda e: e.tensor_copy(w[:, k0:k0 + kpiece, :], sg[:]), [(name, 'stg', i % 2)], [(name, 'w', i)])
                    i += 1
            S.barrier()
        return w, []

    def layer(self, l):
        for nm in PHASES:
            getattr(self, 'phase_' + nm)(l)

    def phase_ada(self, l):
        nc, S = self.nc, self.S
        with contextlib.ExitStack() as st:
            sb = lambda n, s, dt=F32: self.sb(n, s, dt, st)
            S.dma(self.pvt[:], self.pv[l], writes=['pvt'])
            cv = sb("cv", [128, KC, 2])
            sc = sb("sc", [128, KC, 2])
            mod = sb("mod", [128, 48, 2])
            S.dma(cv[:], self.cvec[:, :, :], writes=['cv'])
            S.op('act', lambda e: e.activation(sc[:], cv[:], AF.Silu), ['cv'], ['sc'])
            stg = [sb("ada_stg%d" % i, [128, KC, 512]) for i in range(2)]
            ps = st.enter_context(self.pst("ada_ps", [128, 48, 2], F32))
            src = self.w_ada[l].rearrange("(k p) n -> p k n", p=128)
            for pc in range(12):
                sg = stg[pc % 2]
                S.dma(sg[:], src[:, :, pc * 512:(pc + 1) * 512], writes=[('adastg', pc % 2)])
                for jj in range(4):
                    j = pc * 4 + jj
                    for k in range(KC):
                        S.op('pe', lambda e: e.matmul(ps[:, j, :], sg[:, k, jj * 128:(jj + 1) * 128], sc[:, k, :],
                                                      start=(k == 0), stop=(k == KC - 1)),
                             [('adastg', pc % 2), 'sc'], ['adaps'], signal=(k == KC - 1))
            bb = self.pvs(0, 48).unsqueeze(2).to_broadcast([128, 48, 2])
            S.op('dve', lambda e: e.tensor_tensor(mod[:], ps[:], bb, ALU.add), ['adaps', 'pvt'], ['mod'])
            mv = self.modv
            g = lambda off: self.pvs(off, 8).unsqueeze(2).to_broadcast([128, 8, 2])
            S.op('dve', lambda e: e.scalar_tensor_tensor(mv[:, 0], mod[:, 8:16, :], 1.0, g(48), ALU.add, ALU.mult), ['mod', 'pvt'], ['modv0'])
            S.op('dve', lambda e: e.tensor_copy(mv[:, 1], mod[:, 0:8, :]), ['mod'], ['modv1'])
            S.op('dve', lambda e: e.tensor_tensor(mv[:, 2], mod[:, 16:24, :], g(56), ALU.mult), ['mod', 'pvt'], ['modv2'])
            S.op('dve', lambda e: e.scalar_tensor_tensor(mv[:, 3], mod[:, 32:40, :], 1.0, g(64), ALU.add, ALU.mult), ['mod', 'pvt'], ['modv3'])
            S.op('dve', lambda e: e.tensor_copy(mv[:, 4], mod[:, 24:32, :]), ['mod'], ['modv4'])
            S.op('dve', lambda e: e.tensor_tensor(mv[:, 5], mod[:, 40:48, :], g(72), ALU.mult), ['mod', 'pvt'], ['modv5'])
            S.barrier()

    def norm_mod(self, t0, tn, s, xt, sq, xn, h, rstd, tmp, ss_ps, ia, ib, par, bpar=None, xnk=None, hpar=None):
        S = self.S
        kx = ('xt', par)
        bpar = par if bpar is None else bpar
        hpar = par if hpar is None else hpar
        xnk = [('xn', bpar, k) for k in range(KC)] if xnk is None else xnk
        if True:
            S.dma(xt[:, :, :tn], self.xT.rearrange("(k p) t -> p k t", p=128)[:, :, t0:t0 + tn], writes=[kx])
        S.op('act', lambda e: e.activation(sq[:, :, :tn], xt[:, :, :tn], AF.Square), [kx], [('sq', bpar)])
        for k in range(KC):
            S.op('pe', lambda e: e.matmul(ss_ps[:, :tn], self.onesb[:], sq[:, k, :tn], start=(k == 0), stop=(k == KC - 1)),
                 [('sq', bpar), 'onesb'], [('ss', par)], signal=(k == KC - 1))
        self.rstd_from_ss(ss_ps, rstd, tn, 1.0 / D, [('ss', par)], [('rstd', par)], tmp)
        S.op('dve', lambda e: e.tensor_tensor(xn[:, :, :tn], xt[:, :, :tn], rstd[:, :tn].unsqueeze(1).to_broadcast([128, KC, tn]), ALU.mult),
             [kx, ('rstd', par)], xnk)
        for k in range(KC):
            S.op('pool', lambda e: e.tensor_scalar(h[:, k, :tn], xn[:, k, :tn], self.modv[:, ia, k, s:s + 1], self.modv[:, ib, k, s:s + 1],
                                                   ALU.mult, ALU.add),
                 [xnk[k], 'modv%d' % ia, 'modv%d' % ib], [('h', hpar, k)])

    def phase_proj(self, l):
        nc, S = self.nc, self.S
        with contextlib.ExitStack() as st:
            sb = lambda n, s, dt=F32: self.sb(n, s, dt, st)
            w, wkeys = self.load_weight_bf(st, "win", self.w_in[l], KC, IN_WX)
            xt = [sb("b_xt%d" % i, [128, KC, 512]) for i in range(2)]
            xn = [sb("b_xn%d" % i, [128, KC, 512]) for i in range(2)]
            sq = [sb("b_sq%d" % i, [128, KC, 512], BF16) for i in range(2)]
            h = [sb("b_h%d" % i, [128, KC, 512], BF16) for i in range(2)]
            rstd = [sb("b_rstd%d" % i, [128, 512]) for i in range(2)]
            tmp = [sb("b_tmp%d" % i, [128, 512]) for i in range(2)]
            ost = [sb("b_ost%d" % i, [128, 512]) for i in range(4)]
            ss_ps = [st.enter_context(self.pst("b_ss%d" % i, [128, 512], F32)) for i in range(2)]
            ops = [st.enter_context(self.pst("b_ops%d" % i, [128, 512], F32)) for i in range(4)]
            bps = st.enter_context(self.pst("b_bps", [128, 4, 16], F32))
            groups = []
            for c in range(12):
                groups.append((self.P_qkv, c * 128, c * 128, 128))
            for c in range(4):
                groups.append((self.P_z, c * 128, 1536 + c * 128, 128))
            for c in range(2):
                groups.append((self.P_lx, c * 128, 2064 + c * 128, 128))
            for c in range(2):
                groups.append((self.P_lg, c * 128, 2320 + c * 128, 128))
            for c in range(2):
                groups.append((self.P_cq, c * 128, 2576 + c * 128, 128))
            groups.append((self.P_ckv, 0, 2832, 128))
            groups.append((self.P_kr, 0, 2960, 64))
            ei = 0
            for ti, (t0, tn, s) in enumerate(self.tiles(512)):
                p = ti % 2
                self.norm_mod(t0, tn, s, xt[p], sq[p], xn[p], h[p], rstd[p], tmp[p], ss_ps[p], 0, 1, p)
                hk = [('h', p, k) for k in range(KC)]
                for gi, (dr, r0, c0, m) in enumerate(groups):
                    o = ops[ei % 4]
                    og = ost[ei % 4]
                    for k in range(KC):
                        S.op('pe', lambda e: e.matmul(o[:m, :tn], w[:, k, c0:c0 + m], h[p][:, k, :tn], start=(k == 0), stop=(k == KC - 1)),
                             hk + wkeys, [('ops', ei % 4)], signal=(k == KC - 1))
                    if ei % 2 == 0:
                        S.op('act', lambda e: e.copy(og[:m, :tn], o[:m, :tn]), [('ops', ei % 4)], [('ost', ei % 4)])
                    else:
                        S.op('dve', lambda e: e.tensor_copy(og[:m, :tn], o[:m, :tn]), [('ops', ei % 4)], [('ost', ei % 4)])
                    S.dma(dr[r0:r0 + m, t0:t0 + tn], og[:m, :tn], reads=[('ost', ei % 4)])
                    ei += 1
                nblk = tn // 128
                for bi in range(nblk):
                    for k in range(KC):
                        S.op('pe', lambda e: e.matmul(bps[:, bi, :], h[p][:, k, bi * 128:(bi + 1) * 128], w[:, k, 2048:2064],
                                                      start=(k == 0), stop=(k == KC - 1)),
                             hk + wkeys, ['bps'], signal=(k == KC - 1))
                b0 = t0 // 128
                S.op('dve', lambda e: e.tensor_copy(self.ba[:, b0:b0 + nblk, :], bps[:, :nblk, :]), ['bps'], [('ba', ti)])
                if self.dbg:
                    S.dma(self.dbg_ba[:, b0 * 16:(b0 + nblk) * 16], self.ba[:, b0:b0 + nblk, :].rearrange('p b c -> p (b c)'), reads=[('ba', ti)])
            S.barrier()

    def conv4(self, acc, xb, n, woff, kr, kw):
        S = self.S
        wv = lambda j: self.pvt[:, woff + j:woff + j + 1]
        S.op('dve', lambda e: e.tensor_scalar(acc[:, :n], xb[:, 0:n], wv(0), None, ALU.mult), kr + ['pvt'], kw)
        S.op('dve', lambda e: e.scalar_tensor_tensor(acc[:, :n], xb[:, 1:n + 1], wv(1), acc[:, :n], ALU.mult, ALU.add), kr + kw + ['pvt'], kw)
        S.op('dve', lambda e: e.scalar_tensor_tensor(acc[:, :n], xb[:, 2:n + 2], wv(2), acc[:, :n], ALU.mult, ALU.add), kr + kw + ['pvt'], kw)
        S.op('dve', lambda e: e.scalar_tensor_tensor(acc[:, :n], xb[:, 3:n + 3], wv(3), acc[:, :n], ALU.mult, ALU.add), kr + kw + ['pvt'], kw)

    def load_halo(self, xb, dram_rows, t0, tn, par, key):
        S = self.S
        seg0, seg1 = (0, self.Lc) if t0 < self.Lc else (self.Lc, self.Lt)
        a = max(seg0, t0 - 2)
        b = min(seg1, t0 + tn + 1)
        k = (key, par)
        if a > t0 - 2:
            S.op('pool', lambda e: e.memset(xb[:, 0:2], 0.0), [], [k])
        if b < t0 + tn + 1:
            S.op('pool', lambda e: e.memset(xb[:, tn + 2:tn + 3], 0.0), [k], [k])
        S.dma(xb[:, a - (t0 - 2):b - (t0 - 2)], dram_rows[:, a:b], reads=[k], writes=[k])
        return k

    def phase_gdn_prep(self, l):
        nc, S = self.nc, self.S
        with contextlib.ExitStack() as st:
            sb = lambda n, s, dt=F32: self.sb(n, s, dt, st)
            xb = [sb("c_xb%d" % i, [128, 515]) for i in range(2)]
            acc = [sb("c_acc%d" % i, [128, 512]) for i in range(2)]
            sl = [sb("c_sl%d" % i, [128, 512]) for i in range(2)]
            sq = [sb("c_sq%d" % i, [128, 512], BF16) for i in range(2)]
            rs = [sb("c_rs%d" % i, [128, 512]) for i in range(2)]
            tmp = [sb("c_tmp%d" % i, [128, 512]) for i in range(2)]
            ob = [sb("c_ob%d" % i, [128, 512], BF16) for i in range(2)]
            ss = [st.enter_context(self.pst("c_ss%d" % i, [128, 512], F32)) for i in range(2)]
            it = 0
            for c in range(12):
                part = c // 4
                rows = self.P_qkv[c * 128:(c + 1) * 128, :]
                for (t0, tn, s) in self.tiles(512):
                    p = it % 2
                    it += 1
                    kx = self.load_halo(xb[p], rows, t0, tn, p, 'cxb')
                    self.conv4(acc[p], xb[p], tn, 80 + c * 4, [kx], [('cacc', p)])
                    S.op('act', lambda e: e.activation(sl[p][:, :tn], acc[p][:, :tn], AF.Silu), [('cacc', p)], [('csl', p)])
                    if part < 2:
                        S.op('act', lambda e: e.activation(sq[p][:, :tn], sl[p][:, :tn], AF.Square), [('csl', p)], [('csq', p)])
                        S.op('pe', lambda e: e.matmul(ss[p][:, :tn], self.onesb[:], sq[p][:, :tn], start=True, stop=True),
                             [('csq', p), 'onesb'], [('css', p)])
                        self.rstd_from_ss(ss[p], rs[p], tn, 1.0, [('css', p)], [('crs', p)], tmp[p])
                        scale = (128.0 ** -0.5) if part == 0 else 1.0
                        S.op('dve', lambda e: e.scalar_tensor_tensor(ob[p][:, :tn], sl[p][:, :tn], scale, rs[p][:, :tn], ALU.mult, ALU.mult),
                             [('csl', p), ('crs', p)], [('cob', p)])
                    else:
                        S.op('dve', lambda e: e.tensor_copy(ob[p][:, :tn], sl[p][:, :tn]), [('csl', p)], [('cob', p)])
                    S.dma(self.qkvT[c * 128:(c + 1) * 128, t0:t0 + tn], ob[p][:, :tn], reads=[('cob', p)])
            S.barrier()

    def phase_lru(self, l):
        nc, S = self.nc, self.S
        Lt, Lc = self.Lt, self.Lc
        with contextlib.ExitStack() as st:
            sb = lambda n, s, dt=F32: self.sb(n, s, dt, st)
            wst = sb("l_wst", [128, 8, 128])
            wbf = sb("l_wbf", [128, 8, 128], BF16)
            S.dma(wst[:], self.lru_w[l].rearrange("a d c p n -> p (a d c) n"), writes=['lwst'])
            S.op('dve', lambda e: e.tensor_copy(wbf[:], wst[:]), ['lwst'], ['lwbf'])
            cs = sb("l_cs", [128, 4])
            S.op('act', lambda e: e.activation(cs[:], self.pvs(156, 4), AF.Exp, scale=-1.0), ['pvt'], ['lcs'])
            S.op('act', lambda e: e.activation(cs[:], cs[:], AF.Ln, bias=1.0), ['lcs'], ['lcs'])
            S.op('dve', lambda e: e.tensor_scalar(cs[:], cs[:], -8.0, None, ALU.mult), ['lcs'], ['lcs'])
            xb = sb("l_xb", [128, Lt + 6])
            xc = sb("l_xc", [128, Lt])
            xcb = sb("l_xcb", [128, Lt], BF16)
            rr = sb("l_r", [128, Lt])
            ii = sb("l_i", [128, Lt])
            aa = sb("l_a", [128, Lt])
            uu = sb("l_u", [128, Lt])
            hf = sb("l_hf", [128, Lt])
            hb = sb("l_hb", [128, Lt])
            gt = sb("l_gt", [128, Lt])
            yo = sb("l_yo", [128, Lt], BF16)
            gps = [st.enter_context(self.pst("l_gps%d" % i, [128, 512], F32)) for i in range(4)]
            for c in range(2):
                S.op('pool', lambda e: e.memset(xb[:], 0.0), [], ['lxb'])
                rows = self.P_lx[c * 128:(c + 1) * 128, :]
                S.dma(xb[:, 2:2 + Lc], rows[:, 0:Lc], reads=['lxb'], writes=['lxb'])
                S.dma(xb[:, Lc + 5:Lc + 5 + self.Ll], rows[:, Lc:Lt], reads=['lxb'], writes=['lxb'])
                S.dma(gt[:], self.P_lg[c * 128:(c + 1) * 128, :], writes=['lgt'])
                wv = lambda j: self.pvt[:, 140 + c * 4 + j:140 + c * 4 + j + 1]
                for (o0, t0, n) in ((0, 0, Lc), (Lc + 3, Lc, self.Ll)):
                    kk = [('lxc', t0)]
                    S.op('dve', lambda e: e.tensor_scalar(xc[:, t0:t0 + n], xb[:, o0:o0 + n], wv(0), self.pvt[:, 148 + c:149 + c], ALU.mult, ALU.add),
                         ['lxb', 'pvt'], kk)
                    for j in (1, 2, 3):
                        S.op('dve',
                             lambda e: e.scalar_tensor_tensor(xc[:, t0:t0 + n], xb[:, o0 + j:o0 + j + n], wv(j), xc[:, t0:t0 + n], ALU.mult, ALU.add),
                             ['lxb', 'pvt'] + kk, kk)
                kxc = [('lxc', 0), ('lxc', Lc)]
                S.op('act', lambda e: e.copy(xcb[:], xc[:]), kxc, ['lxcb'])
                S.op('act', lambda e: e.activation(gt[:], gt[:], AF.Gelu_apprx_tanh), ['lgt'], ['lgt'])
                tl = self.tiles(512)
                kr = [('lr', t0) for (t0, _, _) in tl]
                ki = [('li', t0) for (t0, _, _) in tl]
                gi = 0
                for d in range(2):
                    for (t0, tn, s_) in tl:
                        for ai, dst in ((0, rr), (1, ii)):
                            g = gps[gi % 4]
                            kg_ = ('lgps', gi % 4)
                            gi += 1
                            widx = (ai * 2 + d) * 2 + c
                            S.op('pe', lambda e: e.matmul(g[:, :tn], wbf[:, widx, :], xcb[:, t0:t0 + tn], start=True, stop=True),
                                 ['lwbf', 'lxcb'], [kg_])
                            bo = (160 if ai == 0 else 164) + d * 2 + c
                            S.op('act', lambda e: e.activation(dst[:, t0:t0 + tn], g[:, :tn], AF.Sigmoid, bias=self.pvt[:, bo:bo + 1]),
                                 [kg_, 'pvt'], [('lr' if ai == 0 else 'li', t0)])
                    ci = d * 2 + c
                    S.op('act', lambda e: e.activation(aa[:], rr[:], AF.Exp, scale=cs[:, ci:ci + 1]), kr + ['lcs'], ['la'])
                    S.op('dve', lambda e: e.tensor_tensor(rr[:], aa[:], aa[:], ALU.mult), ['la'] + kr, kr)
                    S.op('act', lambda e: e.activation(rr[:], rr[:], AF.Sqrt, bias=1.0, scale=-1.0), kr, kr)
                    S.op('pool', lambda e: e.tensor_tensor(uu[:], ii[:], xc[:], ALU.mult), ki + kxc, ['lu'])
                    S.op('dve', lambda e: e.tensor_tensor(uu[:], uu[:], rr[:], ALU.mult), ['lu'] + kr, ['lu'])
                    if d == 0:
                        S.op('dve', lambda e: e.tensor_tensor_scan(hf[:], aa[:], uu[:], 0.0, ALU.mult, ALU.add), ['la', 'lu'], ['lhf'])
                    else:
                        S.op('dve', lambda e: e.tensor_tensor_scan(hb[:, 0:Lc][:, ::-1], aa[:, 0:Lc][:, ::-1], uu[:, 0:Lc][:, ::-1],
                                                                   0.0, ALU.mult, ALU.add),
                             ['la', 'lu'], ['lhb'])
                        S.op('dve', lambda e: e.tensor_tensor_scan(hb[:, Lc:Lt][:, ::-1], aa[:, Lc:Lt][:, ::-1], uu[:, Lc:Lt][:, ::-1],
                                                                   hb[:, 0:1], ALU.mult, ALU.add),
                             ['la', 'lu', 'lhb'], ['lhb'])
                S.op('dve', lambda e: e.tensor_tensor(hf[:], hf[:], hb[:], ALU.add), ['lhf', 'lhb'], ['lhf'])
                S.op('dve', lambda e: e.tensor_tensor(yo[:], hf[:], gt[:], ALU.mult), ['lhf', 'lgt'], ['lyo'])
                S.dma(self.yT[512 + c * 128:512 + (c + 1) * 128, :], yo[:], reads=['lyo'])
            S.barrier()

    def phase_mla(self, l):
        nc, S = self.nc, self.S
        Lt, Lc, NB, NBc = self.Lt, self.Lc, self.NB, self.NBc
        with contextlib.ExitStack() as st:
            sb = lambda n, s, dt=F32: self.sb(n, s, dt, st)
            qT = sb("m_qT", [96, 4, Lt], BF16)
            kT = sb("m_kT", [96, 4, Lt], BF16)
            V = sb("m_V", [128, NB, 4, 65], BF16)
            tab = sb("m_tab", [96, 2, Lt])
            S.dma(tab[64:96, 0, :], self.rope[0], writes=['tab'])
            S.dma(tab[64:96, 1, :], self.rope[1], reads=['tab'], writes=['tab'])
            S.op('pool', lambda e: e.memset(V[:, :, :, 64:65], 1.0), [], ['Vones'])
            with contextlib.ExitStack() as st2:
                sb2 = lambda n, s, dt=F32: self.sb(n, s, dt, st2)
                wq, wqk = self.load_weight_bf(st2, "wuq", self.w_uq[l], 2, 768)
                wkv, wkvk = self.load_weight_bf(st2, "wukv", self.w_ukv[l], 1, 512)
                cq = [sb2("m_cq%d" % i, [128, 2, 512]) for i in range(2)]
                sq = [sb2("m_sq%d" % i, [128, 2, 512], BF16) for i in range(2)]
                cqn = [sb2("m_cqn%d" % i, [128, 2, 512], BF16) for i in range(2)]
                ckv = [sb2("m_ckv%d" % i, [128, 512]) for i in range(2)]
                sk = [sb2("m_sk%d" % i, [128, 512], BF16) for i in range(2)]
                ckn = [sb2("m_ckn%d" % i, [128, 512], BF16) for i in range(2)]
                rs = [sb2("m_rs%d" % i, [128, 512]) for i in range(2)]
                rk = [sb2("m_rk%d" % i, [128, 512]) for i in range(2)]
                tmp = [sb2("m_tmp%d" % i, [128, 512]) for i in range(2)]
                kr = [sb2("m_kr%d" % i, [96, 2, 512]) for i in range(2)]
                r1 = [sb2("m_r1%d" % i, [96, 512]) for i in range(2)]
                r2 = [sb2("m_r2%d" % i, [96, 512]) for i in range(2)]
                ssq = [st2.enter_context(self.pst("m_ssq%d" % i, [128, 512], F32)) for i in range(2)]
                qps = [st2.enter_context(self.pst("m_qps%d" % i, [128, 512], F32)) for i in range(4)]
                vps = st2.enter_context(self.pst("m_vps", [128, 4, 64], F32))
                qi = 0
                for ti, (t0, tn, s) in enumerate(self.tiles(512)):
                    p = ti % 2
                    S.dma(cq[p][:, :, :tn], self.P_cq.rearrange("(k p) t -> p k t", p=128)[:, :, t0:t0 + tn], writes=[('mcq', p)])
                    S.op('act', lambda e: e.activation(sq[p][:, :, :tn], cq[p][:, :, :tn], AF.Square), [('mcq', p)], [('msq', p)])
                    for k in range(2):
                        S.op('pe', lambda e: e.matmul(ssq[p][:, :tn], self.onesb[:], sq[p][:, k, :tn], start=(k == 0), stop=(k == 1)),
                             [('msq', p), 'onesb'], [('mssq', p)], signal=(k == 1))
                    self.rstd_from_ss(ssq[p], rs[p], tn, 1.0 / 256, [('mssq', p)], [('mrs', p)], tmp[p])
                    for k in range(2):
                        S.op('dve', lambda e: e.scalar_tensor_tensor(cqn[p][:, k, :tn], cq[p][:, k, :tn], self.pvt[:, 168 + k:169 + k], rs[p][:, :tn],
                                                                     ALU.mult, ALU.mult),
                             [('mcq', p), ('mrs', p), 'pvt'], [('mcqn', p, k)])
                    kq = [('mcqn', p, 0), ('mcqn', p, 1)]
                    for hh in range(4):
                        qa = qps[qi % 4]
                        qb = qps[(qi + 1) % 4]
                        ka, kb = ('mqps', qi % 4), ('mqps', (qi + 1) % 4)
                        qi += 2
                        for k in range(2):
                            S.op('pe', lambda e: e.matmul(qa[:96, :tn], wq[:, k, hh * 96:(hh + 1) * 96], cqn[p][:, k, :tn], start=(k == 0), stop=(k == 1)),
                                 kq + wqk, [ka], signal=(k == 1))
                        for k in range(2):
                            S.op('pe', lambda e: e.matmul(qb[:96, :tn], wq[:, k, 384 + hh * 96:384 + (hh + 1) * 96], cqn[p][:, k, :tn],
                                                          start=(k == 0), stop=(k == 1)),
                                 kq + wqk, [kb], signal=(k == 1))
                        S.op('act', lambda e: e.copy(qT[0:64, hh, t0:t0 + tn], qa[0:64, :tn]), [ka], [('qTn', hh, t0)])
                        S.op('dve', lambda e: e.tensor_tensor(r1[p][64:96, :tn], qa[64:96, :tn], tab[64:96, 0, t0:t0 + tn], ALU.mult),
                             [ka, 'tab'], [('mr1', p)])
                        S.op('dve', lambda e: e.tensor_tensor(r2[p][64:96, :tn], qb[64:96, :tn], tab[64:96, 1, t0:t0 + tn], ALU.mult),
                             [kb, 'tab'], [('mr2', p)])
                        S.op('pool', lambda e: e.tensor_tensor(qT[64:96, hh, t0:t0 + tn], r1[p][64:96, :tn], r2[p][64:96, :tn], ALU.add),
                             [('mr1', p), ('mr2', p)], [('qTr', hh, t0)])
                    S.dma(ckv[p][:, :tn], self.P_ckv[:, t0:t0 + tn], writes=[('mckv', p)])
                    S.op('act', lambda e: e.activation(sk[p][:, :tn], ckv[p][:, :tn], AF.Square), [('mckv', p)], [('msk', p)])
                    S.op('pe', lambda e: e.matmul(ssq[p][:, :tn], self.onesb[:], sk[p][:, :tn], start=True, stop=True),
                         [('msk', p), 'onesb'], [('mssq', p)])
                    self.rstd_from_ss(ssq[p], rk[p], tn, 1.0 / 128, [('mssq', p)], [('mrk', p)], tmp[p])
                    S.op('dve', lambda e: e.scalar_tensor_tensor(ckn[p][:, :tn], ckv[p][:, :tn], self.pvt[:, 170:171], rk[p][:, :tn], ALU.mult, ALU.mult),
                         [('mckv', p), ('mrk', p), 'pvt'], [('mckn', p)])
                    for hh in range(4):
                        qa = qps[qi % 4]
                        ka = ('mqps', qi % 4)
                        qi += 1
                        S.op('pe', lambda e: e.matmul(qa[:64, :tn], wkv[:, 0, hh * 64:(hh + 1) * 64], ckn[p][:, :tn], start=True, stop=True),
                             [('mckn', p)] + wkvk, [ka])
                        S.op('act', lambda e: e.copy(kT[0:64, hh, t0:t0 + tn], qa[0:64, :tn]), [ka], [('kTn', hh, t0)])
                    for bi in range(tn // 128):
                        S.op('pe', lambda e: e.matmul(vps[:].rearrange("p h d -> p (h d)"), ckn[p][:, bi * 128:(bi + 1) * 128], wkv[:, 0, 256:512],
                                                      start=True, stop=True),
                             [('mckn', p)] + wkvk, ['mvps'])
                        blk = t0 // 128 + bi
                        S.op('dve', lambda e: e.tensor_copy(V[:, blk, :, 0:64], vps[:]), ['mvps'], [('V', blk)])
                    S.dma(kr[p][64:96, 0, :tn], self.P_kr[0:32, t0:t0 + tn], writes=[('mkr', p)])
                    S.dma(kr[p][64:96, 1, :tn], self.P_kr[32:64, t0:t0 + tn], reads=[('mkr', p)], writes=[('mkr', p)])
                    S.op('dve', lambda e: e.tensor_tensor(kr[p][64:96, :, :tn], kr[p][64:96, :, :tn], tab[64:96, :, t0:t0 + tn], ALU.mult),
                         [('mkr', p), 'tab'], [('mkr', p)])
                    S.op('dve', lambda e: e.tensor_tensor(r1[p][64:96, :tn], kr[p][64:96, 0, :tn], kr[p][64:96, 1, :tn], ALU.add),
                         [('mkr', p)], [('mr1', p)])
                    for hh in range(4):
                        S.op('pool', lambda e: e.tensor_copy(kT[64:96, hh, t0:t0 + tn], r1[p][64:96, :tn]), [('mr1', p)], [('kTr', hh, t0)])
                S.barrier()
            pT = [sb("m_pT%d" % i, [128, 512], BF16) for i in range(3)]
            atm = [sb("m_atm%d" % i, [128, 4, 256]) for i in range(2)]
            rec = sb("m_rec", [128, 8])
            yob = [sb("m_yob%d" % i, [128, 2, 512], BF16) for i in range(2)]
            sps = [st.enter_context(self.pst("m_sps%d" % i, [128, 512], F32)) for i in range(2)]
            acc = [st.enter_context(self.pst("m_acc%d" % i, [128, 512], F32)) for i in range(4)]
            tps = [st.enter_context(self.pst("m_tps%d" % i, [128, 512], F32)) for i in range(2)]
            scl = 96.0 ** -0.5
            si = 0
            ri = 0
            for ti, (t0, tn, s) in enumerate(self.tiles(512)):
                p = ti % 2
                nq = tn // 128
                kblocks = list(range(NBc)) if s == 1 else list(range(NB))
                items = [(hh, kb) for hh in range(4) for kb in kblocks]

                def emit_s(j):
                    hh, kb = items[j]
                    sp_ = sps[j % 2]
                    S.op('pe', lambda e: e.matmul(sp_[:, :tn], kT[0:96, hh, kb * 128:(kb + 1) * 128], qT[0:96, hh, t0:t0 + tn], start=True, stop=True),
                         [], [('sps', j % 2)])
                emit_s(0)
                for j, (hh, kb) in enumerate(items):
                    sp_ = sps[j % 2]
                    pt_ = pT[j % 3]
                    ks, kp = ('sps', j % 2), ('pT', j % 3)
                    S.op('act', lambda e: e.activation(pt_[:, :tn], sp_[:, :tn], AF.Exp, scale=scl), [ks], [kp])
                    if j + 1 < len(items):
                        emit_s(j + 1)
                    for qb in range(nq):
                        S.op('pe', lambda e: e.matmul(acc[qb][:, 0:65], pt_[:, qb * 128:(qb + 1) * 128], V[:, kb, hh, :],
                                                      start=(kb == kblocks[0]), stop=(kb == kblocks[-1])),
                             [kp], [('acc', qb)])
                    if kb == kblocks[-1]:
                        for qb in range(nq):
                            rc = rec[:, ri % 8:ri % 8 + 1]
                            kr_ = ('rec', ri % 8)
                            ri += 1
                            S.op('dve', lambda e: e.reciprocal(rc, acc[qb][:, 64:65]), [('acc', qb)], [kr_])
                            S.op('dve', lambda e: e.tensor_scalar(atm[p][:, qb, hh * 64:(hh + 1) * 64], acc[qb][:, 0:64], rc, None, ALU.mult),
                                 [('acc', qb), kr_], [('atm', p, qb, hh)])
                for qb in range(nq):
                    for c in range(2):
                        tp = tps[(qb * 2 + c) % 2]
                        kt = ('tps', (qb * 2 + c) % 2)
                        S.op('pe', lambda e: e.transpose(tp[:, 0:128], atm[p][:, qb, c * 128:(c + 1) * 128], self.ident()),
                             [('atm', p, qb, hh) for hh in range(4)] + ['cst'], [kt])
                        S.op('act', lambda e: e.copy(yob[p][:, c, qb * 128:(qb + 1) * 128], tp[:, 0:128]), [kt], [('yob', p, c, qb)])
                for c in range(2):
                    S.dma(self.yT[768 + c * 128:768 + (c + 1) * 128, t0:t0 + tn], yob[p][:, c, :tn], reads=[('yob', p, c, qb) for qb in range(nq)])
            S.barrier()

    def phase_gdn(self, l):
        nc, S = self.nc, self.S
        Lt, Lc, NB, NBc = self.Lt, self.Lc, self.NB, self.NBc
        with contextlib.ExitStack() as st:
            sb = lambda n, s, dt=F32: self.sb(n, s, dt, st)
            oacc = sb("g_oacc", [128, 4, Lt])
            beta = sb("g_beta", [128, NB, 8])
            nbeta = sb("g_nbeta", [128, NB, 8])
            gg = sb("g_gg", [128, NB, 8])
            t1 = sb("g_t1", [128, NB, 8])
            t2 = sb("g_t2", [128, NB, 8])
            negA = sb("g_negA", [128, 8])
            kba = [('ba', ti) for ti in range(len(self.tiles(512)))]
            if GSTOP < -1:
                S.barrier()
                return
            S.op('act', lambda e: e.activation(beta[:], self.ba[:, :, 0:8], AF.Sigmoid), kba, ['gbeta'])
            S.op('dve', lambda e: e.tensor_scalar(nbeta[:], beta[:], -1.0, None, ALU.mult), ['gbeta'], ['gnbeta'])
            S.op('dve', lambda e: e.tensor_tensor(t1[:], self.ba[:, :, 8:16], self.pvs(128, 8).unsqueeze(1).to_broadcast([128, NB, 8]), ALU.add),
                 kba + ['pvt'], ['gt1'])
            S.op('act', lambda e: e.activation(t2[:], t1[:], AF.Abs), ['gt1'], ['gt2'])
            S.op('act', lambda e: e.activation(t2[:], t2[:], AF.Exp, scale=-1.0), ['gt2'], ['gt2'])
            S.op('act', lambda e: e.activation(t2[:], t2[:], AF.Ln, bias=1.0), ['gt2'], ['gt2'])
            S.op('dve', lambda e: e.scalar_tensor_tensor(t1[:], t1[:], 0.0, t2[:], ALU.max, ALU.add), ['gt1', 'gt2'], ['gt1'])
            S.op('act', lambda e: e.activation(negA[:], self.pvs(172, 8), AF.Exp), ['pvt'], ['gnegA'])
            S.op('dve', lambda e: e.tensor_scalar(negA[:], negA[:], -1.0, None, ALU.mult), ['gnegA'], ['gnegA'])
            S.op('dve', lambda e: e.tensor_tensor(gg[:], t1[:], negA[:].unsqueeze(1).to_broadcast([128, NB, 8]), ALU.mult), ['gt1', 'gnegA'], ['ggg'])

            if GSTOP < 0:
                S.barrier()
                return
            def T(n, dt=F32):
                return [sb("g_%s%d" % (n, i), [128, 4, 128], dt) for i in range(2)]
            qkv = [sb("g_qkv%d" % i, [128, 12, 128], BF16) for i in range(2)]
            GM, EGb, Dm, E, NK, NBm, ub = T("GM"), T("EGb"), T("Dm"), T("E"), T("NK"), T("NBm"), T("ub")
            attnT, Tb = T("attnT", BF16), T("Tb", BF16)
            T1 = lambda n, dt=F32: sb("g1_" + n, [128, 4, 128], dt)
            Nn1, NT1, Xa1, XTa1, Xb1, XTb1, Qa1, Qb1 = [T1(n) for n in ("Nn", "NT", "Xa", "XTa", "Xb", "XTb", "Qa", "Qb")]
            Ba1, BTa1, Bb1, BTb1, Cc1, C2c1, Nl1, NlT1 = [T1(n, BF16) for n in ("Ba", "BTa", "Bb", "BTb", "Cc", "C2c", "Nl", "NlT")]
            kE, kg, vtm, wT, qgT, vnew = T("kE", BF16), T("kg", BF16), T("vtm", BF16), T("wT", BF16), T("qgT", BF16), T("vnew", BF16)
            gsm = [sb("g_gsm%d" % i, [128, 16]) for i in range(2)]
            gla = [sb("g_gla%d" % i, [128, 4]) for i in range(2)]
            S32 = sb("g_S32", [128, 4, 128])
            Sbf = sb("g_Sbf", [128, 4, 128], BF16)
            pf = [st.enter_context(self.pst("g_pf%d" % i, [128, 4, 128], F32)) for i in range(6)]
            pb = [st.enter_context(self.pst("g_pb%d" % i, [128, 4, 128], BF16)) for i in range(2)]
            cnt = dict(f=0, b=0)

            def PF():
                i = cnt['f'] % 6
                cnt['f'] += 1
                return pf[i], ('pf', i)

            def PB():
                i = cnt['b'] % 2
                cnt['b'] += 1
                return pb[i], ('pb', i)

            bc_h = lambda ap2: ap2.unsqueeze(1).to_broadcast([128, 4, 128])
            onesH = sb("g_onesH", [128, 4, 128])
            MdH = [sb("g_MdH%d" % i, [128, 4, 128]) for i in range(2)]
            strictH = [sb("g_strictH%d" % i, [128, 4, 128]) for i in range(2)]
            S.op('dve', lambda e: e.memset(onesH[:], 1.0), [], ['onesH'])
            maskH = {}
            for mi_ in (0, 10, 11, 12, 13):
                maskH[mi_] = sb("g_maskH%d" % mi_, [128, 4, 128])
                S.op('dve', lambda e: e.tensor_tensor(maskH[mi_][:], onesH[:], bc_h(self.cst[:, mi_, :]), ALU.mult), ['onesH', 'cst'], ['cst'])
            for dd in range(2):
                S.op('dve', lambda e: e.tensor_tensor(MdH[dd][:], onesH[:], bc_h(self.cst[:, 2 + dd, :]), ALU.mult), ['onesH', 'cst'], [('MdH', dd)])
                S.op('dve', lambda e: e.tensor_tensor(strictH[dd][:], onesH[:], bc_h(self.cst[:, 6 + dd, :]), ALU.mult), ['onesH', 'cst'], [('strictH', dd)])
            bc_i = lambda ap2: ap2.unsqueeze(2).to_broadcast([128, 4, 128])
            qsrc = self.qkvT.rearrange("(c p) t -> p c t", p=128)
            it = 0
            for d in range(2):
                Md, negm, strict = self.cst[:, 2 + d, :], self.cst[:, 4 + d, :], self.cst[:, 6 + d, :]
                if GSKIP != 2:
                    S.op('pool', lambda e: e.memset(S32[:], 0.0), [('S32', h) for h in range(4)], [('S32', h) for h in range(4)])
                    S.op('pool', lambda e: e.memset(Sbf[:], 0.0), [('Sbf', h) for h in range(4)], [('Sbf', h) for h in range(4)])
                if d == 0:
                    order = list(range(NB))
                else:
                    order = list(range(NBc - 1, -1, -1)) + list(range(NB - 1, NBc - 1, -1))
                for b in order:
                    p = it % 2
                    it += 1
                    K = lambda *n: n + (p,)
                    tok = slice(b * 128, (b + 1) * 128)
                    if GSKIP != 1:
                        S.dma(qkv[p][:], qsrc[:, :, tok], writes=[K('qkv')])
                    qTb = lambda h: qkv[p][:, h, :]
                    kTb = lambda h: qkv[p][:, 4 + h, :]
                    vTb = lambda h: qkv[p][:, 8 + h, :]
                    gcol = gg[:, b, d * 4:(d + 1) * 4]
                    if GSTOP < 1:
                        continue
                    S.op('dve', lambda e: e.tensor_tensor(GM[p][:], MdH[d][:], bc_i(gcol), ALU.mult), [('MdH', d), 'ggg'], [K('GM')])
                    if GSUB < 1:
                        continue
                    gp, kgp = PF()
                    gpv = gp[:].rearrange("p h c -> p (h c)")
                    S.op('pe', lambda e: e.matmul(gpv[:, 0:4], Md, gcol, start=True, stop=True), ['cst', 'ggg'], [kgp], signal=False)
                    S.op('pe', lambda e: e.matmul(gpv[:, 4:8], self.ones32(), gcol, start=True, stop=True), ['cst', 'ggg'], [kgp])
                    if GSUB < 2:
                        continue
                    S.op('dve', lambda e: e.tensor_copy(gsm[p][:, 0:8], gpv[:, 0:8]), [kgp], [K('gsm')])
                    S.op('act', lambda e: e.activation(gsm[p][:, 8:12], gsm[p][:, 0:4], AF.Exp), [K('gsm')], [K('gsm2')])
                    S.op('dve', lambda e: e.tensor_tensor(gsm[p][:, 12:16], gsm[p][:, 4:8], gsm[p][:, 0:4], ALU.subtract), [K('gsm')], [K('gsm3')])
                    S.op('act', lambda e: e.activation(gsm[p][:, 12:16], gsm[p][:, 12:16], AF.Exp), [K('gsm3')], [K('gsm3')])
                    S.op('act', lambda e: e.activation(gla[p][:], gsm[p][:, 4:8], AF.Exp), [K('gsm')], [K('gla')])
                    if GSUB < 3:
                        continue
                    gb, kgb = PF()
                    for h in range(4):
                        if GSKIP == 4:
                            break
                        S.op('pe', lambda e: e.matmul(gb[:, h, :], self.ones32(), GM[p][:, h, :], start=True, stop=True),
                             ['cst', K('GM')], [kgb], signal=(h == 3))
                    if GSKIP == 5000:
                        S.op('act', lambda e: e.activation(EGb[p][:].rearrange("p h c -> p (h c)"), gb[:].rearrange("p h c -> p (h c)"), AF.Exp), [kgb], [K('EGb')])
                    elif GSKIP != 7:
                        S.op('dve', lambda e: e.tensor_copy(EGb[p][:], gb[:]), [kgb], [K('EGb')])
                        S.op('act', lambda e: e.activation(EGb[p][:], EGb[p][:], AF.Exp), [K('EGb')], [K('EGb')])
                    elif GSKIP != 3:
                        S.op('act', lambda e: e.activation(EGb[p][:], gb[:], AF.Exp), [kgb], [K('EGb')])
                    if GSUB < 4:
                        continue
                    S.op('dve', lambda e: e.tensor_tensor(Dm[p][:], gb[:], bc_i(gsm[p][:, 0:4]), ALU.subtract), [kgb, K('gsm')], [K('Dm')])
                    S.op('dve', lambda e: e.scalar_tensor_tensor(Dm[p][:], Dm[p][:], 0.0, bc_h(negm), ALU.min, ALU.add), [K('Dm'), 'cst'], [K('Dm')])
                    S.op('act', lambda e: e.activation(E[p][:], Dm[p][:], AF.Exp), [K('Dm')], [K('E')])
                    if GSUB < 5:
                        continue
                    S.op('dve', lambda e: e.tensor_tensor(NBm[p][:], strictH[d][:], bc_i(nbeta[:, b, d * 4:(d + 1) * 4]), ALU.mult),
                         [('strictH', d), 'gnbeta'], [K('NBm')])
                    S.op('dve', lambda e: e.tensor_tensor(qgT[p][:], qkv[p][:, 0:4, :], EGb[p][:], ALU.mult), [K('qkv'), K('EGb')], [K('qgT')])
                    if GSTOP < 2:
                        continue
                    tk, ktk = PB()
                    for h in range(4):
                        S.op('pe', lambda e: e.transpose(tk[:, h, :], kTb(h), self.identb[:]), [K('qkv'), 'identb'], [ktk], signal=(h == 3))
                    S.op('dve', lambda e: e.tensor_tensor(kE[p][:], tk[:], bc_i(gsm[p][:, 8:12]), ALU.mult), [ktk, K('gsm2')], [K('kE')])
                    S.op('dve', lambda e: e.tensor_tensor(kg[p][:], tk[:], bc_i(gsm[p][:, 12:16]), ALU.mult), [ktk, K('gsm3')], [K('kg')])
                    tv, ktv = PB()
                    for h in range(4):
                        S.op('pe', lambda e: e.transpose(tv[:, h, :], vTb(h), self.identb[:]), [K('qkv'), 'identb'], [ktv], signal=(h == 3))
                    S.op('act', lambda e: e.copy(vtm[p][:], tv[:]), [ktv], [K('vtm')])
                    if GSTOP < 3:
                        continue
                    kk, kkk = PF()
                    for h in range(4):
                        S.op('pe', lambda e: e.matmul(kk[:, h, :], kTb(h), kTb(h), start=True, stop=True), [K('qkv')], [kkk], signal=(h == 3))
                    qk, kqk = PF()
                    for h in range(4):
                        S.op('pe', lambda e: e.matmul(qk[:, h, :], kTb(h), qTb(h), start=True, stop=True), [K('qkv')], [kqk], signal=(h == 3))
                    S.op('dve', lambda e: e.tensor_tensor(attnT[p][:], qk[:], E[p][:], ALU.mult), [kqk, K('E')], [K('attnT')])
                    S.op('dve', lambda e: e.tensor_tensor(NK[p][:], kk[:], E[p][:], ALU.mult), [kkk, K('E')], [K('NK')])
                    kN, kNT = ('i', 'Nn'), ('i', 'NT')
                    mk = lambda i: maskH[i][:]
                    id32 = self.ident()
                    S.op('pool', lambda e: e.tensor_tensor(Nn1[:], NK[p][:], NBm[p][:], ALU.mult), [K('NK'), K('NBm')], [kN])
                    tnf, ktnf = PF()
                    for h in range(4):
                        S.op('pe', lambda e: e.transpose(tnf[:, h, :], Nn1[:, h, :], id32), [kN, 'cst'], [ktnf])
                    S.op('dve', lambda e: e.tensor_copy(NT1[:], tnf[:]), [ktnf], [kNT])
                    if GSTOP < 4:
                        continue
                    S.op('pool', lambda e: e.tensor_tensor(Xa1[:], Nn1[:], mk(10), ALU.mult), [kN, 'cst'], [('i', 'Xa')])
                    S.op('pool', lambda e: e.tensor_tensor(XTa1[:], NT1[:], mk(10), ALU.mult), [kNT, 'cst'], [('i', 'XTa')])
                    S.op('pool', lambda e: e.tensor_tensor(Qa1[:], Xa1[:], mk(0), ALU.add), [('i', 'Xa'), 'cst'], [('i', 'Qa')])
                    X, XT, kX, kXT = Xa1, XTa1, ('i', 'Xa'), ('i', 'XTa')
                    Qc, kQ = Qa1, ('i', 'Qa')
                    xb_ = [(Xb1, XTb1, ('i', 'Xb'), ('i', 'XTb')), (Xa1, XTa1, ('i', 'Xa'), ('i', 'XTa'))]
                    qb_ = [(Qb1, ('i', 'Qb')), (Qa1, ('i', 'Qa'))]
                    for lv in range(1, 4):
                        X2, X2T, kX2, kX2T = xb_[(lv - 1) % 2]
                        Qn, kQn = qb_[(lv - 1) % 2]
                        a, ka = PF()
                        for h in range(4):
                            S.op('pe', lambda e: e.matmul(a[:, h, :], X[:, h, :], XT[:, h, :], start=True, stop=True), [kX, kXT], [ka])
                        a2, ka2 = (None, None)
                        if lv < 3:
                            a2, ka2 = PF()
                            for h in range(4):
                                S.op('pe', lambda e: e.matmul(a2[:, h, :], XT[:, h, :], X[:, h, :], start=True, stop=True), [kX, kXT], [ka2])
                        S.op('act', lambda e: e.copy(X2T[:], a[:]), [ka], [kX2T])
                        if lv < 3:
                            S.op('dve', lambda e: e.tensor_copy(X2[:], a2[:]), [ka2], [kX2])
                        a3, ka3 = PF()
                        for h in range(4):
                            S.op('pe', lambda e: e.matmul(a3[:, h, :], X2T[:, h, :], Qc[:, h, :], start=True, stop=True), [kX2T, kQ], [ka3])
                        S.op('dve', lambda e: e.tensor_tensor(Qn[:], a3[:], Qc[:], ALU.add), [ka3, kQ], [kQn])
                        X, XT, kX, kXT = X2, X2T, kX2, kX2T
                        Qc, kQ = Qn, kQn
                    tq, ktq = PF()
                    for h in range(4):
                        S.op('pe', lambda e: e.transpose(tq[:, h, :], Qc[:, h, :], id32), [kQ, 'cst'], [ktq])
                    S.op('dve', lambda e: e.tensor_copy(BTa1[:], tq[:]), [ktq], [('i', 'BTa')])
                    S.op('pool', lambda e: e.tensor_copy(Ba1[:], Qc[:]), [kQ], [('i', 'Ba')])
                    Bc, BTc, kB, kBT = Ba1, BTa1, ('i', 'Ba'), ('i', 'BTa')
                    mb_ = [(Bb1, BTb1, ('i', 'Bb'), ('i', 'BTb')), (Ba1, BTa1, ('i', 'Ba'), ('i', 'BTa'))]
                    for mi, midx in enumerate((11, 12, 13)):
                        Bn, BTn, kBn, kBTn = mb_[mi % 2]
                        S.op('pool', lambda e: e.tensor_tensor(NlT1[:], NT1[:], mk(midx), ALU.mult), [kNT, 'cst'], [('i', 'NlT')])
                        c, kc = PF()
                        for h in range(4):
                            S.op('pe', lambda e: e.matmul(c[:, h, :], NlT1[:, h, :], Bc[:, h, :], start=True, stop=True), [('i', 'NlT'), kB], [kc])
                        S.op('act', lambda e: e.copy(Cc1[:], c[:]), [kc], [('i', 'Cc')])
                        if mi < 2:
                            S.op('pool', lambda e: e.tensor_tensor(Nl1[:], Nn1[:], mk(midx), ALU.mult), [kN, 'cst'], [('i', 'Nl')])
                            c2, kc2 = PF()
                            for h in range(4):
                                S.op('pe', lambda e: e.matmul(c2[:, h, :], Nl1[:, h, :], BTc[:, h, :], start=True, stop=True), [('i', 'Nl'), kBT], [kc2])
                            S.op('act', lambda e: e.copy(C2c1[:], c2[:]), [kc2], [('i', 'C2c')])
                        bn, kbn = PF()
                        for h in range(4):
                            S.op('pe', lambda e: e.matmul(bn[:, h, :], BTc[:, h, :], Cc1[:, h, :], start=True, stop=True), [kBT, ('i', 'Cc')], [kbn])
                        S.op('dve', lambda e: e.tensor_tensor(Bn[:], bn[:], Bc[:], ALU.add), [kbn, kB], [kBn])
                        if mi < 2:
                            bt, kbt = PF()
                            for h in range(4):
                                S.op('pe', lambda e: e.matmul(bt[:, h, :], Bc[:, h, :], C2c1[:, h, :], start=True, stop=True), [kB, ('i', 'C2c')], [kbt])
                            S.op('dve', lambda e: e.tensor_tensor(BTn[:], bt[:], BTc[:], ALU.add), [kbt, kBT], [kBTn])
                        Bc, kB = Bn, kBn
                        if mi < 2:
                            BTc, kBT = BTn, kBTn
                    S.op('pool', lambda e: e.tensor_copy(Tb[p][:], Bc[:]), [kB], [K('Tb')])
                    if GSTOP < 5:
                        continue
                    u_, ku = PF()
                    for h in range(4):
                        S.op('pe', lambda e: e.matmul(u_[:, h, :], Tb[p][:, h, :], vtm[p][:, h, :], start=True, stop=True), [K('Tb'), K('vtm')], [ku])
                    w_, kw = PF()
                    for h in range(4):
                        S.op('pe', lambda e: e.matmul(w_[:, h, :], kE[p][:, h, :], Tb[p][:, h, :], start=True, stop=True), [K('Tb'), K('kE')], [kw])
                    S.op('dve', lambda e: e.tensor_tensor(ub[p][:], u_[:], bc_i(beta[:, b, d * 4:(d + 1) * 4]), ALU.mult), [ku, 'gbeta'], [K('ub')])
                    S.op('act', lambda e: e.copy(wT[p][:], w_[:]), [kw], [K('wT')])
                    if GSTOP < 6:
                        continue
                    if d == 0 and b == 0 and l == 0:
                        fl = lambda t: t[:].rearrange("p h c -> p (h c)")
                        self.dump("gsm", gsm[p][:], [K('gsm'), K('gsm2'), K('gsm3')])
                        self.dump("E", fl(E[p]), [K('E')])
                        self.dump("EGb", fl(EGb[p]), [K('EGb')])
                        self.dump("attnT", fl(attnT[p]), [K('attnT')])
                        self.dump("Tb", fl(Tb[p]), [K('Tb')])
                        self.dump("ub", fl(ub[p]), [K('ub')])
                        self.dump("wT", fl(wT[p]), [K('wT')])
                        self.dump("kE", fl(kE[p]), [K('kE')])
                        self.dump("kg", fl(kg[p]), [K('kg')])
                        self.dump("vtm", fl(vtm[p]), [K('vtm')])
                        self.dump("qgT", fl(qgT[p]), [K('qgT')])
                    p1, kp1 = PF()
                    for h in range(4):
                        S.op('pe', lambda e: e.matmul(p1[:, h, :], wT[p][:, h, :], Sbf[:, h, :], start=True, stop=True), [K('wT'), ('Sbf', h)], [kp1],
                             signal=(h == 3))
                    for h in range(4):
                        nb_ = nbeta[:, b, d * 4 + h:d * 4 + h + 1]
                        S.op('dve', lambda e: e.scalar_tensor_tensor(vnew[p][:, h, :], p1[:, h, :], nb_, ub[p][:, h, :], ALU.mult, ALU.add),
                             [kp1, 'gnbeta', K('ub')], [K('vnew', h)])
                    o_, ko = PF()
                    for h in range(4):
                        S.op('pe', lambda e: e.matmul(o_[:, h, :], Sbf[:, h, :], qgT[p][:, h, :], start=True, stop=False), [('Sbf', h), K('qgT')], [ko], signal=False)
                        S.op('pe', lambda e: e.matmul(o_[:, h, :], vnew[p][:, h, :], attnT[p][:, h, :], start=False, stop=True), [K('vnew', h), K('attnT')], [ko],
                             signal=(h == 3))
                    if d == 0:
                        S.op('dve', lambda e: e.tensor_copy(oacc[:, :, tok], o_[:]), [ko], [('oacc', b)])
                    else:
                        S.op('dve', lambda e: e.tensor_tensor(oacc[:, :, tok], o_[:], oacc[:, :, tok], ALU.add), [ko, ('oacc', b)], [('oacc', b)])
                    su, ksu = PF()
                    for h in range(4):
                        S.op('pe', lambda e: e.matmul(su[:, h, :], kg[p][:, h, :], vnew[p][:, h, :], start=True, stop=True), [K('kg'), K('vnew', h)], [ksu],
                             signal=(h == 3))
                    for h in range(4):
                        S.op('dve', lambda e: e.scalar_tensor_tensor(S32[:, h, :], S32[:, h, :], gla[p][:, h:h + 1], su[:, h, :], ALU.mult, ALU.add),
                             [ksu, K('gla'), ('S32', h)], [('S32', h)])
                        S.op('dve', lambda e: e.tensor_copy(Sbf[:, h, :], S32[:, h, :]), [('S32', h)], [('Sbf', h)])
            if GSTOP < 7:
                S.barrier()
                return
            if l == 0:
                self.dump("oacc", oacc[:].rearrange("p h t -> p (h t)"), [('oacc', b) for b in range(NB)])
            S.barrier()
            fl_ = lambda t: t[:].rearrange("p h c -> p (h c)")
            zt = [fl_(GM[i]) for i in range(2)]
            sq = [fl_(attnT[i]) for i in range(2)]
            rs = [fl_(Dm[i]) for i in range(2)]
            tmp = [fl_(E[i]) for i in range(2)]
            on = [fl_(NK[i]) for i in range(2)]
            yo = [fl_(kE[i]) for i in range(2)]
            it = 0
            for (t0, tn, s) in self.tiles(512):
                kb_ = [('oacc', b) for b in range(t0 // 128, (t0 + tn) // 128)]
                for h in range(4):
                    p = it % 2
                    it += 1
                    ss, kss = PF()
                    ssv = ss[:].rearrange("p h c -> p (h c)")
                    S.dma(zt[p][:, :tn], self.P_z[h * 128:(h + 1) * 128, t0:t0 + tn], writes=[('gzt', p)])
                    S.op('act', lambda e: e.activation(zt[p][:, :tn], zt[p][:, :tn], AF.Silu), [('gzt', p)], [('gzt', p)])
                    S.op('act', lambda e: e.activation(sq[p][:, :tn], oacc[:, h, t0:t0 + tn], AF.Square), kb_, [('gsq', p)])
                    S.op('pe', lambda e: e.matmul(ssv[:, :tn], self.onesb[:], sq[p][:, :tn], start=True, stop=True), [('gsq', p), 'onesb'], [kss])
                    self.rstd_from_ss(ssv, rs[p], tn, 1.0 / 128, [kss], [('grs', p)], tmp[p])
                    S.op('dve', lambda e: e.tensor_tensor(on[p][:, :tn], oacc[:, h, t0:t0 + tn], rs[p][:, :tn], ALU.mult), kb_ + [('grs', p)], [('gon', p)])
                    S.op('dve', lambda e: e.scalar_tensor_tensor(yo[p][:, :tn], on[p][:, :tn], self.pvt[:, 136:137], zt[p][:, :tn], ALU.mult, ALU.mult),
                         [('gon', p), ('gzt', p), 'pvt'], [('gyo', p)])
                    S.dma(self.yT[h * 128:(h + 1) * 128, t0:t0 + tn], yo[p][:, :tn], reads=[('gyo', p)])
            S.barrier()

    def post_norm_residual(self, y32, xt, sq, rstd, tmp, ss_ps, t0, tn, s, ic, par, yk, xk, sqk):
        S = self.S
        S.op('act', lambda e: e.activation(sq[:, :, :tn], y32[:, :, :tn], AF.Square), yk, [sqk])
        for k in range(KC):
            S.op('pe', lambda e: e.matmul(ss_ps[:, :tn], self.onesb[:], sq[:, k, :tn], start=(k == 0), stop=(k == KC - 1)),
                 [sqk, 'onesb'], [('pss', par)], signal=(k == KC - 1))
        self.rstd_from_ss(ss_ps, rstd, tn, 1.0 / D, [('pss', par)], [('prstd', par)], tmp)
        S.op('dve', lambda e: e.tensor_tensor(y32[:, :, :tn], y32[:, :, :tn], rstd[:, :tn].unsqueeze(1).to_broadcast([128, KC, tn]), ALU.mult),
             yk + [('prstd', par)], yk)
        for k in range(KC):
            S.op('dve',
                 lambda e: e.scalar_tensor_tensor(xt[:, k, :tn], y32[:, k, :tn], self.modv[:, ic, k, s:s + 1], xt[:, k, :tn], ALU.mult, ALU.add),
                 yk + ['modv%d' % ic] + xk, [('pxo', par, k)])
        S.dma(self.xT.rearrange("(k p) t -> p k t", p=128)[:, :, t0:t0 + tn], xt[:, :, :tn], reads=[('pxo', par, k) for k in range(KC)] + xk)

    def phase_wout(self, l):
        nc, S = self.nc, self.S
        with contextlib.ExitStack() as st:
            sb = lambda n, s, dt=F32: self.sb(n, s, dt, st)
            w, wkeys = self.load_weight_bf(st, "wout", self.w_out[l], KC, D)
            yt = [sb("f_yt%d" % i, [128, KC, 512], BF16) for i in range(2)]
            xt = [sb("f_xt%d" % i, [128, KC, 512]) for i in range(2)]
            y32 = [sb("f_y32%d" % i, [128, KC, 512]) for i in range(2)]
            sq = [sb("f_sq%d" % i, [128, KC, 512], BF16) for i in range(2)]
            rstd = [sb("f_rstd%d" % i, [128, 512]) for i in range(2)]
            tmp = [sb("f_tmp%d" % i, [128, 512]) for i in range(2)]
            ss_ps = [st.enter_context(self.pst("f_ss%d" % i, [128, 512], F32)) for i in range(2)]
            ops = [st.enter_context(self.pst("f_ops%d" % i, [128, 512], F32)) for i in range(4)]
            ei = 0
            for ti, (t0, tn, s) in enumerate(self.tiles(512)):
                p = ti % 2
                S.dma(yt[p][:, :, :tn], self.yT.rearrange("(k p) t -> p k t", p=128)[:, :, t0:t0 + tn], writes=[('fyt', p)])
                S.dma(xt[p][:, :, :tn], self.xT.rearrange("(k p) t -> p k t", p=128)[:, :, t0:t0 + tn], writes=[('fxt', p)])
                for oc in range(KC):
                    o = ops[ei % 4]
                    ko = ('fops', ei % 4)
                    ei += 1
                    for k in range(KC):
                        S.op('pe', lambda e: e.matmul(o[:, :tn], w[:, k, oc * 128:(oc + 1) * 128], yt[p][:, k, :tn], start=(k == 0), stop=(k == KC - 1)),
                             [('fyt', p)] + wkeys, [ko], signal=(k == KC - 1))
                    S.op('act' if oc % 2 else 'dve', lambda e: (e.copy if oc % 2 else e.tensor_copy)(y32[p][:, oc, :tn], o[:, :tn]), [ko], [('fy32', p, oc)])
                self.post_norm_residual(y32[p], xt[p], sq[p], rstd[p], tmp[p], ss_ps[p], t0, tn, s, 2, p,
                                        [('fy32', p, oc) for oc in range(KC)], [('fxt', p)], ('fsq', p))
            S.barrier()

    def phase_mlp(self, l):
        nc, S = self.nc, self.S
        TN = 256
        with contextlib.ExitStack() as st:
            sb = lambda n, s, dt=F32: self.sb(n, s, dt, st)
            w1, w1k = self.load_weight_bf(st, "wm1", self.w_m1[l], KC, 4 * D, piece=256)
            w2, w2k = self.load_weight_bf(st, "wm2", self.w_m2[l], 32, D, kpiece=4)
            xt = [sb("h_xt%d" % i, [128, KC, TN]) for i in range(2)]
            sq = [sb("h_sq%d" % i, [128, KC, TN], BF16) for i in range(1)]
            h = [sb("h_h%d" % i, [128, KC, TN], BF16) for i in range(1)]
            hid = [sb("h_hid%d" % i, [128, 32, TN], BF16) for i in range(1)]
            rl = [sb("h_rl%d" % i, [128, TN], BF16) for i in range(4)]
            y32 = [sb("h_y32%d" % i, [128, KC, TN]) for i in range(1)]
            rstd = [sb("h_rstd%d" % i, [128, TN]) for i in range(2)]
            tmp = [sb("h_tmp%d" % i, [128, TN]) for i in range(2)]
            ss_ps = [st.enter_context(self.pst("h_ss%d" % i, [128, 512], F32)) for i in range(2)]
            ops = [st.enter_context(self.pst("h_ops%d" % i, [128, 512], F32)) for i in range(4)]
            ei = 0
            for ti, (t0, tn, s) in enumerate(self.tiles(TN)):
                p = ti % 2
                if MSTOP < 1:
                    continue
                yxk = [('hy32', oc) for oc in range(KC)]
                self.norm_mod(t0, tn, s, xt[p], sq[0], y32[0], h[0], rstd[p], tmp[p], ss_ps[p], 3, 4, p, bpar='m', xnk=yxk, hpar='m')
                hk = [('h', 'm', k) for k in range(KC)]
                if MSTOP < 2:
                    continue
                for j in range(32):
                    o = ops[ei % 4]
                    ko = ('hops', ei % 4)
                    r_ = rl[ei % 4]
                    kr_ = ('hrl', ei % 4)
                    ei += 1
                    for k in range(KC):
                        S.op('pe', lambda e: e.matmul(o[:, :tn], w1[:, k, j * 128:(j + 1) * 128], h[0][:, k, :tn], start=(k == 0), stop=(k == KC - 1)),
                             hk + w1k, [ko], signal=(k == KC - 1))
                    S.op('act', lambda e: e.activation(r_[:, :tn], o[:, :tn], AF.Relu), [ko], [kr_])
                    S.op('pool' if j % 2 else 'dve', lambda e: e.tensor_tensor(hid[0][:, j, :tn], r_[:, :tn], r_[:, :tn], ALU.mult), [kr_], [('hid', j)])
                hidk = [('hid', j) for j in range(32)]
                if MSTOP < 3:
                    continue
                for oc in range(KC):
                    o = ops[ei % 4]
                    ko = ('hops', ei % 4)
                    ei += 1
                    for j in range(32):
                        S.op('pe', lambda e: e.matmul(o[:, :tn], w2[:, j, oc * 128:(oc + 1) * 128], hid[0][:, j, :tn], start=(j == 0), stop=(j == 31)),
                             hidk + w2k, [ko], signal=(j == 31))
                    S.op('act' if oc % 2 else 'dve', lambda e: (e.copy if oc % 2 else e.tensor_copy)(y32[0][:, oc, :tn], o[:, :tn]), [ko], [('hy32', oc)])
                if MSTOP < 4:
                    continue
                self.post_norm_residual(y32[0], xt[p], sq[0], rstd[p], tmp[p], ss_ps[p], t0, tn, s, 5, p,
                                        [('hy32', oc) for oc in range(KC)], [('xt', p)], ('sq', 'm'))
            S.barrier()


def _pp(v, nch):
    return np.ascontiguousarray(np.asarray(v, np.float32).reshape(nch, 128).T)


def make_consts():
    c = np.zeros((128, 14, 128), np.float32)
    idx = np.arange(128)
    c[:, 0] = np.eye(128)
    c[:, 1] = 1.0
    c[:, 2] = (idx[:, None] <= idx[None, :])
    c[:, 3] = (idx[:, None] >= idx[None, :])
    c[:, 4] = np.where(idx[None, :] >= idx[:, None], 0.0, -30000.0)
    c[:, 5] = np.where(idx[None, :] <= idx[:, None], 0.0, -30000.0)
    c[:, 6] = (idx[None, :] > idx[:, None])
    c[:, 7] = (idx[None, :] < idx[:, None])
    same = lambda n: (idx[:, None] // n) == (idx[None, :] // n)
    c[:, 10] = same(16)
    c[:, 11] = same(32) & ~same(16)
    c[:, 12] = same(64) & ~same(32)
    c[:, 13] = ~same(64)
    return c.reshape(128, 14 * 128)


def make_rope(Lc, Ll):
    rows = Ll // 64
    row = np.repeat(np.arange(rows, dtype=np.float32), 64)
    col = np.tile(np.arange(64, dtype=np.float32), rows)
    half = 16
    inv = (np.float32(10000.0) ** (-np.arange(0, half, 2, dtype=np.float32) / half)).astype(np.float32)
    ang = np.stack([row[:, None] * inv, col[:, None] * inv], axis=1)
    ang = np.concatenate([ang, ang], axis=-1)
    cos = np.cos(ang).reshape(Ll, 32).T
    sin = np.sin(ang).reshape(Ll, 32).T.copy()
    sgn = np.ones(32, np.float32)
    sgn[0:8] = -1
    sgn[16:24] = -1
    sin = sin * sgn[:, None]
    out = np.zeros((2, 32, Lc + Ll), np.float32)
    out[0, :, :Lc] = 1.0
    out[0, :, Lc:] = cos
    out[1, :, Lc:] = sin
    return out


_SWAP = np.concatenate([np.arange(8, 16), np.arange(0, 8), np.arange(24, 32), np.arange(16, 24)])


def prep_shared(inp, depth):
    f = lambda k: np.asarray(inp[k], np.float32)
    pv = np.zeros((depth, 128, NPV), np.float32)
    for l in range(depth):
        pv[l, :, 0:48] = _pp(f('b_ada')[l], 48)
        pv[l, :, 48:56] = _pp(f('g_attn_pre')[l], 8)
        pv[l, :, 56:64] = _pp(f('g_attn_post')[l], 8)
        pv[l, :, 64:72] = _pp(f('g_mlp_pre')[l], 8)
        pv[l, :, 72:80] = _pp(f('g_mlp_post')[l], 8)
        cw = f('gdn_conv_w')[l]
        pv[l, :, 80:128] = cw.reshape(4, 12, 128).transpose(2, 1, 0).reshape(128, 48)
        pv[l, :, 128:136] = f('gdn_dt_bias')[l].reshape(1, 8)
        pv[l, :, 136:137] = f('gdn_norm_w')[l].reshape(128, 1)
        lw = f('lru_conv_w')[l]
        pv[l, :, 140:148] = lw.reshape(4, 2, 128).transpose(2, 1, 0).reshape(128, 8)
        pv[l, :, 148:150] = _pp(f('lru_conv_b')[l], 2)
        pv[l, :, 156:160] = f('lru_lambda')[l].reshape(2, 2, 128).transpose(2, 0, 1).reshape(128, 4)
        pv[l, :, 160:164] = f('lru_b_a')[l].reshape(2, 2, 128).transpose(2, 0, 1).reshape(128, 4)
        pv[l, :, 164:168] = f('lru_b_i')[l].reshape(2, 2, 128).transpose(2, 0, 1).reshape(128, 4)
        pv[l, :, 168:170] = _pp(f('mla_q_norm')[l], 2)
        pv[l, :, 170:171] = f('mla_kv_norm')[l].reshape(128, 1)
        pv[l, :, 172:180] = f('gdn_a_log')[l].reshape(1, 8)
    w_in = f('w_in')
    w_in_x = np.concatenate([w_in, w_in[:, :, 2960:2992][:, :, _SWAP]], axis=2)
    lw = np.zeros((depth, 2, 2, 2, 128, 128), np.float32)
    for ai, nm in enumerate(('lru_w_a', 'lru_w_i')):
        w = f(nm)
        for c in range(2):
            for gsub in range(2):
                lw[:, ai, :, c, gsub * 64:(gsub + 1) * 64, gsub * 64:(gsub + 1) * 64] = w[:, :, c * 2 + gsub]
    wuq = f('mla_w_uq')
    wuq_sw = np.zeros_like(wuq)
    for h in range(4):
        wuq_sw[:, :, h * 96 + 64:(h + 1) * 96] = wuq[:, :, h * 96 + 64:(h + 1) * 96][:, :, _SWAP]
    wuq_x = np.concatenate([wuq, wuq_sw], axis=2)
    wukv = f('mla_w_ukv').reshape(depth, 128, 4, 128)
    wukv_x = np.concatenate([wukv[:, :, :, :64].reshape(depth, 128, 256), wukv[:, :, :, 64:].reshape(depth, 128, 256)], axis=2)
    return dict(pv=pv, consts=make_consts(), w_ada=f('w_ada')[:depth], w_in=np.ascontiguousarray(w_in_x[:depth]),
                lru_w=lw[:depth], w_uq=np.ascontiguousarray(wuq_x[:depth]), w_ukv=np.ascontiguousarray(wukv_x[:depth]),
                w_out=f('w_out')[:depth], w_m1=f('w_mlp1')[:depth], w_m2=f('w_mlp2')[:depth])


def prep_core(inp, b, shared, Lc, Ll):
    x = np.asarray(inp['x'], np.float32)[b]
    ctx = np.asarray(inp['ctx'], np.float32)[b]
    xT = np.ascontiguousarray(np.concatenate([ctx, x], axis=0).T)
    cv = np.stack([_pp(np.asarray(inp['c'], np.float32)[b], 8), _pp(np.asarray(inp['c_ctx'], np.float32), 8)], axis=2)
    m = dict(xT=xT, cvec=np.ascontiguousarray(cv), rope=make_rope(Lc, Ll))
    for k in ('consts', 'w_ada', 'w_in', 'lru_w', 'w_uq', 'w_ukv', 'w_out', 'w_m1', 'w_m2'):
        m[k] = shared[k]
    m['pv'] = shared['pv']
    return m


_PROG_CACHE = {}


def run(inp, depth=DEPTH, ncores=8, dbg=False):
    Ll = inp['x'].shape[1]
    Lc = inp['ctx'].shape[1]
    key = (Lc, Ll, depth, dbg)
    if key not in _PROG_CACHE:
        p1 = Prog(Lc, Ll, depth, dbg)
        p1.build()
        p2 = Prog(Lc, Ll, depth, dbg, needed=p1.S.needed)
        _PROG_CACHE[key] = p2.build()
        print("sched: ops=%d signals %d -> %d, waits %d" % (sum(p1.S.pos.values()), p1.S.nsig, p2.S.nsig, p2.S.nwait))
    nc = _PROG_CACHE[key]
    shared = prep_shared(inp, depth)
    in_maps = [prep_core(inp, b, shared, Lc, Ll) for b in range(ncores)]
    res = run_bass_kernel_spmd(nc, in_maps, core_ids=list(range(ncores)))
    return res


def kernel(**inputs):
    res = run(inputs, DEPTH, 8, False)
    out = np.stack([np.ascontiguousarray(r["outT"].T) for r in res.results], axis=0)
    return out.astype(np.float32)
```
